# Optimizing a Trainium2 kernel written in Bass

```python
import jax, jax.numpy as jnp
from jax import lax
import numpy as np

D_MODEL = 1024
BATCH = 16
SEQ = 2048
DEPTH = 1
DEC_BATCH = 4
DEC_SEQ = 8192
PAST_LEN = 128

N_HEADS = 16
N_KV_HEADS = 4
HEAD_DIM = 64
ATTN_WIDTH = N_HEADS * HEAD_DIM
KV_WIDTH = N_KV_HEADS * HEAD_DIM
WINDOW = 128
BLOCK = 128
ROPE_THETA = 10000.0
POOL_WIDTH = D_MODEL
POOL_WINDOWS = (2, 4, 8, 16)
N_POOL_GROUPS = 4
POOL_GROUP = POOL_WIDTH // N_POOL_GROUPS
N_BRANCHES = 2
RMS_EPS = 1e-6
NEG_INF = -1e30
IN_WIDTH = 2 * POOL_WIDTH + 2 * ATTN_WIDTH + 2 * KV_WIDTH + N_BRANCHES * D_MODEL

kernel_name = "hybrid_pool_swa_gated_encoder"


def rmsnorm(x, g):
    xf = x.astype(jnp.float32)
    y = xf * lax.rsqrt(jnp.mean(xf * xf, axis=-1, keepdims=True) + RMS_EPS) * g.astype(jnp.float32)
    return y.astype(x.dtype)


def rope(x, pos):
    half = HEAD_DIM // 2
    inv = ROPE_THETA ** (-jnp.arange(half, dtype=jnp.float32) / half)
    ang = pos.astype(jnp.float32)[:, None] * inv[None, :]
    cos = jnp.cos(ang)[None, :, None, :]
    sin = jnp.sin(ang)[None, :, None, :]
    xf = x.astype(jnp.float32)
    x1, x2 = xf[..., :half], xf[..., half:]
    out = jnp.concatenate([x1 * cos - x2 * sin, x2 * cos + x1 * sin], axis=-1)
    return out.astype(x.dtype)


def multiscale_pool(u):
    B, S, _ = u.shape
    uf = u.astype(jnp.float32).reshape(B, S, N_POOL_GROUPS, POOL_GROUP)
    c = jnp.concatenate([jnp.zeros((B, 1, N_POOL_GROUPS, POOL_GROUP), jnp.float32),
                         jnp.cumsum(uf, axis=1)], axis=1)
    t = jnp.arange(S)
    outs = []
    for g, w in enumerate(POOL_WINDOWS):
        lo = jnp.clip(t - w // 2, 0, S)
        hi = jnp.clip(t - w // 2 + w, 0, S)
        cg = c[:, :, g]
        mean = (cg[:, hi] - cg[:, lo]) / (hi - lo).astype(jnp.float32)[None, :, None]
        outs.append(mean)
    return jnp.stack(outs, axis=2) - uf


def windowed_attention(q, k, v, sink):
    B, S = q.shape[0], q.shape[1]
    nb = S // BLOCK
    G = N_HEADS // N_KV_HEADS
    span = BLOCK + 2 * WINDOW
    qg = jnp.moveaxis(q.reshape(B, nb, BLOCK, N_KV_HEADS, G, HEAD_DIM), 1, 0)
    pad = ((0, 0), (WINDOW, WINDOW), (0, 0), (0, 0))
    kp = jnp.pad(k, pad)
    vp = jnp.pad(v, pad)
    rel = jnp.arange(BLOCK)[:, None] - (jnp.arange(span)[None, :] - WINDOW)
    band = jnp.abs(rel) <= WINDOW
    scale = HEAD_DIM ** -0.5
    sink_b = sink.astype(jnp.float32).reshape(1, N_KV_HEADS, G, 1, 1)

    def one_block(args):
        qb, i = args
        start = i * BLOCK
        kb = lax.dynamic_slice_in_dim(kp, start, span, axis=1)
        vb = lax.dynamic_slice_in_dim(vp, start, span, axis=1)
        kpos = start - WINDOW + jnp.arange(span)
        valid = band & ((kpos >= 0) & (kpos < S))[None, :]
        s = jnp.einsum('bqhgd,bkhd->bhgqk', qb, kb).astype(jnp.float32) * scale
        s = jnp.where(valid, s, NEG_INF)
        logits = jnp.concatenate([s, jnp.broadcast_to(sink_b, s.shape[:-1] + (1,))], axis=-1)
        p = jax.nn.softmax(logits, axis=-1)[..., :span]
        return jnp.einsum('bhgqk,bkhd->bqhgd', p.astype(vb.dtype), vb)

    o = lax.map(one_block, (qg, jnp.arange(nb)))
    return jnp.moveaxis(o, 0, 1).reshape(B, S, ATTN_WIDTH)


def encoder_layer(x, norm_pre, w_in, w_pool_group, pool_scale, w_pool_proj,
                  attn_sink, w_attn_proj, w_out, norm_post):
    B, S, _ = x.shape
    h = rmsnorm(x, norm_pre)
    z = h @ w_in
    cuts = np.cumsum([POOL_WIDTH, POOL_WIDTH, ATTN_WIDTH, KV_WIDTH, KV_WIDTH, ATTN_WIDTH]).tolist()
    pool_u, pool_g, q, k, v, attn_g, merge = jnp.split(z, cuts, axis=-1)

    pooled = multiscale_pool(pool_u).astype(x.dtype)
    pb = jnp.einsum('bsgc,gcd->bsgd', pooled, w_pool_group).reshape(B, S, POOL_WIDTH)
    pb = pb * pool_scale * jax.nn.silu(pool_g)

    pos = jnp.arange(S)
    q = rope(q.reshape(B, S, N_HEADS, HEAD_DIM), pos)
    k = rope(k.reshape(B, S, N_KV_HEADS, HEAD_DIM), pos)
    v = v.reshape(B, S, N_KV_HEADS, HEAD_DIM)
    ab = windowed_attention(q, k, v, attn_sink) * jax.nn.silu(attn_g)

    gates = jax.nn.sigmoid(merge.astype(jnp.float32)).astype(x.dtype)
    gate_pool, gate_attn = gates[..., :D_MODEL], gates[..., D_MODEL:]
    m = gate_pool * (pb @ w_pool_proj) + gate_attn * (ab @ w_attn_proj)
    out = m @ w_out
    return x + rmsnorm(out, norm_post)


def setup_inputs(seed: int = 0) -> dict:
    key = jax.random.key(seed)
    ks = jax.random.split(key, 12)
    f32 = jnp.float32
    nrm = lambda k, shape, s: jax.random.normal(k, shape, f32) * s
    return {
        "x_prompt": nrm(ks[0], (BATCH, SEQ, D_MODEL), 1.0),
        "x_sample": nrm(ks[1], (DEC_BATCH, DEC_SEQ, D_MODEL), 1.0),
        "norm_pre": 1.0 + nrm(ks[2], (DEPTH, D_MODEL), 0.05),
        "w_in": nrm(ks[3], (DEPTH, D_MODEL, IN_WIDTH), D_MODEL ** -0.5),
        "w_pool_group": nrm(ks[4], (DEPTH, N_POOL_GROUPS, POOL_GROUP, POOL_GROUP), POOL_GROUP ** -0.5),
        "pool_scale": 1.0 + nrm(ks[5], (DEPTH, POOL_WIDTH), 0.1),
        "w_pool_proj": nrm(ks[6], (DEPTH, POOL_WIDTH, D_MODEL), POOL_WIDTH ** -0.5),
        "attn_sink": nrm(ks[7], (DEPTH, N_HEADS), 1.0),
        "w_attn_proj": nrm(ks[8], (DEPTH, ATTN_WIDTH, D_MODEL), ATTN_WIDTH ** -0.5),
        "w_out": nrm(ks[9], (DEPTH, D_MODEL, D_MODEL), D_MODEL ** -0.5),
        "norm_post": 1.0 + nrm(ks[10], (DEPTH, D_MODEL), 0.05),
    }


def reference(x_prompt, x_sample, norm_pre, w_in, w_pool_group, pool_scale, w_pool_proj,
              attn_sink, w_attn_proj, w_out, norm_post):
    y_prompt = x_prompt
    y_sample = x_sample
    for l in range(DEPTH):
        y_prompt = encoder_layer(y_prompt, norm_pre[l], w_in[l], w_pool_group[l], pool_scale[l],
                                 w_pool_proj[l], attn_sink[l], w_attn_proj[l], w_out[l], norm_post[l])
        y_sample = encoder_layer(y_sample, norm_pre[l], w_in[l], w_pool_group[l], pool_scale[l],
                                 w_pool_proj[l], attn_sink[l], w_attn_proj[l], w_out[l], norm_post[l])
    return (y_prompt, y_sample)
```

```python
import numpy as np
import ml_dtypes
from contextlib import ExitStack
import concourse.bass as bass
import concourse.mybir as mybir
from concourse.bass_utils import run_bass_kernel_spmd

F32 = mybir.dt.float32
BF16 = mybir.dt.bfloat16
AF = mybir.ActivationFunctionType
ALU = mybir.AluOpType

D = 1024
NH, NKV, HD = 16, 4, 64
TILE = 512
BLK = 128
POOL_WINDOWS = (2, 4, 8, 16)
EPS = 1e-6
THETA = 10000.0
IN_W = 6656
C_U, C_PG, C_Q, C_K, C_V, C_AG, C_GP, C_GA = 0, 1024, 2048, 3072, 3328, 3584, 4608, 5632
NRING = 4


class Op:
    __slots__ = ("eng", "fn", "reads", "writes", "dma", "chan", "ndma", "idx", "deps", "sig", "waits", "name")


class Prog:
    ENGS = ("sp", "act", "dve", "pool", "pe")

    def __init__(self, same_eng_dist=3):
        self.ops = {e: [] for e in self.ENGS}
        self.last_w = {}
        self.readers = {}
        self.same_eng_dist = same_eng_dist
        self.all = []

    def add(self, eng, fn, reads=(), writes=(), dma=False, chan=None, ndma=1, name=""):
        o = Op()
        o.eng = eng; o.fn = fn; o.reads = tuple(reads); o.writes = tuple(writes)
        o.dma = dma; o.chan = chan; o.ndma = ndma; o.name = name
        o.idx = len(self.ops[eng]); o.deps = []; o.sig = None; o.waits = []
        deps = {}
        for r in o.reads:
            w = self.last_w.get(r)
            if w is not None:
                deps[id(w)] = (w, True)
            if r.startswith("bank") or r == "pTb":
                for rd in self.readers.get(r, ()):
                    if id(rd) not in deps and rd.eng != eng:
                        deps[id(rd)] = (rd, False)
        for r in o.writes:
            w = self.last_w.get(r)
            if w is not None and id(w) not in deps:
                deps[id(w)] = (w, False)
            for rd in self.readers.get(r, ()):
                if id(rd) not in deps:
                    deps[id(rd)] = (rd, False)
        for d, israw in deps.values():
            if d is o:
                continue
            if (not d.dma) and (not o.dma) and d.eng == o.eng:
                if o.eng == "pe":
                    continue
                if not israw:
                    continue
                if o.idx - d.idx >= self.same_eng_dist:
                    continue
            o.deps.append(d)
        for r in o.reads:
            self.readers.setdefault(r, []).append(o)
        for r in o.writes:
            self.last_w[r] = o
            self.readers[r] = []
        self.ops[eng].append(o)
        self.all.append(o)
        return o

    def finalize(self):
        need = set()
        for o in self.all:
            for d in o.deps:
                need.add(id(d))
        cnt = {}
        for e in self.ENGS:
            for o in self.ops[e]:
                if o.dma:
                    key = "dma_" + o.chan
                    cnt[key] = cnt.get(key, 0) + 16 * o.ndma
                    o.sig = (key, cnt[key])
                elif id(o) in need:
                    key = "eng_" + e
                    cnt[key] = cnt.get(key, 0) + 1
                    o.sig = (key, cnt[key])
        self.final_counts = cnt
        know_eng = {e: {} for e in self.ENGS}
        know_op = {}
        for o in self.all:
            ke = know_eng[o.eng]
            w = {}
            src = {}
            for d in o.deps:
                k, v = d.sig
                if ke.get(k, 0) >= v:
                    continue
                if v > w.get(k, 0):
                    w[k] = v
                    src[k] = d
            for k, v in w.items():
                ke[k] = v
                for kk, vv in know_op.get(id(src[k]), {}).items():
                    if vv > ke.get(kk, 0):
                        ke[kk] = vv
            o.waits = sorted(w.items())
            if o.sig is not None and id(o) in need or o.dma:
                snap = dict(ke)
                snap[o.sig[0]] = max(snap.get(o.sig[0], 0), o.sig[1])
                know_op[id(o)] = snap
        return sorted(cnt.keys())


def stream_elements():
    el = []
    el.append(("u0", [("win", C_U, 512)]))
    el.append(("u1", [("win", C_U + 512, 512)]))
    el.append(("vv", [("win", C_V, 256), ("win", C_V, 256)]))
    el.append(("kk", [("win", C_K + 64 * (i // 2), 64) for i in range(8)]))
    el.append(("pg0", [("win", C_PG, 512)]))
    el.append(("pg1", [("win", C_PG + 512, 512)]))
    for g in range(4):
        el.append((f"qa{g}", [("win", C_Q + 256 * g, 256), ("win", C_AG + 256 * g, 256)]))
    for op in range(4):
        el.append((f"mg{op}", [("win", C_GP + 256 * op, 256), ("win", C_GA + 256 * op, 256)]))
        el.append((f"pr{op}", [("wpp", 256 * op, 256), ("wap", 256 * op, 256)]))
    el.append(("wo0", [("wout", 0, 512)]))
    el.append(("wo1", [("wout", 512, 512)]))
    return el


ELEMS = stream_elements()
ELEM_IDX = {n: i for i, (n, _) in enumerate(ELEMS)}
NEL = len(ELEMS)


def build_program(segs, dbg=False, limit=99):
    nc = bass.Bass("TRN2", target_bir_lowering=False)
    ntok = sum(n for _, n in segs)
    nsamp = sum(1 for k, _ in segs if k == "sample")
    ntab = sum(n + 256 for _, n in segs)

    def din(name, shape, dt=F32):
        return nc.dram_tensor(name, list(shape), dt, kind="ExternalInput").ap()

    xm = din("xm", [ntok, D])
    xh = din("xh", [max(nsamp, 1) * 2, BLK, D])
    w_in = din("w_in", [D, IN_W])
    w_pg = din("w_pg", [4, 256, 256])
    w_pp = din("w_pp", [D, D])
    w_ap = din("w_ap", [D, D])
    w_out = din("w_out", [D, D])
    gpre_b = din("gpre_b", [128, D])
    pscale_c = din("pscale_c", [128, 8])
    gpost_b = din("gpost_b", [128, D])
    sink_rows = din("sink_rows", [1, 16])
    tabs = din("tabs", [2, 128, ntab])
    bands_d = din("bands", [128, 28 * 128], BF16)
    masks_d = din("masks", [128, 4 * 512], BF16)
    cst_d = din("cst", [128, 2 * 128], BF16)
    sel_d = din("sel", [1, 2 * 128], BF16)
    ym = nc.dram_tensor("ym", [ntok, D], F32, kind="ExternalOutput").ap()
    wsc = nc.dram_tensor("wsc", [NEL, 128, 8 * 512], BF16, kind="Internal").ap()
    dbg_out = {}
    if dbg:
        for nm, shp, dt in (("d_hT", [128, 8 * 1024], BF16), ("d_kT", [128, 4 * 768], BF16),
                            ("d_ub", [128, 6 * 1024], BF16), ("d_va", [128, 6 * 4 * 192], BF16),
                            ("d_pbT", [128, 8 * 512], BF16), ("d_abT", [128, 8 * 512], BF16),
                            ("d_mT", [128, 8 * 512], BF16)):
            dbg_out[nm] = nc.dram_tensor(nm, shp, dt, kind="ExternalOutput").ap()

    wsrc = {"win": w_in, "wpp": w_pp, "wap": w_ap, "wout": w_out}
    P = Prog()
    es = ExitStack()
    with es:
        def sb(name, shape, dt):
            return es.enter_context(nc.sbuf_tensor("s_" + name, list(shape), dt))

        def ps(name, shape, dt):
            return es.enter_context(nc.psum_tensor("p_" + name, list(shape), dt))

        ring = [sb(f"ring{i}", [128, 8, 512], BF16) for i in range(NRING)]
        wpg = sb("wpg", [128, 4, 2, 256], BF16)
        xbuf = [sb(f"xbuf{i}", [128, D], F32) for i in range(2)]
        ybuf = [sb(f"ybuf{i}", [128, D], F32) for i in range(2)]
        xs = [sb(f"xs{i}", [128, D], BF16) for i in range(2)]
        hT = sb("hT", [128, 8, 8 * 128], BF16)
        kpre = [sb(f"kpre{i}", [128, 768], BF16) for i in range(2)]
        kT = sb("kT", [128, 4, 768], BF16)
        vaug = sb("vaug", [128, 6, 4, 192], BF16)
        ub = sb("ub", [128, 6, D], BF16)
        poolT = [sb(f"poolT{i}", [128, 2, 512], BF16) for i in range(2)]
        sg = [sb("sg0", [128, 512], BF16)] * 2
        pbT = sb("pbT", [128, 8, 512], BF16)
        abT = sb("abT", [128, 8, 512], BF16)
        mT = sb("mT", [128, 8, 512], BF16)
        qpre = [sb(f"qpre{i}", [128, 512], BF16) for i in range(2)]
        qg = [sb(f"qg{i}", [128, 2, 512], BF16) for i in range(2)]
        sag = [sb(f"sag{i}", [128, 2, 512], BF16) for i in range(2)]
        rt1 = [sb("rt1_0", [128, 768], F32)] * 2
        rt2 = [sb("rt2_0", [128, 768], F32)] * 2
        tab = sb("tab", [128, 2, 768], F32)
        PT = [sb(f"PT{i}", [128, 3, 512], BF16) for i in range(3)]
        rden = [sb(f"rden{i}", [128, 256], F32) for i in range(2)]
        otmp = [sb(f"otmp{i}", [128, 256], F32) for i in range(2)]
        gsig = [sb(f"gsig{i}", [128, 512], F32) for i in range(2)] * 2
        mt1 = [sb("mt1_0", [128, 512], F32)] * 2
        mt2 = [sb("mt2_0", [128, 512], F32)] * 2
        ytmp = [sb(f"ytmp{i}", [128, 512], F32) for i in range(2)]
        stat = sb("stat", [128, 64], F32)
        gpre = sb("gpre", [128, D], F32)
        pscale = sb("pscale", [128, 8], F32)
        psh = sb("psh", [128, 8], F32)
        gpost = sb("gpost", [128, D], F32)
        sinkf = sb("sinkf", [1, 16], F32)
        sinkb = sb("sinkb", [1, 16], BF16)
        sinkT = sb("sinkT", [128, 4, 256], F32)
        pvs = [sb(f"pvs{i}", [128, 512], F32) for i in range(2)]
        bands = sb("bands", [128, 28, 128], BF16)
        masks = sb("masks", [128, 4, 512], BF16)
        cst = sb("cst", [128, 2, 128], BF16)
        sel = sb("sel", [1, 2, 128], BF16)
        epsT = sb("epsT", [128, 1], F32)
        bank = [ps(f"bank{i}", [128, 512], F32) for i in range(8)]
        pTb = bank[7][:].bitcast(BF16).rearrange("p (c t) -> p c t", c=8)

        ident = cst[:, 0, :]
        perm = cst[:, 1, :]

        state = {"gen": 0, "genlist": list(range(7)), "rr": 0, "stream": 0, "n": 0}

        def gbank():
            lst = state["genlist"]
            b = lst[state["gen"] % len(lst)]
            state["gen"] += 1
            return b

        def ew_eng():
            e = ("dve", "pool", "act")[state["rr"] % 3]
            state["rr"] += 1
            return e

        def uniq(p):
            state["n"] += 1
            return f"{p}{state['n']}"

        def ld(dst_ap, src_ap, res, chan):
            P.add("sp", lambda e: e.dma_start(out=dst_ap, in_=src_ap), writes=[res], dma=True, chan=chan)

        ld(gpre[:], gpre_b, "gpre", "c0")
        ld(pscale[:], pscale_c, "pscale", "c1")
        ld(gpost[:], gpost_b, "gpost", "c2")
        ld(sinkf[:], sink_rows, "sinkf", "c3")
        ld(bands[:].rearrange("p a b -> p (a b)"), bands_d, "bands", "c4")
        ld(masks[:].rearrange("p a b -> p (a b)"), masks_d, "masks", "c5")
        ld(cst[:].rearrange("p a b -> p (a b)"), cst_d, "cst", "c6")
        ld(sel[:].rearrange("p a b -> p (a b)"), sel_d, "sel", "c7")
        P.add("act", lambda e: e.activation(out=sinkb[:], in_=sinkf[:], func=AF.Exp),
              reads=["sinkf"], writes=["sinkb"])
        for g in range(4):
            for par in range(2):
                sb0 = sinkb[0:1, 4 * g + par:4 * g + par + 1]
                srhs = bass.AP(sinkb, sb0.offset, [[sb0.ap[0][0], 1], [2, 2], [0, 128]])
                P.add("pe", lambda e, g=g, par=par, srhs=srhs: e.matmul(bank[g // 2][:, (g % 2) * 256:(g % 2 + 1) * 256].rearrange("p (j q) -> p j q", j=2),
                                                                     lhsT=sel[0:1, par, :], rhs=srhs, start=(par == 0), stop=(par == 1)),
                      reads=["sel", "sinkb"], writes=[f"bank{g // 2}"])
        for h2 in range(2):
            P.add("dve", lambda e, h2=h2: e.tensor_copy(out=sinkT[:, 2 * h2:2 * h2 + 2, :].rearrange("p a b -> p (a b)"), in_=bank[h2][:, :]),
                  reads=[f"bank{h2}"], writes=["sinkT"])
        P.add("dve", lambda e: e.memset(epsT[:], EPS), writes=["epsT"])
        P.add("dve", lambda e: e.tensor_scalar(out=psh[:], in0=pscale[:], scalar1=0.25, scalar2=None, op0=ALU.mult), reads=["pscale"], writes=["psh"])
        P.add("pool", lambda e: e.memset(vaug[:, :, :, 64:128], 4.0), writes=["vaug_ones"])

        stg32 = [(xbuf[0], "xbuf0"), (xbuf[1], "xbuf1"), (ybuf[0], "ybuf0"), (ybuf[1], "ybuf1")]
        for half in range(2):
            buf, res = stg32[half]
            src = w_pg[2 * half:2 * half + 2].rearrange("g (kc p) d -> p g kc d", p=128)
            dst = buf[:].rearrange("p (g kc d) -> p g kc d", g=2, kc=2)
            P.add("sp", lambda e, dst=dst, src=src: e.dma_start(out=dst, in_=src), writes=[res], dma=True, chan="pl" + res)
            P.add("dve", lambda e, buf=buf, half=half: e.tensor_copy(
                out=wpg[:, 2 * half:2 * half + 2, :, :].rearrange("p g kc d -> p (g kc d)"), in_=buf[:]),
                reads=[res], writes=["wpg"])
        for ei, (ename, pieces) in enumerate(ELEMS):
            dview = wsc[ei].rearrange("p (dc c) -> p dc c", dc=8)
            col = 0
            dmas = []
            for (src, c0, ncol) in pieces:
                sap = wsrc[src].rearrange("(dc p) c -> p dc c", p=128)[:, :, c0:c0 + ncol]
                dmas.append((dview[:, :, col:col + ncol], sap))
                col += ncol

            def fn(e, dmas=dmas):
                return [e.dma_start(out=d, in_=s_) for d, s_ in dmas]
            P.add("pool", fn, writes=[f"wsc{ei}"], dma=True, chan=f"pw{ei}", ndma=len(dmas))

        stream_order = []

        def issue_stream(k):
            if k >= len(stream_order):
                return
            ei = ELEM_IDX[stream_order[k]]
            slot = k % NRING
            P.add("sp", lambda e, ei=ei, slot=slot: e.dma_start(out=ring[slot][:].rearrange("p a b -> p (a b)"), in_=wsc[ei]),
                  reads=[f"wsc{ei}"], writes=[f"ring{slot}"], dma=True, chan=f"rg{slot}")

        def take(name):
            k = state["stream"]
            assert stream_order[k] == name, (k, stream_order[k], name)
            state["stream"] += 1
            issue_stream(k + NRING - 2)
            return ring[k % NRING], f"ring{k % NRING}"

        tiles = []
        tok0 = 0
        tabo = 0
        si = 0
        for kind, n in segs:
            nblk = n // BLK
            for t in range(n // TILE):
                tiles.append(dict(kind=kind, nblk=nblk, b0=4 * t, tok0=tok0 + t * TILE, tabo=tabo, si=si, first=(t == 0), last=(t == n // TILE - 1)))
            tok0 += n
            tabo += n + 256
            if kind == "sample":
                si += 1
        for _ in tiles:
            stream_order.extend(n for n, _ in ELEMS)
        for k in range(NRING - 2):
            issue_stream(k)

        def front_a(tl, gb):
            kind, nblk = tl["kind"], tl["nblk"]
            if gb < 0 or gb >= nblk:
                if kind == "prompt":
                    return None
                src = xh[2 * tl["si"] + (0 if gb < 0 else 1)]
            else:
                t0 = tl["tok0"] - tl["b0"] * BLK + gb * BLK
                src = xm[t0:t0 + BLK, :]
            i = state.setdefault("xb", 0)
            state["xb"] += 1
            xb, xres = xbuf[i % 2], f"xbuf{i % 2}"
            xs_, xsres = xs[i % 2], f"xs{i % 2}"
            c = (i % 8) * 4
            P.add("sp", lambda e: e.dma_start(out=xb[:], in_=src), writes=[xres], dma=True, chan="ld" + xres)
            P.add("dve", lambda e: e.scalar_tensor_tensor(out=xs_[:], in0=xb[:], scalar=1.0, in1=xb[:], op0=ALU.mult, op1=ALU.mult, accum_out=stat[:, c:c + 1]),
                  reads=[xres], writes=[xsres, f"st{c}"])
            P.add("act", lambda e: e.activation(out=stat[:, c + 1:c + 2], in_=stat[:, c:c + 1], func=AF.Sqrt, bias=epsT[:], scale=1.0 / D),
                  reads=[f"st{c}", "epsT"], writes=[f"st{c + 1}"])
            P.add("dve", lambda e: e.reciprocal(out=stat[:, c + 2:c + 3], in_=stat[:, c + 1:c + 2]), reads=[f"st{c + 1}"], writes=[f"st{c + 2}"])
            P.add("dve", lambda e: e.scalar_tensor_tensor(out=xs_[:], in0=xb[:], scalar=stat[:, c + 2:c + 3], in1=gpre[:], op0=ALU.mult, op1=ALU.mult),
                  reads=[xres, f"st{c + 2}", "gpre"], writes=[xsres])
            return (gb % 8, xs_, xsres)

        def front_b(tok):
            if tok is None:
                return
            slot, xs_, xsres = tok
            for ch in range(8):
                P.add("pe", lambda e, ch=ch: e.transpose(pTb[:, ch, :], xs_[:, ch * 128:(ch + 1) * 128], ident),
                      reads=[xsres, "cst"], writes=["bank7"])
            P.add("act", lambda e: e.activation(out=hT[:, :, slot * 128:(slot + 1) * 128], in_=pTb, func=AF.Copy),
                  reads=["bank7"], writes=[f"hT{slot}"])

        def front(tl, gb):
            front_b(front_a(tl, gb))

        def present(tl, gb):
            return not (tl["kind"] == "prompt" and (gb < 0 or gb >= tl["nblk"]))

        def evac_copy(i, out_ap, in_ap, reads, writes):
            if i % 2 == 0:
                P.add("act", lambda e: e.activation(out=out_ap, in_=in_ap, func=AF.Copy), reads=reads, writes=writes)
            else:
                P.add("dve", lambda e: e.tensor_copy(out=out_ap, in_=in_ap), reads=reads, writes=writes)

        def inproj_fm(el, elres, q, rhs_ap, rhs_res, n, bk, col0=0):
            for dc in range(8):
                P.add("pe", lambda e, dc=dc: e.matmul(bank[bk][:, col0:col0 + n], lhsT=el[:, dc, q * 128:(q + 1) * 128],
                                                      rhs=rhs_ap(dc), start=(dc == 0), stop=(dc == 7)),
                      reads=[elres] + rhs_res, writes=[f"bank{bk}"])

        def tile_body(ti, tl):
            b0, nblk, kind = tl["b0"], tl["nblk"], tl["kind"]
            state["genlist"] = list(range(7))
            pend_ = None
            for gb in range(b0 - 1 if tl["first"] else b0 + 1, b0 + 5):
                if (ti, gb) not in state.setdefault("fronted", set()):
                    tok_ = front_a(tl, gb)
                    front_b(pend_)
                    pend_ = tok_
            front_b(pend_)
            nxt = tiles[ti + 1] if ti + 1 < len(tiles) else None
            hq = {"L": [], "t5only": set(), "pend": None, "pend_gb": None}
            if nxt is not None:
                nb0 = nxt["b0"]
                live = {(b0 + j) % 8 for j in range(4)}
                cand = [gb for gb in range(nb0 - 1 if nxt["first"] else nb0 + 1, nb0 + 5) if present(nxt, gb)]
                t4b = [gb for gb in cand if (gb % 8) not in live]
                t5b = [gb for gb in cand if (gb % 8) in live]
                hq["L"] = t4b + t5b
                hq["t5only"] = set(t5b)
                for gb in cand:
                    state.setdefault("fronted", set()).add((ti + 1, gb))

            def hoist_event(in_t5):
                if hq["pend"] is not None and (in_t5 or hq["pend_gb"] not in hq["t5only"]):
                    front_b(hq["pend"])
                    hq["pend"] = None
                if hq["pend"] is None and hq["L"]:
                    gb_ = hq["L"].pop(0)
                    hq["pend"] = front_a(nxt, gb_)
                    hq["pend_gb"] = gb_
            mslot = (b0 % 8)
            main_res = [f"hT{mslot + j}" for j in range(4)]
            hmain = lambda dc: hT[:, dc, mslot * 128:(mslot + 4) * 128]
            kbs = [kb for kb in range(6) if present(tl, b0 - 1 + kb)]
            usl = lambda kb: (b0 + kb) % 6
            newkb = [kb for kb in kbs if tl["first"] or kb >= 2]
            hblk = lambda kb, dc: hT[:, dc, ((b0 - 1 + kb) % 8) * 128:((b0 - 1 + kb) % 8 + 1) * 128]
            hres = lambda kb: [f"hT{(b0 - 1 + kb) % 8}"]
            if limit < 2:
                return
            to = tl["tabo"] + b0 * BLK
            P.add("sp", lambda e, to=to: e.dma_start(out=tab[:], in_=tabs[:, :, to:to + 768].rearrange("a p t -> p a t")),
                  writes=["tab"], dma=True, chan="tab")
            import os as _os
            T1N = int(_os.environ.get("T1_N", "99"))
            if T1N < 3:
                return
            if T1N < 4:
                return
            elu = [take("u0"), take("u1")]
            n_e = 0
            for kb in [k_ for k_ in (1, 2, 3, 4, 0, 5) if k_ in newkb]:
                for half in range(2):
                    bk = gbank()
                    el_, r_ = elu[half]
                    for dc in range(8):
                        P.add("pe", lambda e, dc=dc, kb=kb, el_=el_, bk=bk: e.matmul(bank[bk][:, :], lhsT=hblk(kb, dc), rhs=el_[:, dc, :], start=(dc == 0), stop=(dc == 7)),
                              reads=[r_] + hres(kb), writes=[f"bank{bk}"])
                    evac_copy(n_e, ub[:, usl(kb), half * 512:(half + 1) * 512], bank[bk][:, :], [f"bank{bk}"], [f"ub{usl(kb)}_{half}"])
                    n_e += 1
            el, elres = take("vv")
            for pr in range(3):
                bk = gbank()
                any_ = False
                for j in range(2):
                    kb = 2 * pr + j
                    if kb not in newkb:
                        continue
                    any_ = True
                    for dc in range(8):
                        P.add("pe", lambda e, dc=dc, kb=kb, j=j, bk=bk: e.matmul(bank[bk][:, j * 256:(j + 1) * 256], lhsT=hblk(kb, dc), rhs=el[:, dc, 0:256],
                                                                         start=(dc == 0), stop=(dc == 7)),
                              reads=[elres] + hres(kb), writes=[f"bank{bk}"])
                    for cp, off in enumerate((0, 128)):
                        evac_copy(kb + cp, vaug[:, usl(kb), :, off:off + 64], bank[bk][:, j * 256:(j + 1) * 256].rearrange("p (g d) -> p g d", g=4),
                                  [f"bank{bk}"], [f"vaug{usl(kb)}_{cp}"])
            elk, elkres = take("kk")

            kbanks = {}

            def k_main(g):
                bkm = gbank()
                kbanks[g] = bkm
                for dc in range(8):
                    P.add("pe", lambda e, dc=dc: e.matmul(bank[bkm][:, :], lhsT=elk[:, dc, g * 128:(g + 1) * 128], rhs=hmain(dc), start=(dc == 0), stop=(dc == 7)),
                          reads=[elkres] + main_res, writes=[f"bank{bkm}"])

            def k_halo(g):
                kp = kpre[g % 2]
                kr = f"kpre{g % 2}"
                bkm = kbanks[g]
                bkh = gbank()
                for hi, kb in enumerate((0, 5)):
                    if kb not in kbs:
                        continue
                    for dc in range(8):
                        P.add("pe", lambda e, dc=dc, hi=hi, kb=kb: e.matmul(bank[bkh][:, hi * 128:(hi + 1) * 128], lhsT=elk[:, dc, g * 128:(g + 1) * 128], rhs=hblk(kb, dc),
                                                                         start=(dc == 0), stop=(dc == 7)),
                              reads=[elkres] + hres(kb), writes=[f"bank{bkh}"])
                P.add("act", lambda e: e.activation(out=kp[:, 128:640], in_=bank[bkm][:, :], func=AF.Copy), reads=[f"bank{bkm}"], writes=[kr + "_m"])
                for hi, kb in enumerate((0, 5)):
                    if kb in kbs:
                        P.add("dve", lambda e, hi=hi, kb=kb: e.tensor_copy(out=kp[:, kb * 128:(kb + 1) * 128], in_=bank[bkh][:, hi * 128:(hi + 1) * 128]),
                              reads=[f"bank{bkh}"], writes=[kr + f"_h{hi}"])
                    else:
                        P.add("dve", lambda e, kb=kb: e.memset(kp[:, kb * 128:(kb + 1) * 128], 0.0), writes=[kr + f"_h{hi}"])

            def k_rope(g):
                kp = kpre[g % 2]
                kr = f"kpre{g % 2}"
                r = g % 2
                bp0 = gbank()
                bp1 = gbank()
                P.add("pe", lambda e: e.matmul(bank[bp0][:, :], lhsT=perm, rhs=kp[:, 0:512], start=True, stop=True),
                      reads=["cst", kr + "_m", kr + "_h0"], writes=[f"bank{bp0}"])
                P.add("pe", lambda e: e.matmul(bank[bp1][:, 0:256], lhsT=perm, rhs=kp[:, 512:768], start=True, stop=True),
                      reads=["cst", kr + "_m", kr + "_h1"], writes=[f"bank{bp1}"])
                P.add("pool", lambda e: e.tensor_tensor(out=rt1[r][:, :], in0=kp[:, :], in1=tab[:, 0, :], op=ALU.mult),
                      reads=[kr + "_m", kr + "_h0", kr + "_h1", "tab"], writes=["rt1_0"])
                P.add("dve", lambda e: e.tensor_tensor(out=rt2[r][:, 0:512], in0=bank[bp0][:, :], in1=tab[:, 1, 0:512], op=ALU.mult),
                      reads=[f"bank{bp0}", "tab"], writes=["rt2_0a"])
                P.add("dve", lambda e: e.tensor_tensor(out=rt2[r][:, 512:768], in0=bank[bp1][:, 0:256], in1=tab[:, 1, 512:768], op=ALU.mult),
                      reads=[f"bank{bp1}", "tab"], writes=["rt2_0b"])
                P.add("pool", lambda e: e.tensor_tensor(out=kT[:, g, :], in0=rt1[r][:, :], in1=rt2[r][:, :], op=ALU.add),
                      reads=["rt1_0", "rt2_0a", "rt2_0b"], writes=[f"kT{g}"])

            for g in range(4):
                k_main(g)
            state["genlist"] = [b for b in range(7) if b not in kbanks.values()]
            k_halo(0)
            for g in range(4):
                if g + 1 < 4:
                    k_halo(g + 1)
                k_rope(g)
            state["genlist"] = list(range(7))
            if dbg and ti == 0:
                P.add("sp", lambda e: e.dma_start(out=dbg_out["d_hT"], in_=hT[:].rearrange("p a b -> p (a b)")), reads=[f"hT{s}" for s in range(8)], dma=True, chan="dbg")
                P.add("sp", lambda e: e.dma_start(out=dbg_out["d_kT"], in_=kT[:].rearrange("p a b -> p (a b)")), reads=[f"kT{g}" for g in range(4)], dma=True, chan="dbg")
                P.add("sp", lambda e: e.dma_start(out=dbg_out["d_ub"], in_=ub[:].rearrange("p a b -> p (a b)")), reads=[f"ub{usl(kb)}_{h}" for kb in kbs for h in range(2)], dma=True, chan="dbg")
                P.add("sp", lambda e: e.dma_start(out=dbg_out["d_va"], in_=vaug[:].rearrange("p a b c -> p (a b c)")),
                      reads=[f"vaug{usl(kb)}_{cp}" for kb in kbs for cp in range(2)] + ["vaug_ones"], dma=True, chan="dbg")

            if limit < 3:
                return
            elpg = [take("pg0"), take("pg1")]
            for g in range(4):
                pt = poolT[g % 2]
                ptres = f"poolT{g % 2}"
                for kc in range(2):
                    c = 2 * g + kc
                    bk = gbank()
                    for ob in range(4):
                        gb = b0 + ob
                        terms = []
                        for rel in (-1, 0, 1):
                            sgb = gb + rel
                            if kind == "prompt" and (sgb < 0 or sgb >= nblk):
                                continue
                            if rel == 0:
                                if gb == 0:
                                    bi = (12 + g) if kind == "prompt" else (20 + g)
                                elif gb == nblk - 1:
                                    bi = (16 + g) if kind == "prompt" else (24 + g)
                                else:
                                    bi = 3 * g + 1
                            else:
                                bi = 3 * g + (0 if rel < 0 else 2)
                            terms.append((ob + 1 + rel, bi))
                        for i, (kb, bi) in enumerate(terms):
                            P.add("pe", lambda e, kb=kb, bi=bi, c=c, ob=ob, i=i, nt=len(terms), bk=bk: e.matmul(
                                bank[bk][:, ob * 128:(ob + 1) * 128], lhsT=ub[:, (b0 + kb) % 6, c * 128:(c + 1) * 128], rhs=bands[:, bi, :],
                                start=(i == 0), stop=(i == nt - 1)),
                                reads=[f"ub{usl(kb)}_{c // 4}", "bands"], writes=[f"bank{bk}"])
                    evac_copy(c, pt[:, kc, :], bank[bk][:, :], [f"bank{bk}"], [ptres + f"_{kc}"])
                gtmp = []
                for oh in range(2):
                    oc = 2 * g + oh
                    bkg = gbank()
                    el_, r_ = elpg[oc // 4]
                    inproj_fm(el_, r_, oc % 4, hmain, main_res, 512, bkg)
                    s_ = sg[0]
                    yt = ytmp[oh]
                    P.add("act", lambda e, s_=s_, bkg=bkg: e.activation(out=s_[:], in_=bank[bkg][:, :], func=AF.Tanh, scale=0.5), reads=[f"bank{bkg}"], writes=["sg0"])
                    P.add("dve", lambda e, s_=s_, bkg=bkg, yt=yt: e.scalar_tensor_tensor(out=yt[:], in0=s_[:], scalar=1.0, in1=bank[bkg][:, :], op0=ALU.add, op1=ALU.mult),
                          reads=[f"bank{bkg}", "sg0"], writes=[f"ytmp{oh}"])
                for oh in range(2):
                    oc = 2 * g + oh
                    bkw = gbank()
                    yt = ytmp[oh]
                    for kc in range(2):
                        P.add("pe", lambda e, kc=kc, oh=oh, g=g, pt=pt, bkw=bkw: e.matmul(bank[bkw][:, :], lhsT=wpg[:, g, kc, oh * 128:(oh + 1) * 128], rhs=pt[:, kc, :],
                                                                                 start=(kc == 0), stop=(kc == 1)),
                              reads=["wpg", ptres + "_0", ptres + "_1"], writes=[f"bank{bkw}"])
                    P.add("dve", lambda e, bkw=bkw, oc=oc, yt=yt: e.scalar_tensor_tensor(out=pbT[:, oc, :], in0=bank[bkw][:, :], scalar=psh[:, oc:oc + 1], in1=yt[:],
                                                                                  op0=ALU.mult, op1=ALU.mult),
                          reads=[f"bank{bkw}", f"ytmp{oh}", "psh"], writes=["pbT", f"pbT{oc}"])
            if dbg and ti == 0:
                P.add("sp", lambda e: e.dma_start(out=dbg_out["d_pbT"], in_=pbT[:].rearrange("p a b -> p (a b)")), reads=[f"pbT{o}" for o in range(8)], dma=True, chan="dbg")

            if limit < 4:
                return

            def qa_items(g):
                box = {}
                qq = qg[g % 2]
                qres = f"qg{g % 2}"
                sa = sag[g % 2]
                sres = f"sag{g % 2}"

                def get_el():
                    if "el" not in box:
                        box["el"] = take(f"qa{g}")
                    return box["el"]

                def q_item(j):
                    el, elres = get_el()
                    bk = gbank()
                    inproj_fm(el, elres, j, hmain, main_res, 512, bk)
                    qp = qpre[j]
                    P.add("act", lambda e: e.activation(out=qp[:], in_=bank[bk][:, :], func=AF.Copy), reads=[f"bank{bk}"], writes=[f"qpre{j}"])
                    bp = 6

                    def tail():
                        P.add("pe", lambda e: e.matmul(bank[bp][:, :], lhsT=perm, rhs=qp[:], start=True, stop=True), reads=["cst", f"qpre{j}"], writes=[f"bank{bp}"])
                        P.add("pool", lambda e: e.tensor_tensor(out=rt1[j][:, 0:512], in0=qp[:], in1=tab[:, 0, 128:640], op=ALU.mult),
                              reads=[f"qpre{j}", "tab"], writes=["rt1_0"])
                        P.add("dve", lambda e: e.tensor_tensor(out=rt2[j][:, 0:512], in0=bank[bp][:, :], in1=tab[:, 1, 128:640], op=ALU.mult),
                              reads=[f"bank{bp}", "tab"], writes=["rt2_0a"])
                        P.add("pool", lambda e: e.tensor_tensor(out=qq[:, j, :], in0=rt1[j][:, 0:512], in1=rt2[j][:, 0:512], op=ALU.add),
                              reads=["rt1_0", "rt2_0a"], writes=[qres + f"_{j}"])
                    return tail

                def ag_item(j):
                    el, elres = get_el()
                    bk = gbank()
                    inproj_fm(el, elres, 2 + j, hmain, main_res, 512, bk)
                    P.add("act", lambda e: e.activation(out=sa[:, j, :], in_=bank[bk][:, :], func=AF.Tanh, scale=0.5), reads=[f"bank{bk}"], writes=[sres + f"_{j}"])
                    P.add("dve", lambda e: e.scalar_tensor_tensor(out=sa[:, j, :], in0=sa[:, j, :], scalar=1.0, in1=bank[bk][:, :], op0=ALU.add, op1=ALU.mult),
                          reads=[f"bank{bk}", sres + f"_{j}"], writes=[sres + f"_{j}"])

                return [lambda: q_item(0), lambda: q_item(1), lambda: ag_item(0), lambda: ag_item(1)]

            def barrier_mm():
                bk = gbank()
                P.add("pe", lambda e: e.matmul(bank[bk][:, 0:2], lhsT=ident, rhs=cst[:, 0, 0:2], start=True, stop=True), reads=["cst"], writes=[f"bank{bk}"])

            def chunks_of(qb):
                gb = b0 + qb
                return [c for c in range(3) if not (kind == "prompt" and (gb - 1 + c < 0 or gb - 1 + c >= nblk))]

            def s_part(idx, g, qb, par):
                qq = qg[g % 2]
                qres = f"qg{g % 2}"
                sset = (0, 1, 2) if idx % 2 == 0 else (3, 4, 5)
                rows = slice(0, 64) if par == 0 else slice(64, 128)
                cols = slice(par * 256, (par + 1) * 256)
                for c in chunks_of(qb):
                    kcols = slice((qb + c) * 128, (qb + c + 1) * 128)
                    bk = sset[c]
                    P.add("pe", lambda e, bk=bk, kcols=kcols: e.matmul(bank[bk][:, cols].rearrange("p (j q) -> p j q", j=2), lhsT=kT[rows, g, kcols],
                                                                      rhs=qq[rows, :, qb * 128:(qb + 1) * 128], start=True, stop=True),
                          reads=[f"kT{g}", qres + "_0", qres + "_1"], writes=[f"bank{bk}"])

            def softmax_part(idx, g, qb):
                gb = b0 + qb
                sset = (0, 1, 2) if idx % 2 == 0 else (3, 4, 5)
                pt_ = PT[idx % 3]
                ptres = f"PT{idx % 3}"
                for c in chunks_of(qb):
                    bk = sset[c]
                    P.add("act", lambda e, c=c, bk=bk: e.activation(out=pt_[:, c, :], in_=bank[bk][:, :], func=AF.Exp, scale=HD ** -0.5),
                          reads=[f"bank{bk}"], writes=[ptres + f"_{c}"])
                    mi = None
                    if c == 0:
                        mi = 2 if (kind == "sample" and gb == 0) else 0
                    elif c == 2:
                        mi = 3 if (kind == "sample" and gb == nblk - 1) else 1
                    if mi is not None:
                        P.add("dve", lambda e, c=c, mi=mi: e.tensor_tensor(out=pt_[:, c, :], in0=pt_[:, c, :], in1=masks[:, mi, :], op=ALU.mult),
                              reads=[ptres + f"_{c}", "masks"], writes=[ptres + f"_{c}"])

            def pv_part(idx, g, qb):
                pt_ = PT[idx % 3]
                ptres = f"PT{idx % 3}"
                sa = sag[g % 2]
                sres = f"sag{g % 2}"
                chunks = chunks_of(qb)
                for par in range(2):
                    cols = slice(par * 256, (par + 1) * 256)
                    for i, c in enumerate(chunks):
                        kb = qb + c
                        lw = vaug[:, usl(kb), g, 0:128] if par == 0 else vaug[:, usl(kb), g, 64:192]
                        P.add("pe", lambda e, lw=lw, c=c, cols=cols, i=i, n=len(chunks): e.matmul(bank[6][:, cols], lhsT=lw, rhs=pt_[:, c, cols], start=(i == 0), stop=(i == n - 1)),
                              reads=[f"vaug{usl(kb)}_0", f"vaug{usl(kb)}_1", "vaug_ones", ptres + f"_{c}"], writes=["bank6"])
                pv = pvs[idx % 2]
                pres = f"pvs{idx % 2}"
                ds = rden[idx % 2]
                rd = rden[idx % 2]
                ot = otmp[idx % 2]
                P.add("act", lambda e: e.activation(out=pv[:], in_=bank[6][:, :], func=AF.Copy), reads=["bank6"], writes=[pres])
                P.add("dve", lambda e: e.tensor_tensor(out=ds[0:64, :], in0=pv[64:128, 0:256], in1=sinkT[64:128, g, :], op=ALU.add),
                      reads=[pres, "sinkT"], writes=[f"dsum{idx % 2}e"])
                P.add("dve", lambda e: e.tensor_tensor(out=ds[64:128, :], in0=pv[0:64, 256:512], in1=sinkT[0:64, g, :], op=ALU.add),
                      reads=[pres, "sinkT"], writes=[f"dsum{idx % 2}o"])
                P.add("dve", lambda e: e.reciprocal(out=rd[:, :], in_=ds[:, :]), reads=[f"dsum{idx % 2}e", f"dsum{idx % 2}o"], writes=[f"rden{idx % 2}"])
                P.add("dve", lambda e: e.tensor_tensor(out=ot[0:64, :], in0=pv[0:64, 0:256], in1=rd[0:64, :], op=ALU.mult),
                      reads=[pres, f"rden{idx % 2}"], writes=[f"otmp{idx % 2}e"])
                P.add("dve", lambda e: e.tensor_tensor(out=ot[64:128, :], in0=pv[64:128, 256:512], in1=rd[64:128, :], op=ALU.mult),
                      reads=[pres, f"rden{idx % 2}"], writes=[f"otmp{idx % 2}o"])
                P.add("pool", lambda e: e.tensor_tensor(out=abT[:, 2 * g:2 * g + 2, qb * 128:(qb + 1) * 128],
                                                        in0=ot[:, :].rearrange("p (j q) -> p j q", j=2),
                                                        in1=sa[:, :, qb * 128:(qb + 1) * 128], op=ALU.mult),
                      reads=[f"otmp{idx % 2}e", f"otmp{idx % 2}o", sres + "_0", sres + "_1"], writes=["abT", f"abT{g}_{qb}"])

            state["genlist"] = [7]
            def run_item(it):
                t_ = it()
                if t_ is not None:
                    t_()

            for it in qa_items(0):
                run_item(it)
            seq = [(g, qb) for g in range(4) for qb in range(4)]
            fillers = []
            for idx, (g, qb) in enumerate(seq):
                if qb == 0:
                    for it in fillers:
                        run_item(it)
                    fillers = qa_items(g + 1) if g < 3 else []
                s_part(idx, g, qb, 0)
                if idx > 1:
                    pv_part(idx - 2, *seq[idx - 2])
                tail_ = None
                if fillers:
                    tail_ = fillers.pop(0)()
                elif idx <= 1:
                    barrier_mm()
                s_part(idx, g, qb, 1)
                if tail_ is not None:
                    tail_()
                softmax_part(idx, g, qb)
            pv_part(len(seq) - 2, *seq[-2])
            pv_part(len(seq) - 1, *seq[-1])
            state["genlist"] = list(range(7))
            if dbg and ti == 0:
                P.add("sp", lambda e: e.dma_start(out=dbg_out["d_abT"], in_=abT[:].rearrange("p a b -> p (a b)")),
                      reads=[f"abT{g}_{qb}" for g in range(4) for qb in range(4)], dma=True, chan="dbg")

            if limit < 5:
                return
            pb_all = [f"pbT{o}" for o in range(8)]
            ab_all = [f"abT{g}_{qb}" for g in range(4) for qb in range(4)]
            for op_ in range(4):
                elm, rm = take(f"mg{op_}")
                elp, rp = take(f"pr{op_}")
                for j in range(2):
                    o = 2 * op_ + j
                    gi = 0
                    bk1 = gbank()
                    inproj_fm(elm, rm, j, hmain, main_res, 512, bk1)
                    P.add("act", lambda e, gi=gi, bk1=bk1: e.activation(out=gsig[gi][:], in_=bank[bk1][:, :], func=AF.Tanh, scale=0.5), reads=[f"bank{bk1}"], writes=[f"gsig{gi}"])
                    bk2 = gbank()
                    inproj_fm(elm, rm, 2 + j, hmain, main_res, 512, bk2)
                    P.add("act", lambda e, gi=gi, bk2=bk2: e.activation(out=gsig[gi + 1][:], in_=bank[bk2][:, :], func=AF.Tanh, scale=0.5), reads=[f"bank{bk2}"], writes=[f"gsig{gi + 1}"])
                    bk3 = gbank()
                    for k in range(8):
                        P.add("pe", lambda e, k=k, j=j, bk3=bk3, elp=elp: e.matmul(bank[bk3][:, :], lhsT=elp[:, k, j * 128:(j + 1) * 128], rhs=pbT[:, k, :], start=(k == 0), stop=(k == 7)),
                              reads=[rp] + pb_all, writes=[f"bank{bk3}"])
                    bk4 = gbank()
                    for k in range(8):
                        P.add("pe", lambda e, k=k, j=j, bk4=bk4, elp=elp: e.matmul(bank[bk4][:, :], lhsT=elp[:, k, 256 + j * 128:256 + (j + 1) * 128], rhs=abT[:, k, :], start=(k == 0), stop=(k == 7)),
                              reads=[rp] + ab_all, writes=[f"bank{bk4}"])
                    m1, m2 = mt1[o % 2], mt2[o % 2]
                    P.add("dve", lambda e, m1=m1, gi=gi, bk3=bk3: e.scalar_tensor_tensor(out=m1[:], in0=gsig[gi][:], scalar=1.0, in1=bank[bk3][:, :], op0=ALU.add, op1=ALU.mult),
                          reads=[f"bank{bk3}", f"gsig{gi}"], writes=["mt1_0"])
                    P.add("dve", lambda e, m2=m2, gi=gi, bk4=bk4: e.scalar_tensor_tensor(out=m2[:], in0=gsig[gi + 1][:], scalar=1.0, in1=bank[bk4][:, :], op0=ALU.add, op1=ALU.mult),
                          reads=[f"bank{bk4}", f"gsig{gi + 1}"], writes=["mt2_0"])
                    P.add("pool", lambda e, m1=m1, m2=m2, o=o: e.tensor_tensor(out=mT[:, o, :], in0=m1[:], in1=m2[:], op=ALU.add),
                          reads=["mt1_0", "mt2_0"], writes=["mT", f"mT{o}"])
                hoist_event(False)
            if dbg and ti == 0:
                P.add("sp", lambda e: e.dma_start(out=dbg_out["d_mT"], in_=mT[:].rearrange("p a b -> p (a b)")), reads=[f"mT{o}" for o in range(8)], dma=True, chan="dbg")

            if limit < 6:
                return
            elo = [take("wo0"), take("wo1")]
            m_all = [f"mT{o}" for o in range(8)]
            for ob in range(4):
                t0 = tl["tok0"] + ob * BLK
                i = state.setdefault("yb", 0)
                state["yb"] += 1
                yb, yres = ybuf[i % 2], f"ybuf{i % 2}"
                c = 32 + (i % 4) * 8
                P.add("pool", lambda e, yb=yb, t0=t0: e.dma_start(out=yb[:], in_=xm[t0:t0 + BLK, :]), writes=[yres], dma=True, chan="ld" + yres)
                bks = [gbank(), gbank()]
                for half in range(2):
                    el_, r_ = elo[half]
                    for k in range(8):
                        P.add("pe", lambda e, k=k, half=half, ob=ob, el_=el_, bks=bks: e.matmul(bank[bks[half]][:, :], lhsT=mT[:, k, ob * 128:(ob + 1) * 128], rhs=el_[:, k, :],
                                                                                     start=(k == 0), stop=(k == 7)),
                              reads=[r_] + m_all, writes=[f"bank{bks[half]}"])
                    P.add("act", lambda e, half=half, bks=bks, c=c: e.activation(out=PT[0][:, half, :], in_=bank[bks[half]][:, :], func=AF.Square,
                                                                              accum_out=stat[:, c + half:c + half + 1]),
                          reads=[f"bank{bks[half]}"], writes=[f"PT0_{half}", f"st{c + half}"])
                P.add("dve", lambda e, c=c: e.tensor_tensor(out=stat[:, c + 2:c + 3], in0=stat[:, c:c + 1], in1=stat[:, c + 1:c + 2], op=ALU.add),
                      reads=[f"st{c}", f"st{c + 1}"], writes=[f"st{c + 2}"])
                P.add("act", lambda e, c=c: e.activation(out=stat[:, c + 3:c + 4], in_=stat[:, c + 2:c + 3], func=AF.Sqrt, bias=epsT[:], scale=1.0 / D),
                      reads=[f"st{c + 2}", "epsT"], writes=[f"st{c + 3}"])
                P.add("dve", lambda e, c=c: e.reciprocal(out=stat[:, c + 4:c + 5], in_=stat[:, c + 3:c + 4]), reads=[f"st{c + 3}"], writes=[f"st{c + 4}"])
                for half in range(2):
                    yt = ytmp[half]
                    P.add("dve", lambda e, half=half, yt=yt, bks=bks, c=c: e.scalar_tensor_tensor(out=yt[:], in0=bank[bks[half]][:, :], scalar=stat[:, c + 4:c + 5],
                                                                                        in1=gpost[:, half * 512:(half + 1) * 512], op0=ALU.mult, op1=ALU.mult),
                          reads=[f"bank{bks[half]}", f"st{c + 4}", "gpost"], writes=[f"ytmp{half}"])
                    P.add("pool", lambda e, half=half, yt=yt, yb=yb: e.tensor_tensor(out=yb[:, half * 512:(half + 1) * 512], in0=yt[:], in1=yb[:, half * 512:(half + 1) * 512], op=ALU.add),
                          reads=[f"ytmp{half}", yres], writes=[yres])
                P.add("pool", lambda e, yb=yb, t0=t0: e.dma_start(out=ym[t0:t0 + BLK, :], in_=yb[:]), reads=[yres], dma=True, chan="st" + yres)
                hoist_event(True)
            while hq["pend"] is not None or hq["L"]:
                hoist_event(True)


        for ti, tl in enumerate(tiles):
            if limit < 1 or (limit < 6 and ti > 0):
                break
            tile_body(ti, tl)

        fin = sb("fin", [128, 8], F32)
        P.add("act", lambda e: e.activation(out=fin[:, 0:1], in_=epsT[:], func=AF.Copy), reads=["epsT"], writes=["fin_act"])
        P.add("dve", lambda e: e.memset(fin[:, 1:2], 0.0), writes=["fin_dve"])
        P.add("pool", lambda e: e.memset(fin[:, 2:3], 0.0), writes=["fin_pool"])
        P.add("pe", lambda e: e.matmul(bank[0][:, 0:1], lhsT=sel[0:1, 0, :], rhs=sel[0:1, 0, 0:1], start=True, stop=True), reads=["sel"], writes=["bank0"])
        P.add("dve", lambda e: e.tensor_copy(out=fin[:, 3:4], in_=bank[0][:, 0:1]), reads=["bank0"], writes=["fin_pe"])
        P.add("sp", lambda e: e.dma_start(out=fin[:, 5:6], in_=fin[:, 4:5]), reads=["fin_act", "fin_dve", "fin_pool", "fin_pe"], writes=["fin_sp"], dma=True, chan="fin")

        keys = P.finalize()
        sems = {k: es.enter_context(nc.semaphore(k)) for k in keys}
        block = es.enter_context(nc.Block())
        out_chans = [k for k in keys if k.startswith("dma_")]

        def emit_eng(ename):
            def body(eng):
                for o in P.ops[ename]:
                    for k, v in o.waits:
                        eng.wait_ge(sems[k], v)
                    if o.dma:
                        insts = o.fn(eng)
                        if not isinstance(insts, (list, tuple)):
                            insts = [insts]
                        assert len(insts) == o.ndma
                        for i_ in insts:
                            i_.then_inc(sems[o.sig[0]], 16)
                    else:
                        inst = o.fn(eng)
                        if o.sig is not None:
                            inst.then_inc(sems[o.sig[0]], 1)
                if ename == "sp":
                    for k in out_chans:
                        eng.wait_ge(sems[k], P.final_counts[k])
            return body
        block.sync(emit_eng("sp"))
        block.scalar(emit_eng("act"))
        block.vector(emit_eng("dve"))
        block.gpsimd(emit_eng("pool"))
        block.tensor(emit_eng("pe"))
    nc._prog_stats = {e: len(P.ops[e]) for e in P.ENGS}
    return nc


def rope_tables(positions):
    half = HD // 2
    inv = (np.float32(THETA) ** (-(np.arange(half, dtype=np.float32) / np.float32(half)))).astype(np.float32)
    ang = positions.astype(np.float32)[None, :] * inv[:, None]
    cos = np.cos(ang).astype(np.float32)
    sin = np.sin(ang).astype(np.float32)
    p = np.arange(128)
    ct = cos[p % 32]
    sgn = np.where((p % 64) < 32, -1.0, 1.0).astype(np.float32)
    st = sin[p % 32] * sgn[:, None]
    return np.stack([ct, st], 0)


def band_mat(g, rel, mode):
    w = POOL_WINDOWS[g]
    B = 1024
    S = 1 << 30
    if mode == "first":
        B = 0
    if mode == "last":
        S = B + 128
    s = B + rel * 128 + np.arange(128)[:, None]
    t = B + np.arange(128)[None, :]
    lo = np.maximum(t - w // 2, 0)
    hi = np.minimum(t - w // 2 + w, S)
    inr = (s >= lo) & (s < hi)
    val = inr / (hi - lo).astype(np.float64) - (s == t)
    return val.astype(np.float32)


def make_consts(valid_left, valid_right):
    bands = np.zeros((128, 28, 128), np.float32)
    for g in range(4):
        bands[:, 3 * g + 0] = band_mat(g, -1, "int")
        bands[:, 3 * g + 1] = band_mat(g, 0, "int")
        bands[:, 3 * g + 2] = band_mat(g, 1, "int")
        bands[:, 12 + g] = band_mat(g, 0, "first")
        bands[:, 16 + g] = band_mat(g, 0, "last")
        bands[:, 20 + g] = band_mat(g, 0, "int" if valid_left else "first")
        bands[:, 24 + g] = band_mat(g, 0, "int" if valid_right else "last")
    j = np.arange(128)[:, None]
    i = np.arange(128)[None, :]
    mp = (j >= i).astype(np.float32)
    mn = (j <= i).astype(np.float32)
    masks = np.zeros((128, 4, 512), np.float32)
    masks[:, 0] = np.tile(mp, (1, 4))
    masks[:, 1] = np.tile(mn, (1, 4))
    masks[:, 2] = np.tile(mp, (1, 4)) * (1.0 if valid_left else 0.0)
    masks[:, 3] = np.tile(mn, (1, 4)) * (1.0 if valid_right else 0.0)
    cst = np.zeros((128, 2, 128), np.float32)
    cst[:, 0] = np.eye(128)
    k = np.arange(128)
    partner = (k // 64) * 64 + ((k % 64) + 32) % 64
    cst[partner, 1, k] = 1.0
    sel = np.zeros((1, 2, 128), np.float32)
    sel[0, 0, 64:] = 4.0
    sel[0, 1, :64] = 4.0
    bf = ml_dtypes.bfloat16
    return (bands.reshape(128, -1).astype(bf), masks.reshape(128, -1).astype(bf),
            cst.reshape(128, -1).astype(bf), sel.reshape(1, -1).astype(bf))


def sink_layout(attn_sink):
    return np.asarray(attn_sink, np.float32).reshape(1, NH)


def core_inputs(segs_x, halos, seg_kinds, pos0s, valid_left, valid_right, shared):
    xm = np.ascontiguousarray(np.concatenate(segs_x, 0))
    tabs = []
    for x, p0 in zip(segs_x, pos0s):
        n = x.shape[0]
        tabs.append(rope_tables(np.arange(p0 - 128, p0 + n + 128)))
    tabs = np.ascontiguousarray(np.concatenate(tabs, 2))
    bands, masks, cst, sel = make_consts(valid_left, valid_right)
    d = dict(shared)
    d.update(xm=xm, xh=np.ascontiguousarray(halos), tabs=tabs, bands=bands, masks=masks, cst=cst, sel=sel)
    return d


def shared_inputs(norm_pre, w_in, w_pool_group, pool_scale, w_pool_proj, attn_sink, w_attn_proj, w_out, norm_post):
    f = lambda a: np.ascontiguousarray(np.asarray(a, np.float32))
    return dict(
        w_in=f(w_in[0]), w_pg=f(w_pool_group[0]), w_pp=f(w_pool_proj[0]), w_ap=f(w_attn_proj[0]), w_out=f(w_out[0]),
        gpre_b=f(np.tile(np.asarray(norm_pre[0]).reshape(1, D), (128, 1))), pscale_c=f(np.asarray(pool_scale[0]).reshape(8, 128).T),
        gpost_b=f(np.tile(np.asarray(norm_post[0]).reshape(1, D), (128, 1))), sink_rows=f(sink_layout(attn_sink[0])),
    )


_NC_CACHE = {}


def kernel(x_prompt, x_sample, norm_pre, w_in, w_pool_group, pool_scale, w_pool_proj,
           attn_sink, w_attn_proj, w_out, norm_post):
    x_prompt = np.asarray(x_prompt, np.float32)
    x_sample = np.asarray(x_sample, np.float32)
    shared = shared_inputs(norm_pre, w_in, w_pool_group, pool_scale, w_pool_proj, attn_sink, w_attn_proj, w_out, norm_post)
    segs = [("prompt", 2048), ("prompt", 2048), ("sample", 4096)]
    if "nc" not in _NC_CACHE:
        _NC_CACHE["nc"] = build_program(segs)
    nc = _NC_CACHE["nc"]
    in_maps = []
    for c in range(8):
        sb_, hf = c // 2, c % 2
        sx = [x_prompt[2 * c], x_prompt[2 * c + 1], x_sample[sb_, hf * 4096:(hf + 1) * 4096]]
        halos = np.zeros((2, 128, D), np.float32)
        if hf == 1:
            halos[0] = x_sample[sb_, 4096 - 128:4096]
        else:
            halos[1] = x_sample[sb_, 4096:4096 + 128]
        in_maps.append(core_inputs(sx, halos, [k for k, _ in segs], [0, 0, hf * 4096], hf == 1, hf == 0, shared))
    res = run_bass_kernel_spmd(nc, in_maps, core_ids=list(range(8)))
    y_prompt = np.empty_like(x_prompt)
    y_sample = np.empty_like(x_sample)
    for c in range(8):
        ymc = res.results[c]["ym"]
        y_prompt[2 * c] = ymc[0:2048]
        y_prompt[2 * c + 1] = ymc[2048:4096]
        y_sample[c // 2, (c % 2) * 4096:(c % 2 + 1) * 4096] = ymc[4096:8192]
    return (y_prompt, y_sample)
```

```python
import numpy as np
import ml_dtypes
from contextlib import ExitStack
import concourse.bass as bass
import concourse.mybir as mybir
from concourse.bass_utils import run_bass_kernel_spmd

F32 = mybir.dt.float32
BF16 = mybir.dt.bfloat16
AF = mybir.ActivationFunctionType
ALU = mybir.AluOpType

D = 1024
NH, NKV, HD = 16, 4, 64
TILE = 512
BLK = 128
POOL_WINDOWS = (2, 4, 8, 16)
EPS = 1e-6
THETA = 10000.0
IN_W = 6656
C_U, C_PG, C_Q, C_K, C_V, C_AG, C_GP, C_GA = 0, 1024, 2048, 3072, 3328, 3584, 4608, 5632
NRING = 4


class Op:
    __slots__ = ("eng", "fn", "reads", "writes", "dma", "chan", "ndma", "idx", "deps", "sig", "waits", "name")


class Prog:
    ENGS = ("sp", "act", "dve", "pool", "pe")

    def __init__(self, same_eng_dist=3):
        self.ops = {e: [] for e in self.ENGS}
        self.last_w = {}
        self.readers = {}
        self.same_eng_dist = same_eng_dist
        self.all = []

    def add(self, eng, fn, reads=(), writes=(), dma=False, chan=None, ndma=1, name=""):
        o = Op()
        o.eng = eng; o.fn = fn; o.reads = tuple(reads); o.writes = tuple(writes)
        o.dma = dma; o.chan = chan; o.ndma = ndma; o.name = name
        o.idx = len(self.ops[eng]); o.deps = []; o.sig = None; o.waits = []
        deps = {}
        for r in o.reads:
            w = self.last_w.get(r)
            if w is not None:
                deps[id(w)] = (w, True)
            if r.startswith("bank") or r == "pTb":
                for rd in self.readers.get(r, ()):
                    if id(rd) not in deps and rd.eng != eng:
                        deps[id(rd)] = (rd, False)
        for r in o.writes:
            w = self.last_w.get(r)
            if w is not None and id(w) not in deps:
                deps[id(w)] = (w, False)
            for rd in self.readers.get(r, ()):
                if id(rd) not in deps:
                    deps[id(rd)] = (rd, False)
        for d, israw in deps.values():
            if d is o:
                continue
            if (not d.dma) and (not o.dma) and d.eng == o.eng:
                if o.eng == "pe":
                    continue
                if not israw:
                    continue
                if o.idx - d.idx >= self.same_eng_dist:
                    continue
            o.deps.append(d)
        for r in o.reads:
            self.readers.setdefault(r, []).append(o)
        for r in o.writes:
            self.last_w[r] = o
            self.readers[r] = []
        self.ops[eng].append(o)
        self.all.append(o)
        return o

    def finalize(self):
        need = set()
        for o in self.all:
            for d in o.deps:
                need.add(id(d))
        cnt = {}
        for e in self.ENGS:
            for o in self.ops[e]:
                if o.dma:
                    key = "dma_" + o.chan
                    cnt[key] = cnt.get(key, 0) + 16 * o.ndma
                    o.sig = (key, cnt[key])
                elif id(o) in need:
                    key = "eng_" + e
                    cnt[key] = cnt.get(key, 0) + 1
                    o.sig = (key, cnt[key])
        self.final_counts = cnt
        for e in self.ENGS:
            seen = {}
            for o in self.ops[e]:
                w = {}
                for d in o.deps:
                    k, v = d.sig
                    if seen.get(k, 0) >= v:
                        continue
                    w[k] = max(w.get(k, 0), v)
                for k, v in w.items():
                    seen[k] = v
                o.waits = sorted(w.items())
        return sorted(cnt.keys())


def stream_elements():
    el = []
    el.append(("u0", [("win", C_U, 512)]))
    el.append(("u1", [("win", C_U + 512, 512)]))
    el.append(("vv", [("win", C_V, 256), ("win", C_V, 256)]))
    el.append(("kk", [("win", C_K + 64 * (i // 2), 64) for i in range(8)]))
    el.append(("pg0", [("win", C_PG, 512)]))
    el.append(("pg1", [("win", C_PG + 512, 512)]))
    for g in range(4):
        el.append((f"qa{g}", [("win", C_Q + 256 * g, 256), ("win", C_AG + 256 * g, 256)]))
    for op in range(4):
        el.append((f"mg{op}", [("win", C_GP + 256 * op, 256), ("win", C_GA + 256 * op, 256)]))
        el.append((f"pr{op}", [("wpp", 256 * op, 256), ("wap", 256 * op, 256)]))
    el.append(("wo0", [("wout", 0, 512)]))
    el.append(("wo1", [("wout", 512, 512)]))
    return el


ELEMS = stream_elements()
ELEM_IDX = {n: i for i, (n, _) in enumerate(ELEMS)}
NEL = len(ELEMS)


def build_program(segs, dbg=False, limit=99):
    nc = bass.Bass("TRN2", target_bir_lowering=False)
    ntok = sum(n for _, n in segs)
    nsamp = sum(1 for k, _ in segs if k == "sample")
    ntab = sum(n + 256 for _, n in segs)

    def din(name, shape, dt=F32):
        return nc.dram_tensor(name, list(shape), dt, kind="ExternalInput").ap()

    xm = din("xm", [ntok, D])
    xh = din("xh", [max(nsamp, 1) * 2, BLK, D])
    w_in = din("w_in", [D, IN_W])
    w_pg = din("w_pg", [4, 256, 256])
    w_pp = din("w_pp", [D, D])
    w_ap = din("w_ap", [D, D])
    w_out = din("w_out", [D, D])
    gpre_b = din("gpre_b", [128, D])
    pscale_c = din("pscale_c", [128, 8])
    gpost_b = din("gpost_b", [128, D])
    sink_rows = din("sink_rows", [1, 16])
    tabs = din("tabs", [2, 128, ntab])
    bands_d = din("bands", [128, 28 * 128], BF16)
    masks_d = din("masks", [128, 4 * 512], BF16)
    cst_d = din("cst", [128, 2 * 128], BF16)
    sel_d = din("sel", [1, 2 * 128], BF16)
    ym = nc.dram_tensor("ym", [ntok, D], F32, kind="ExternalOutput").ap()
    wsc = nc.dram_tensor("wsc", [NEL, 128, 8 * 512], BF16, kind="Internal").ap()
    dbg_out = {}
    if dbg:
        for nm, shp, dt in (("d_hT", [128, 8 * 1024], BF16), ("d_kT", [128, 4 * 768], BF16),
                            ("d_ub", [128, 6 * 1024], BF16), ("d_va", [128, 6 * 4 * 192], BF16),
                            ("d_pbT", [128, 8 * 512], BF16), ("d_abT", [128, 8 * 512], BF16),
                            ("d_mT", [128, 8 * 512], BF16)):
            dbg_out[nm] = nc.dram_tensor(nm, shp, dt, kind="ExternalOutput").ap()

    wsrc = {"win": w_in, "wpp": w_pp, "wap": w_ap, "wout": w_out}
    P = Prog()
    es = ExitStack()
    with es:
        def sb(name, shape, dt):
            return es.enter_context(nc.sbuf_tensor("s_" + name, list(shape), dt))

        def ps(name, shape, dt):
            return es.enter_context(nc.psum_tensor("p_" + name, list(shape), dt))

        ring = [sb(f"ring{i}", [128, 8, 512], BF16) for i in range(NRING)]
        wpg = sb("wpg", [128, 4, 2, 256], BF16)
        xbuf = [sb(f"xbuf{i}", [128, D], F32) for i in range(2)]
        ybuf = [sb(f"ybuf{i}", [128, D], F32) for i in range(2)]
        xs = [sb(f"xs{i}", [128, D], BF16) for i in range(2)]
        hT = sb("hT", [128, 8, 8 * 128], BF16)
        kpre = [sb(f"kpre{i}", [128, 768], BF16) for i in range(2)]
        kT = sb("kT", [128, 4, 768], BF16)
        vaug = sb("vaug", [128, 6, 4, 192], BF16)
        ub = sb("ub", [128, 6, D], BF16)
        poolT = [sb(f"poolT{i}", [128, 2, 512], BF16) for i in range(2)]
        sg = [sb("sg0", [128, 512], BF16)] * 2
        pbT = sb("pbT", [128, 8, 512], BF16)
        abT = sb("abT", [128, 8, 512], BF16)
        mT = sb("mT", [128, 8, 512], BF16)
        qpre = [sb(f"qpre{i}", [128, 512], BF16) for i in range(2)]
        qg = [sb(f"qg{i}", [128, 2, 512], BF16) for i in range(2)]
        sag = [sb(f"sag{i}", [128, 2, 512], BF16) for i in range(2)]
        rt1 = [sb("rt1_0", [128, 768], F32)] * 2
        rt2 = [sb("rt2_0", [128, 768], F32)] * 2
        tab = sb("tab", [128, 2, 768], F32)
        PT = [sb(f"PT{i}", [128, 3, 512], BF16) for i in range(3)]
        rden = [sb(f"rden{i}", [128, 256], F32) for i in range(2)]
        otmp = [sb(f"otmp{i}", [128, 256], F32) for i in range(2)]
        gsig = [sb(f"gsig{i}", [128, 512], F32) for i in range(2)] * 2
        mt1 = [sb("mt1_0", [128, 512], F32)] * 2
        mt2 = [sb("mt2_0", [128, 512], F32)] * 2
        ytmp = [sb(f"ytmp{i}", [128, 512], F32) for i in range(2)]
        stat = sb("stat", [128, 64], F32)
        gpre = sb("gpre", [128, D], F32)
        pscale = sb("pscale", [128, 8], F32)
        psh = sb("psh", [128, 8], F32)
        gpost = sb("gpost", [128, D], F32)
        sinkf = sb("sinkf", [1, 16], F32)
        sinkb = sb("sinkb", [1, 16], BF16)
        sinkT = sb("sinkT", [128, 4, 256], F32)
        pvs = [sb(f"pvs{i}", [128, 512], F32) for i in range(2)]
        bands = sb("bands", [128, 28, 128], BF16)
        masks = sb("masks", [128, 4, 512], BF16)
        cst = sb("cst", [128, 2, 128], BF16)
        sel = sb("sel", [1, 2, 128], BF16)
        epsT = sb("epsT", [128, 1], F32)
        bank = [ps(f"bank{i}", [128, 512], F32) for i in range(8)]
        pTb = bank[7][:].bitcast(BF16).rearrange("p (c t) -> p c t", c=8)

        ident = cst[:, 0, :]
        perm = cst[:, 1, :]

        state = {"gen": 0, "genlist": list(range(7)), "rr": 0, "stream": 0, "n": 0}

        def gbank():
            lst = state["genlist"]
            b = lst[state["gen"] % len(lst)]
            state["gen"] += 1
            return b

        def ew_eng():
            e = ("dve", "pool", "act")[state["rr"] % 3]
            state["rr"] += 1
            return e

        def uniq(p):
            state["n"] += 1
            return f"{p}{state['n']}"

        def ld(dst_ap, src_ap, res, chan):
            P.add("sp", lambda e: e.dma_start(out=dst_ap, in_=src_ap), writes=[res], dma=True, chan=chan)

        ld(gpre[:], gpre_b, "gpre", "c0")
        ld(pscale[:], pscale_c, "pscale", "c1")
        ld(gpost[:], gpost_b, "gpost", "c2")
        ld(sinkf[:], sink_rows, "sinkf", "c3")
        ld(bands[:].rearrange("p a b -> p (a b)"), bands_d, "bands", "c4")
        ld(masks[:].rearrange("p a b -> p (a b)"), masks_d, "masks", "c5")
        ld(cst[:].rearrange("p a b -> p (a b)"), cst_d, "cst", "c6")
        ld(sel[:].rearrange("p a b -> p (a b)"), sel_d, "sel", "c7")
        P.add("act", lambda e: e.activation(out=sinkb[:], in_=sinkf[:], func=AF.Exp),
              reads=["sinkf"], writes=["sinkb"])
        for g in range(4):
            for par in range(2):
                sb0 = sinkb[0:1, 4 * g + par:4 * g + par + 1]
                srhs = bass.AP(sinkb, sb0.offset, [[sb0.ap[0][0], 1], [2, 2], [0, 128]])
                P.add("pe", lambda e, g=g, par=par, srhs=srhs: e.matmul(bank[g // 2][:, (g % 2) * 256:(g % 2 + 1) * 256].rearrange("p (j q) -> p j q", j=2),
                                                                     lhsT=sel[0:1, par, :], rhs=srhs, start=(par == 0), stop=(par == 1)),
                      reads=["sel", "sinkb"], writes=[f"bank{g // 2}"])
        for h2 in range(2):
            P.add("dve", lambda e, h2=h2: e.tensor_copy(out=sinkT[:, 2 * h2:2 * h2 + 2, :].rearrange("p a b -> p (a b)"), in_=bank[h2][:, :]),
                  reads=[f"bank{h2}"], writes=["sinkT"])
        P.add("dve", lambda e: e.memset(epsT[:], EPS), writes=["epsT"])
        P.add("dve", lambda e: e.tensor_scalar(out=psh[:], in0=pscale[:], scalar1=0.25, scalar2=None, op0=ALU.mult), reads=["pscale"], writes=["psh"])
        P.add("pool", lambda e: e.memset(vaug[:, :, :, 64:128], 4.0), writes=["vaug_ones"])

        stg32 = [(xbuf[0], "xbuf0"), (xbuf[1], "xbuf1"), (ybuf[0], "ybuf0"), (ybuf[1], "ybuf1")]
        for half in range(2):
            buf, res = stg32[half]
            src = w_pg[2 * half:2 * half + 2].rearrange("g (kc p) d -> p g kc d", p=128)
            dst = buf[:].rearrange("p (g kc d) -> p g kc d", g=2, kc=2)
            P.add("sp", lambda e, dst=dst, src=src: e.dma_start(out=dst, in_=src), writes=[res], dma=True, chan="pl" + res)
            P.add("dve", lambda e, buf=buf, half=half: e.tensor_copy(
                out=wpg[:, 2 * half:2 * half + 2, :, :].rearrange("p g kc d -> p (g kc d)"), in_=buf[:]),
                reads=[res], writes=["wpg"])
        for ei, (ename, pieces) in enumerate(ELEMS):
            dview = wsc[ei].rearrange("p (dc c) -> p dc c", dc=8)
            col = 0
            dmas = []
            for (src, c0, ncol) in pieces:
                sap = wsrc[src].rearrange("(dc p) c -> p dc c", p=128)[:, :, c0:c0 + ncol]
                dmas.append((dview[:, :, col:col + ncol], sap))
                col += ncol

            def fn(e, dmas=dmas):
                return [e.dma_start(out=d, in_=s_) for d, s_ in dmas]
            P.add("pool", fn, writes=[f"wsc{ei}"], dma=True, chan=f"pw{ei}", ndma=len(dmas))

        stream_order = []

        def issue_stream(k):
            if k >= len(stream_order):
                return
            ei = ELEM_IDX[stream_order[k]]
            slot = k % NRING
            P.add("sp", lambda e, ei=ei, slot=slot: e.dma_start(out=ring[slot][:].rearrange("p a b -> p (a b)"), in_=wsc[ei]),
                  reads=[f"wsc{ei}"], writes=[f"ring{slot}"], dma=True, chan=f"rg{slot}")

        def take(name):
            k = state["stream"]
            assert stream_order[k] == name, (k, stream_order[k], name)
            state["stream"] += 1
            issue_stream(k + NRING - 2)
            return ring[k % NRING], f"ring{k % NRING}"

        tiles = []
        tok0 = 0
        tabo = 0
        si = 0
        for kind, n in segs:
            nblk = n // BLK
            for t in range(n // TILE):
                tiles.append(dict(kind=kind, nblk=nblk, b0=4 * t, tok0=tok0 + t * TILE, tabo=tabo, si=si, first=(t == 0), last=(t == n // TILE - 1)))
            tok0 += n
            tabo += n + 256
            if kind == "sample":
                si += 1
        for _ in tiles:
            stream_order.extend(n for n, _ in ELEMS)
        for k in range(NRING - 2):
            issue_stream(k)

        def front_a(tl, gb):
            kind, nblk = tl["kind"], tl["nblk"]
            if gb < 0 or gb >= nblk:
                if kind == "prompt":
                    return None
                src = xh[2 * tl["si"] + (0 if gb < 0 else 1)]
            else:
                t0 = tl["tok0"] - tl["b0"] * BLK + gb * BLK
                src = xm[t0:t0 + BLK, :]
            i = state.setdefault("xb", 0)
            state["xb"] += 1
            xb, xres = xbuf[i % 2], f"xbuf{i % 2}"
            xs_, xsres = xs[i % 2], f"xs{i % 2}"
            c = (i % 8) * 4
            P.add("sp", lambda e: e.dma_start(out=xb[:], in_=src), writes=[xres], dma=True, chan="ld" + xres)
            P.add("dve", lambda e: e.scalar_tensor_tensor(out=xs_[:], in0=xb[:], scalar=1.0, in1=xb[:], op0=ALU.mult, op1=ALU.mult, accum_out=stat[:, c:c + 1]),
                  reads=[xres], writes=[xsres, f"st{c}"])
            P.add("act", lambda e: e.activation(out=stat[:, c + 1:c + 2], in_=stat[:, c:c + 1], func=AF.Sqrt, bias=epsT[:], scale=1.0 / D),
                  reads=[f"st{c}", "epsT"], writes=[f"st{c + 1}"])
            P.add("dve", lambda e: e.reciprocal(out=stat[:, c + 2:c + 3], in_=stat[:, c + 1:c + 2]), reads=[f"st{c + 1}"], writes=[f"st{c + 2}"])
            P.add("dve", lambda e: e.scalar_tensor_tensor(out=xs_[:], in0=xb[:], scalar=stat[:, c + 2:c + 3], in1=gpre[:], op0=ALU.mult, op1=ALU.mult),
                  reads=[xres, f"st{c + 2}", "gpre"], writes=[xsres])
            return (gb % 8, xs_, xsres)

        def front_b(tok):
            if tok is None:
                return
            slot, xs_, xsres = tok
            for ch in range(8):
                P.add("pe", lambda e, ch=ch: e.transpose(pTb[:, ch, :], xs_[:, ch * 128:(ch + 1) * 128], ident),
                      reads=[xsres, "cst"], writes=["bank7"])
            P.add("act", lambda e: e.activation(out=hT[:, :, slot * 128:(slot + 1) * 128], in_=pTb, func=AF.Copy),
                  reads=["bank7"], writes=[f"hT{slot}"])

        def front(tl, gb):
            front_b(front_a(tl, gb))

        def present(tl, gb):
            return not (tl["kind"] == "prompt" and (gb < 0 or gb >= tl["nblk"]))

        def evac_copy(i, out_ap, in_ap, reads, writes):
            if i % 2 == 0:
                P.add("act", lambda e: e.activation(out=out_ap, in_=in_ap, func=AF.Copy), reads=reads, writes=writes)
            else:
                P.add("dve", lambda e: e.tensor_copy(out=out_ap, in_=in_ap), reads=reads, writes=writes)

        def inproj_fm(el, elres, q, rhs_ap, rhs_res, n, bk, col0=0):
            for dc in range(8):
                P.add("pe", lambda e, dc=dc: e.matmul(bank[bk][:, col0:col0 + n], lhsT=el[:, dc, q * 128:(q + 1) * 128],
                                                      rhs=rhs_ap(dc), start=(dc == 0), stop=(dc == 7)),
                      reads=[elres] + rhs_res, writes=[f"bank{bk}"])

        def tile_body(ti, tl):
            b0, nblk, kind = tl["b0"], tl["nblk"], tl["kind"]
            state["genlist"] = list(range(7))
            pend_ = None
            for gb in range(b0 - 1 if tl["first"] else b0 + 1, b0 + 5):
                if (ti, gb) not in state.setdefault("fronted", set()):
                    tok_ = front_a(tl, gb)
                    front_b(pend_)
                    pend_ = tok_
            front_b(pend_)
            nxt = tiles[ti + 1] if ti + 1 < len(tiles) else None
            hq = {"L": [], "t5only": set(), "pend": None, "pend_gb": None}
            if nxt is not None:
                nb0 = nxt["b0"]
                live = {(b0 + j) % 8 for j in range(4)}
                cand = [gb for gb in range(nb0 - 1 if nxt["first"] else nb0 + 1, nb0 + 5) if present(nxt, gb)]
                t4b = [gb for gb in cand if (gb % 8) not in live]
                t5b = [gb for gb in cand if (gb % 8) in live]
                hq["L"] = t4b + t5b
                hq["t5only"] = set(t5b)
                for gb in cand:
                    state.setdefault("fronted", set()).add((ti + 1, gb))

            def hoist_event(in_t5):
                if hq["pend"] is not None and (in_t5 or hq["pend_gb"] not in hq["t5only"]):
                    front_b(hq["pend"])
                    hq["pend"] = None
                if hq["pend"] is None and hq["L"]:
                    gb_ = hq["L"].pop(0)
                    hq["pend"] = front_a(nxt, gb_)
                    hq["pend_gb"] = gb_
            mslot = (b0 % 8)
            main_res = [f"hT{mslot + j}" for j in range(4)]
            hmain = lambda dc: hT[:, dc, mslot * 128:(mslot + 4) * 128]
            kbs = [kb for kb in range(6) if present(tl, b0 - 1 + kb)]
            usl = lambda kb: (b0 + kb) % 6
            newkb = [kb for kb in kbs if tl["first"] or kb >= 2]
            hblk = lambda kb, dc: hT[:, dc, ((b0 - 1 + kb) % 8) * 128:((b0 - 1 + kb) % 8 + 1) * 128]
            hres = lambda kb: [f"hT{(b0 - 1 + kb) % 8}"]
            if limit < 2:
                return
            to = tl["tabo"] + b0 * BLK
            P.add("sp", lambda e, to=to: e.dma_start(out=tab[:], in_=tabs[:, :, to:to + 768].rearrange("a p t -> p a t")),
                  writes=["tab"], dma=True, chan="tab")
            import os as _os
            T1N = int(_os.environ.get("T1_N", "99"))
            if T1N < 3:
                return
            if T1N < 4:
                return
            elu = [take("u0"), take("u1")]
            n_e = 0
            for kb in [k_ for k_ in (1, 2, 3, 4, 0, 5) if k_ in newkb]:
                for half in range(2):
                    bk = gbank()
                    el_, r_ = elu[half]
                    for dc in range(8):
                        P.add("pe", lambda e, dc=dc, kb=kb, el_=el_, bk=bk: e.matmul(bank[bk][:, :], lhsT=hblk(kb, dc), rhs=el_[:, dc, :], start=(dc == 0), stop=(dc == 7)),
                              reads=[r_] + hres(kb), writes=[f"bank{bk}"])
                    evac_copy(n_e, ub[:, usl(kb), half * 512:(half + 1) * 512], bank[bk][:, :], [f"bank{bk}"], [f"ub{usl(kb)}_{half}"])
                    n_e += 1
            el, elres = take("vv")
            for pr in range(3):
                bk = gbank()
                any_ = False
                for j in range(2):
                    kb = 2 * pr + j
                    if kb not in newkb:
                        continue
                    any_ = True
                    for dc in range(8):
                        P.add("pe", lambda e, dc=dc, kb=kb, j=j, bk=bk: e.matmul(bank[bk][:, j * 256:(j + 1) * 256], lhsT=hblk(kb, dc), rhs=el[:, dc, 0:256],
                                                                         start=(dc == 0), stop=(dc == 7)),
                              reads=[elres] + hres(kb), writes=[f"bank{bk}"])
                    for cp, off in enumerate((0, 128)):
                        evac_copy(kb + cp, vaug[:, usl(kb), :, off:off + 64], bank[bk][:, j * 256:(j + 1) * 256].rearrange("p (g d) -> p g d", g=4),
                                  [f"bank{bk}"], [f"vaug{usl(kb)}_{cp}"])
            elk, elkres = take("kk")

            kbanks = {}

            def k_main(g):
                bkm = gbank()
                kbanks[g] = bkm
                for dc in range(8):
                    P.add("pe", lambda e, dc=dc: e.matmul(bank[bkm][:, :], lhsT=elk[:, dc, g * 128:(g + 1) * 128], rhs=hmain(dc), start=(dc == 0), stop=(dc == 7)),
                          reads=[elkres] + main_res, writes=[f"bank{bkm}"])

            def k_halo(g):
                kp = kpre[g % 2]
                kr = f"kpre{g % 2}"
                bkm = kbanks[g]
                bkh = gbank()
                for hi, kb in enumerate((0, 5)):
                    if kb not in kbs:
                        continue
                    for dc in range(8):
                        P.add("pe", lambda e, dc=dc, hi=hi, kb=kb: e.matmul(bank[bkh][:, hi * 128:(hi + 1) * 128], lhsT=elk[:, dc, g * 128:(g + 1) * 128], rhs=hblk(kb, dc),
                                                                         start=(dc == 0), stop=(dc == 7)),
                              reads=[elkres] + hres(kb), writes=[f"bank{bkh}"])
                P.add("act", lambda e: e.activation(out=kp[:, 128:640], in_=bank[bkm][:, :], func=AF.Copy), reads=[f"bank{bkm}"], writes=[kr + "_m"])
                for hi, kb in enumerate((0, 5)):
                    if kb in kbs:
                        P.add("dve", lambda e, hi=hi, kb=kb: e.tensor_copy(out=kp[:, kb * 128:(kb + 1) * 128], in_=bank[bkh][:, hi * 128:(hi + 1) * 128]),
                              reads=[f"bank{bkh}"], writes=[kr + f"_h{hi}"])
                    else:
                        P.add("dve", lambda e, kb=kb: e.memset(kp[:, kb * 128:(kb + 1) * 128], 0.0), writes=[kr + f"_h{hi}"])

            def k_rope(g):
                kp = kpre[g % 2]
                kr = f"kpre{g % 2}"
                r = g % 2
                bp0 = gbank()
                bp1 = gbank()
                P.add("pe", lambda e: e.matmul(bank[bp0][:, :], lhsT=perm, rhs=kp[:, 0:512], start=True, stop=True),
                      reads=["cst", kr + "_m", kr + "_h0"], writes=[f"bank{bp0}"])
                P.add("pe", lambda e: e.matmul(bank[bp1][:, 0:256], lhsT=perm, rhs=kp[:, 512:768], start=True, stop=True),
                      reads=["cst", kr + "_m", kr + "_h1"], writes=[f"bank{bp1}"])
                P.add("pool", lambda e: e.tensor_tensor(out=rt1[r][:, :], in0=kp[:, :], in1=tab[:, 0, :], op=ALU.mult),
                      reads=[kr + "_m", kr + "_h0", kr + "_h1", "tab"], writes=["rt1_0"])
                P.add("dve", lambda e: e.tensor_tensor(out=rt2[r][:, 0:512], in0=bank[bp0][:, :], in1=tab[:, 1, 0:512], op=ALU.mult),
                      reads=[f"bank{bp0}", "tab"], writes=["rt2_0a"])
                P.add("dve", lambda e: e.tensor_tensor(out=rt2[r][:, 512:768], in0=bank[bp1][:, 0:256], in1=tab[:, 1, 512:768], op=ALU.mult),
                      reads=[f"bank{bp1}", "tab"], writes=["rt2_0b"])
                P.add("pool", lambda e: e.tensor_tensor(out=kT[:, g, :], in0=rt1[r][:, :], in1=rt2[r][:, :], op=ALU.add),
                      reads=["rt1_0", "rt2_0a", "rt2_0b"], writes=[f"kT{g}"])

            for g in range(4):
                k_main(g)
            state["genlist"] = [b for b in range(7) if b not in kbanks.values()]
            k_halo(0)
            for g in range(4):
                if g + 1 < 4:
                    k_halo(g + 1)
                k_rope(g)
            state["genlist"] = list(range(7))
            if dbg and ti == 0:
                P.add("sp", lambda e: e.dma_start(out=dbg_out["d_hT"], in_=hT[:].rearrange("p a b -> p (a b)")), reads=[f"hT{s}" for s in range(8)], dma=True, chan="dbg")
                P.add("sp", lambda e: e.dma_start(out=dbg_out["d_kT"], in_=kT[:].rearrange("p a b -> p (a b)")), reads=[f"kT{g}" for g in range(4)], dma=True, chan="dbg")
                P.add("sp", lambda e: e.dma_start(out=dbg_out["d_ub"], in_=ub[:].rearrange("p a b -> p (a b)")), reads=[f"ub{usl(kb)}_{h}" for kb in kbs for h in range(2)], dma=True, chan="dbg")
                P.add("sp", lambda e: e.dma_start(out=dbg_out["d_va"], in_=vaug[:].rearrange("p a b c -> p (a b c)")),
                      reads=[f"vaug{usl(kb)}_{cp}" for kb in kbs for cp in range(2)] + ["vaug_ones"], dma=True, chan="dbg")

            if limit < 3:
                return
            elpg = [take("pg0"), take("pg1")]
            for g in range(4):
                pt = poolT[g % 2]
                ptres = f"poolT{g % 2}"
                for kc in range(2):
                    c = 2 * g + kc
                    bk = gbank()
                    for ob in range(4):
                        gb = b0 + ob
                        terms = []
                        for rel in (-1, 0, 1):
                            sgb = gb + rel
                            if kind == "prompt" and (sgb < 0 or sgb >= nblk):
                                continue
                            if rel == 0:
                                if gb == 0:
                                    bi = (12 + g) if kind == "prompt" else (20 + g)
                                elif gb == nblk - 1:
                                    bi = (16 + g) if kind == "prompt" else (24 + g)
                                else:
                                    bi = 3 * g + 1
                            else:
                                bi = 3 * g + (0 if rel < 0 else 2)
                            terms.append((ob + 1 + rel, bi))
                        for i, (kb, bi) in enumerate(terms):
                            P.add("pe", lambda e, kb=kb, bi=bi, c=c, ob=ob, i=i, nt=len(terms), bk=bk: e.matmul(
                                bank[bk][:, ob * 128:(ob + 1) * 128], lhsT=ub[:, (b0 + kb) % 6, c * 128:(c + 1) * 128], rhs=bands[:, bi, :],
                                start=(i == 0), stop=(i == nt - 1)),
                                reads=[f"ub{usl(kb)}_{c // 4}", "bands"], writes=[f"bank{bk}"])
                    evac_copy(c, pt[:, kc, :], bank[bk][:, :], [f"bank{bk}"], [ptres + f"_{kc}"])
                gtmp = []
                for oh in range(2):
                    oc = 2 * g + oh
                    bkg = gbank()
                    el_, r_ = elpg[oc // 4]
                    inproj_fm(el_, r_, oc % 4, hmain, main_res, 512, bkg)
                    s_ = sg[0]
                    yt = ytmp[oh]
                    P.add("act", lambda e, s_=s_, bkg=bkg: e.activation(out=s_[:], in_=bank[bkg][:, :], func=AF.Tanh, scale=0.5), reads=[f"bank{bkg}"], writes=["sg0"])
                    P.add("dve", lambda e, s_=s_, bkg=bkg, yt=yt: e.scalar_tensor_tensor(out=yt[:], in0=s_[:], scalar=1.0, in1=bank[bkg][:, :], op0=ALU.add, op1=ALU.mult),
                          reads=[f"bank{bkg}", "sg0"], writes=[f"ytmp{oh}"])
                for oh in range(2):
                    oc = 2 * g + oh
                    bkw = gbank()
                    yt = ytmp[oh]
                    for kc in range(2):
                        P.add("pe", lambda e, kc=kc, oh=oh, g=g, pt=pt, bkw=bkw: e.matmul(bank[bkw][:, :], lhsT=wpg[:, g, kc, oh * 128:(oh + 1) * 128], rhs=pt[:, kc, :],
                                                                                 start=(kc == 0), stop=(kc == 1)),
                              reads=["wpg", ptres + "_0", ptres + "_1"], writes=[f"bank{bkw}"])
                    P.add("dve", lambda e, bkw=bkw, oc=oc, yt=yt: e.scalar_tensor_tensor(out=pbT[:, oc, :], in0=bank[bkw][:, :], scalar=psh[:, oc:oc + 1], in1=yt[:],
                                                                                  op0=ALU.mult, op1=ALU.mult),
                          reads=[f"bank{bkw}", f"ytmp{oh}", "psh"], writes=["pbT", f"pbT{oc}"])
            if dbg and ti == 0:
                P.add("sp", lambda e: e.dma_start(out=dbg_out["d_pbT"], in_=pbT[:].rearrange("p a b -> p (a b)")), reads=[f"pbT{o}" for o in range(8)], dma=True, chan="dbg")

            if limit < 4:
                return

            def qa_items(g):
                box = {}
                qq = qg[g % 2]
                qres = f"qg{g % 2}"
                sa = sag[g % 2]
                sres = f"sag{g % 2}"

                def get_el():
                    if "el" not in box:
                        box["el"] = take(f"qa{g}")
                    return box["el"]

                def q_item(j):
                    el, elres = get_el()
                    bk = gbank()
                    inproj_fm(el, elres, j, hmain, main_res, 512, bk)
                    qp = qpre[j]
                    P.add("act", lambda e: e.activation(out=qp[:], in_=bank[bk][:, :], func=AF.Copy), reads=[f"bank{bk}"], writes=[f"qpre{j}"])
                    bp = 6

                    def tail():
                        P.add("pe", lambda e: e.matmul(bank[bp][:, :], lhsT=perm, rhs=qp[:], start=True, stop=True), reads=["cst", f"qpre{j}"], writes=[f"bank{bp}"])
                        P.add("pool", lambda e: e.tensor_tensor(out=rt1[j][:, 0:512], in0=qp[:], in1=tab[:, 0, 128:640], op=ALU.mult),
                              reads=[f"qpre{j}", "tab"], writes=["rt1_0"])
                        P.add("dve", lambda e: e.tensor_tensor(out=rt2[j][:, 0:512], in0=bank[bp][:, :], in1=tab[:, 1, 128:640], op=ALU.mult),
                              reads=[f"bank{bp}", "tab"], writes=["rt2_0a"])
                        P.add("pool", lambda e: e.tensor_tensor(out=qq[:, j, :], in0=rt1[j][:, 0:512], in1=rt2[j][:, 0:512], op=ALU.add),
                              reads=["rt1_0", "rt2_0a"], writes=[qres + f"_{j}"])
                    return tail

                def ag_item(j):
                    el, elres = get_el()
                    bk = gbank()
                    inproj_fm(el, elres, 2 + j, hmain, main_res, 512, bk)
                    P.add("act", lambda e: e.activation(out=sa[:, j, :], in_=bank[bk][:, :], func=AF.Tanh, scale=0.5), reads=[f"bank{bk}"], writes=[sres + f"_{j}"])
                    P.add("dve", lambda e: e.scalar_tensor_tensor(out=sa[:, j, :], in0=sa[:, j, :], scalar=1.0, in1=bank[bk][:, :], op0=ALU.add, op1=ALU.mult),
                          reads=[f"bank{bk}", sres + f"_{j}"], writes=[sres + f"_{j}"])

                return [lambda: q_item(0), lambda: q_item(1), lambda: ag_item(0), lambda: ag_item(1)]

            def barrier_mm():
                bk = gbank()
                P.add("pe", lambda e: e.matmul(bank[bk][:, 0:2], lhsT=ident, rhs=cst[:, 0, 0:2], start=True, stop=True), reads=["cst"], writes=[f"bank{bk}"])

            def chunks_of(qb):
                gb = b0 + qb
                return [c for c in range(3) if not (kind == "prompt" and (gb - 1 + c < 0 or gb - 1 + c >= nblk))]

            def s_part(idx, g, qb, par):
                qq = qg[g % 2]
                qres = f"qg{g % 2}"
                sset = (0, 1, 2) if idx % 2 == 0 else (3, 4, 5)
                rows = slice(0, 64) if par == 0 else slice(64, 128)
                cols = slice(par * 256, (par + 1) * 256)
                for c in chunks_of(qb):
                    kcols = slice((qb + c) * 128, (qb + c + 1) * 128)
                    bk = sset[c]
                    P.add("pe", lambda e, bk=bk, kcols=kcols: e.matmul(bank[bk][:, cols].rearrange("p (j q) -> p j q", j=2), lhsT=kT[rows, g, kcols],
                                                                      rhs=qq[rows, :, qb * 128:(qb + 1) * 128], start=True, stop=True),
                          reads=[f"kT{g}", qres + "_0", qres + "_1"], writes=[f"bank{bk}"])

            def softmax_part(idx, g, qb):
                gb = b0 + qb
                sset = (0, 1, 2) if idx % 2 == 0 else (3, 4, 5)
                pt_ = PT[idx % 3]
                ptres = f"PT{idx % 3}"
                for c in chunks_of(qb):
                    bk = sset[c]
                    P.add("act", lambda e, c=c, bk=bk: e.activation(out=pt_[:, c, :], in_=bank[bk][:, :], func=AF.Exp, scale=HD ** -0.5),
                          reads=[f"bank{bk}"], writes=[ptres + f"_{c}"])
                    mi = None
                    if c == 0:
                        mi = 2 if (kind == "sample" and gb == 0) else 0
                    elif c == 2:
                        mi = 3 if (kind == "sample" and gb == nblk - 1) else 1
                    if mi is not None:
                        P.add("dve", lambda e, c=c, mi=mi: e.tensor_tensor(out=pt_[:, c, :], in0=pt_[:, c, :], in1=masks[:, mi, :], op=ALU.mult),
                              reads=[ptres + f"_{c}", "masks"], writes=[ptres + f"_{c}"])

            def pv_part(idx, g, qb):
                pt_ = PT[idx % 3]
                ptres = f"PT{idx % 3}"
                sa = sag[g % 2]
                sres = f"sag{g % 2}"
                chunks = chunks_of(qb)
                for par in range(2):
                    cols = slice(par * 256, (par + 1) * 256)
                    for i, c in enumerate(chunks):
                        kb = qb + c
                        lw = vaug[:, usl(kb), g, 0:128] if par == 0 else vaug[:, usl(kb), g, 64:192]
                        P.add("pe", lambda e, lw=lw, c=c, cols=cols, i=i, n=len(chunks): e.matmul(bank[6][:, cols], lhsT=lw, rhs=pt_[:, c, cols], start=(i == 0), stop=(i == n - 1)),
                              reads=[f"vaug{usl(kb)}_0", f"vaug{usl(kb)}_1", "vaug_ones", ptres + f"_{c}"], writes=["bank6"])
                pv = pvs[idx % 2]
                pres = f"pvs{idx % 2}"
                ds = rden[idx % 2]
                rd = rden[idx % 2]
                ot = otmp[idx % 2]
                P.add("act", lambda e: e.activation(out=pv[:], in_=bank[6][:, :], func=AF.Copy), reads=["bank6"], writes=[pres])
                P.add("dve", lambda e: e.tensor_tensor(out=ds[0:64, :], in0=pv[64:128, 0:256], in1=sinkT[64:128, g, :], op=ALU.add),
                      reads=[pres, "sinkT"], writes=[f"dsum{idx % 2}e"])
                P.add("dve", lambda e: e.tensor_tensor(out=ds[64:128, :], in0=pv[0:64, 256:512], in1=sinkT[0:64, g, :], op=ALU.add),
                      reads=[pres, "sinkT"], writes=[f"dsum{idx % 2}o"])
                P.add("dve", lambda e: e.reciprocal(out=rd[:, :], in_=ds[:, :]), reads=[f"dsum{idx % 2}e", f"dsum{idx % 2}o"], writes=[f"rden{idx % 2}"])
                P.add("dve", lambda e: e.tensor_tensor(out=ot[0:64, :], in0=pv[0:64, 0:256], in1=rd[0:64, :], op=ALU.mult),
                      reads=[pres, f"rden{idx % 2}"], writes=[f"otmp{idx % 2}e"])
                P.add("dve", lambda e: e.tensor_tensor(out=ot[64:128, :], in0=pv[64:128, 256:512], in1=rd[64:128, :], op=ALU.mult),
                      reads=[pres, f"rden{idx % 2}"], writes=[f"otmp{idx % 2}o"])
                P.add("pool", lambda e: e.tensor_tensor(out=abT[:, 2 * g:2 * g + 2, qb * 128:(qb + 1) * 128],
                                                        in0=ot[:, :].rearrange("p (j q) -> p j q", j=2),
                                                        in1=sa[:, :, qb * 128:(qb + 1) * 128], op=ALU.mult),
                      reads=[f"otmp{idx % 2}e", f"otmp{idx % 2}o", sres + "_0", sres + "_1"], writes=["abT", f"abT{g}_{qb}"])

            state["genlist"] = [7]
            def run_item(it):
                t_ = it()
                if t_ is not None:
                    t_()

            for it in qa_items(0):
                run_item(it)
            seq = [(g, qb) for g in range(4) for qb in range(4)]
            fillers = []
            for idx, (g, qb) in enumerate(seq):
                if qb == 0:
                    for it in fillers:
                        run_item(it)
                    fillers = qa_items(g + 1) if g < 3 else []
                s_part(idx, g, qb, 0)
                if idx > 1:
                    pv_part(idx - 2, *seq[idx - 2])
                tail_ = None
                if fillers:
                    tail_ = fillers.pop(0)()
                elif idx <= 1:
                    barrier_mm()
                s_part(idx, g, qb, 1)
                if tail_ is not None:
                    tail_()
                softmax_part(idx, g, qb)
            pv_part(len(seq) - 2, *seq[-2])
            pv_part(len(seq) - 1, *seq[-1])
            state["genlist"] = list(range(7))
            if dbg and ti == 0:
                P.add("sp", lambda e: e.dma_start(out=dbg_out["d_abT"], in_=abT[:].rearrange("p a b -> p (a b)")),
                      reads=[f"abT{g}_{qb}" for g in range(4) for qb in range(4)], dma=True, chan="dbg")

            if limit < 5:
                return
            pb_all = [f"pbT{o}" for o in range(8)]
            ab_all = [f"abT{g}_{qb}" for g in range(4) for qb in range(4)]
            for op_ in range(4):
                elm, rm = take(f"mg{op_}")
                elp, rp = take(f"pr{op_}")
                for j in range(2):
                    o = 2 * op_ + j
                    gi = 0
                    bk1 = gbank()
                    inproj_fm(elm, rm, j, hmain, main_res, 512, bk1)
                    P.add("act", lambda e, gi=gi, bk1=bk1: e.activation(out=gsig[gi][:], in_=bank[bk1][:, :], func=AF.Tanh, scale=0.5), reads=[f"bank{bk1}"], writes=[f"gsig{gi}"])
                    bk2 = gbank()
                    inproj_fm(elm, rm, 2 + j, hmain, main_res, 512, bk2)
                    P.add("act", lambda e, gi=gi, bk2=bk2: e.activation(out=gsig[gi + 1][:], in_=bank[bk2][:, :], func=AF.Tanh, scale=0.5), reads=[f"bank{bk2}"], writes=[f"gsig{gi + 1}"])
                    bk3 = gbank()
                    for k in range(8):
                        P.add("pe", lambda e, k=k, j=j, bk3=bk3, elp=elp: e.matmul(bank[bk3][:, :], lhsT=elp[:, k, j * 128:(j + 1) * 128], rhs=pbT[:, k, :], start=(k == 0), stop=(k == 7)),
                              reads=[rp] + pb_all, writes=[f"bank{bk3}"])
                    bk4 = gbank()
                    for k in range(8):
                        P.add("pe", lambda e, k=k, j=j, bk4=bk4, elp=elp: e.matmul(bank[bk4][:, :], lhsT=elp[:, k, 256 + j * 128:256 + (j + 1) * 128], rhs=abT[:, k, :], start=(k == 0), stop=(k == 7)),
                              reads=[rp] + ab_all, writes=[f"bank{bk4}"])
                    m1, m2 = mt1[o % 2], mt2[o % 2]
                    P.add("dve", lambda e, m1=m1, gi=gi, bk3=bk3: e.scalar_tensor_tensor(out=m1[:], in0=gsig[gi][:], scalar=1.0, in1=bank[bk3][:, :], op0=ALU.add, op1=ALU.mult),
                          reads=[f"bank{bk3}", f"gsig{gi}"], writes=["mt1_0"])
                    P.add("dve", lambda e, m2=m2, gi=gi, bk4=bk4: e.scalar_tensor_tensor(out=m2[:], in0=gsig[gi + 1][:], scalar=1.0, in1=bank[bk4][:, :], op0=ALU.add, op1=ALU.mult),
                          reads=[f"bank{bk4}", f"gsig{gi + 1}"], writes=["mt2_0"])
                    P.add("pool", lambda e, m1=m1, m2=m2, o=o: e.tensor_tensor(out=mT[:, o, :], in0=m1[:], in1=m2[:], op=ALU.add),
                          reads=["mt1_0", "mt2_0"], writes=["mT", f"mT{o}"])
                hoist_event(False)
            if dbg and ti == 0:
                P.add("sp", lambda e: e.dma_start(out=dbg_out["d_mT"], in_=mT[:].rearrange("p a b -> p (a b)")), reads=[f"mT{o}" for o in range(8)], dma=True, chan="dbg")

            if limit < 6:
                return
            elo = [take("wo0"), take("wo1")]
            m_all = [f"mT{o}" for o in range(8)]
            for ob in range(4):
                t0 = tl["tok0"] + ob * BLK
                i = state.setdefault("yb", 0)
                state["yb"] += 1
                yb, yres = ybuf[i % 2], f"ybuf{i % 2}"
                c = 32 + (i % 4) * 8
                P.add("sp", lambda e, yb=yb, t0=t0: e.dma_start(out=yb[:], in_=xm[t0:t0 + BLK, :]), writes=[yres], dma=True, chan="ld" + yres)
                bks = [gbank(), gbank()]
                for half in range(2):
                    el_, r_ = elo[half]
                    for k in range(8):
                        P.add("pe", lambda e, k=k, half=half, ob=ob, el_=el_, bks=bks: e.matmul(bank[bks[half]][:, :], lhsT=mT[:, k, ob * 128:(ob + 1) * 128], rhs=el_[:, k, :],
                                                                                     start=(k == 0), stop=(k == 7)),
                              reads=[r_] + m_all, writes=[f"bank{bks[half]}"])
                    P.add("act", lambda e, half=half, bks=bks, c=c: e.activation(out=PT[0][:, half, :], in_=bank[bks[half]][:, :], func=AF.Square,
                                                                              accum_out=stat[:, c + half:c + half + 1]),
                          reads=[f"bank{bks[half]}"], writes=[f"PT0_{half}", f"st{c + half}"])
                P.add("dve", lambda e, c=c: e.tensor_tensor(out=stat[:, c + 2:c + 3], in0=stat[:, c:c + 1], in1=stat[:, c + 1:c + 2], op=ALU.add),
                      reads=[f"st{c}", f"st{c + 1}"], writes=[f"st{c + 2}"])
                P.add("act", lambda e, c=c: e.activation(out=stat[:, c + 3:c + 4], in_=stat[:, c + 2:c + 3], func=AF.Sqrt, bias=epsT[:], scale=1.0 / D),
                      reads=[f"st{c + 2}", "epsT"], writes=[f"st{c + 3}"])
                P.add("dve", lambda e, c=c: e.reciprocal(out=stat[:, c + 4:c + 5], in_=stat[:, c + 3:c + 4]), reads=[f"st{c + 3}"], writes=[f"st{c + 4}"])
                for half in range(2):
                    yt = ytmp[half]
                    P.add("dve", lambda e, half=half, yt=yt, bks=bks, c=c: e.scalar_tensor_tensor(out=yt[:], in0=bank[bks[half]][:, :], scalar=stat[:, c + 4:c + 5],
                                                                                        in1=gpost[:, half * 512:(half + 1) * 512], op0=ALU.mult, op1=ALU.mult),
                          reads=[f"bank{bks[half]}", f"st{c + 4}", "gpost"], writes=[f"ytmp{half}"])
                    P.add("pool", lambda e, half=half, yt=yt, yb=yb: e.tensor_tensor(out=yb[:, half * 512:(half + 1) * 512], in0=yt[:], in1=yb[:, half * 512:(half + 1) * 512], op=ALU.add),
                          reads=[f"ytmp{half}", yres], writes=[yres])
                P.add("pool", lambda e, yb=yb, t0=t0: e.dma_start(out=ym[t0:t0 + BLK, :], in_=yb[:]), reads=[yres], dma=True, chan="st" + yres)
                hoist_event(True)
            while hq["pend"] is not None or hq["L"]:
                hoist_event(True)


        for ti, tl in enumerate(tiles):
            if limit < 1 or (limit < 6 and ti > 0):
                break
            tile_body(ti, tl)

        fin = sb("fin", [128, 8], F32)
        P.add("act", lambda e: e.activation(out=fin[:, 0:1], in_=epsT[:], func=AF.Copy), reads=["epsT"], writes=["fin_act"])
        P.add("dve", lambda e: e.memset(fin[:, 1:2], 0.0), writes=["fin_dve"])
        P.add("pool", lambda e: e.memset(fin[:, 2:3], 0.0), writes=["fin_pool"])
        P.add("pe", lambda e: e.matmul(bank[0][:, 0:1], lhsT=sel[0:1, 0, :], rhs=sel[0:1, 0, 0:1], start=True, stop=True), reads=["sel"], writes=["bank0"])
        P.add("dve", lambda e: e.tensor_copy(out=fin[:, 3:4], in_=bank[0][:, 0:1]), reads=["bank0"], writes=["fin_pe"])
        P.add("sp", lambda e: e.dma_start(out=fin[:, 5:6], in_=fin[:, 4:5]), reads=["fin_act", "fin_dve", "fin_pool", "fin_pe"], writes=["fin_sp"], dma=True, chan="fin")

        keys = P.finalize()
        sems = {k: es.enter_context(nc.semaphore(k)) for k in keys}
        block = es.enter_context(nc.Block())
        out_chans = [k for k in keys if k.startswith("dma_")]

        def emit_eng(ename):
            def body(eng):
                for o in P.ops[ename]:
                    for k, v in o.waits:
                        eng.wait_ge(sems[k], v)
                    if o.dma:
                        insts = o.fn(eng)
                        if not isinstance(insts, (list, tuple)):
                            insts = [insts]
                        assert len(insts) == o.ndma
                        for i_ in insts:
                            i_.then_inc(sems[o.sig[0]], 16)
                    else:
                        inst = o.fn(eng)
                        if o.sig is not None:
                            inst.then_inc(sems[o.sig[0]], 1)
                if ename == "sp":
                    for k in out_chans:
                        eng.wait_ge(sems[k], P.final_counts[k])
            return body
        block.sync(emit_eng("sp"))
        block.scalar(emit_eng("act"))
        block.vector(emit_eng("dve"))
        block.gpsimd(emit_eng("pool"))
        block.tensor(emit_eng("pe"))
    nc._prog_stats = {e: len(P.ops[e]) for e in P.ENGS}
    return nc


def rope_tables(positions):
    half = HD // 2
    inv = (np.float32(THETA) ** (-(np.arange(half, dtype=np.float32) / np.float32(half)))).astype(np.float32)
    ang = positions.astype(np.float32)[None, :] * inv[:, None]
    cos = np.cos(ang).astype(np.float32)
    sin = np.sin(ang).astype(np.float32)
    p = np.arange(128)
    ct = cos[p % 32]
    sgn = np.where((p % 64) < 32, -1.0, 1.0).astype(np.float32)
    st = sin[p % 32] * sgn[:, None]
    return np.stack([ct, st], 0)


def band_mat(g, rel, mode):
    w = POOL_WINDOWS[g]
    B = 1024
    S = 1 << 30
    if mode == "first":
        B = 0
    if mode == "last":
        S = B + 128
    s = B + rel * 128 + np.arange(128)[:, None]
    t = B + np.arange(128)[None, :]
    lo = np.maximum(t - w // 2, 0)
    hi = np.minimum(t - w // 2 + w, S)
    inr = (s >= lo) & (s < hi)
    val = inr / (hi - lo).astype(np.float64) - (s == t)
    return val.astype(np.float32)


def make_consts(valid_left, valid_right):
    bands = np.zeros((128, 28, 128), np.float32)
    for g in range(4):
        bands[:, 3 * g + 0] = band_mat(g, -1, "int")
        bands[:, 3 * g + 1] = band_mat(g, 0, "int")
        bands[:, 3 * g + 2] = band_mat(g, 1, "int")
        bands[:, 12 + g] = band_mat(g, 0, "first")
        bands[:, 16 + g] = band_mat(g, 0, "last")
        bands[:, 20 + g] = band_mat(g, 0, "int" if valid_left else "first")
        bands[:, 24 + g] = band_mat(g, 0, "int" if valid_right else "last")
    j = np.arange(128)[:, None]
    i = np.arange(128)[None, :]
    mp = (j >= i).astype(np.float32)
    mn = (j <= i).astype(np.float32)
    masks = np.zeros((128, 4, 512), np.float32)
    masks[:, 0] = np.tile(mp, (1, 4))
    masks[:, 1] = np.tile(mn, (1, 4))
    masks[:, 2] = np.tile(mp, (1, 4)) * (1.0 if valid_left else 0.0)
    masks[:, 3] = np.tile(mn, (1, 4)) * (1.0 if valid_right else 0.0)
    cst = np.zeros((128, 2, 128), np.float32)
    cst[:, 0] = np.eye(128)
    k = np.arange(128)
    partner = (k // 64) * 64 + ((k % 64) + 32) % 64
    cst[partner, 1, k] = 1.0
    sel = np.zeros((1, 2, 128), np.float32)
    sel[0, 0, 64:] = 4.0
    sel[0, 1, :64] = 4.0
    bf = ml_dtypes.bfloat16
    return (bands.reshape(128, -1).astype(bf), masks.reshape(128, -1).astype(bf),
            cst.reshape(128, -1).astype(bf), sel.reshape(1, -1).astype(bf))


def sink_layout(attn_sink):
    return np.asarray(attn_sink, np.float32).reshape(1, NH)


def core_inputs(segs_x, halos, seg_kinds, pos0s, valid_left, valid_right, shared):
    xm = np.ascontiguousarray(np.concatenate(segs_x, 0))
    tabs = []
    for x, p0 in zip(segs_x, pos0s):
        n = x.shape[0]
        tabs.append(rope_tables(np.arange(p0 - 128, p0 + n + 128)))
    tabs = np.ascontiguousarray(np.concatenate(tabs, 2))
    bands, masks, cst, sel = make_consts(valid_left, valid_right)
    d = dict(shared)
    d.update(xm=xm, xh=np.ascontiguousarray(halos), tabs=tabs, bands=bands, masks=masks, cst=cst, sel=sel)
    return d


def shared_inputs(norm_pre, w_in, w_pool_group, pool_scale, w_pool_proj, attn_sink, w_attn_proj, w_out, norm_post):
    f = lambda a: np.ascontiguousarray(np.asarray(a, np.float32))
    return dict(
        w_in=f(w_in[0]), w_pg=f(w_pool_group[0]), w_pp=f(w_pool_proj[0]), w_ap=f(w_attn_proj[0]), w_out=f(w_out[0]),
        gpre_b=f(np.tile(np.asarray(norm_pre[0]).reshape(1, D), (128, 1))), pscale_c=f(np.asarray(pool_scale[0]).reshape(8, 128).T),
        gpost_b=f(np.tile(np.asarray(norm_post[0]).reshape(1, D), (128, 1))), sink_rows=f(sink_layout(attn_sink[0])),
    )


_NC_CACHE = {}


def kernel(x_prompt, x_sample, norm_pre, w_in, w_pool_group, pool_scale, w_pool_proj,
           attn_sink, w_attn_proj, w_out, norm_post):
    x_prompt = np.asarray(x_prompt, np.float32)
    x_sample = np.asarray(x_sample, np.float32)
    shared = shared_inputs(norm_pre, w_in, w_pool_group, pool_scale, w_pool_proj, attn_sink, w_attn_proj, w_out, norm_post)
    segs = [("prompt", 2048), ("prompt", 2048), ("sample", 4096)]
    if "nc" not in _NC_CACHE:
        _NC_CACHE["nc"] = build_program(segs)
    nc = _NC_CACHE["nc"]
    in_maps = []
    for c in range(8):
        sb_, hf = c // 2, c % 2
        sx = [x_prompt[2 * c], x_prompt[2 * c + 1], x_sample[sb_, hf * 4096:(hf + 1) * 4096]]
        halos = np.zeros((2, 128, D), np.float32)
        if hf == 1:
            halos[0] = x_sample[sb_, 4096 - 128:4096]
        else:
            halos[1] = x_sample[sb_, 4096:4096 + 128]
        in_maps.append(core_inputs(sx, halos, [k for k, _ in segs], [0, 0, hf * 4096], hf == 1, hf == 0, shared))
    res = run_bass_kernel_spmd(nc, in_maps, core_ids=list(range(8)))
    y_prompt = np.empty_like(x_prompt)
    y_sample = np.empty_like(x_sample)
    for c in range(8):
        ymc = res.results[c]["ym"]
        y_prompt[2 * c] = ymc[0:2048]
        y_prompt[2 * c + 1] = ymc[2048:4096]
        y_sample[c // 2, (c % 2) * 4096:(c % 2 + 1) * 4096] = ymc[4096:8192]
    return (y_prompt, y_sample)
```

```python
import numpy as np
import ml_dtypes
from contextlib import ExitStack
import concourse.bass as bass
import concourse.mybir as mybir
from concourse.bass_utils import run_bass_kernel_spmd

F32 = mybir.dt.float32
BF16 = mybir.dt.bfloat16
AF = mybir.ActivationFunctionType
ALU = mybir.AluOpType

D = 1024
NH, NKV, HD = 16, 4, 64
TILE = 512
BLK = 128
POOL_WINDOWS = (2, 4, 8, 16)
EPS = 1e-6
THETA = 10000.0
IN_W = 6656
C_U, C_PG, C_Q, C_K, C_V, C_AG, C_GP, C_GA = 0, 1024, 2048, 3072, 3328, 3584, 4608, 5632
NRING = 4


class Op:
    __slots__ = ("eng", "fn", "reads", "writes", "dma", "chan", "ndma", "idx", "deps", "sig", "waits", "name")


class Prog:
    ENGS = ("sp", "act", "dve", "pool", "pe")

    def __init__(self, same_eng_dist=3):
        self.ops = {e: [] for e in self.ENGS}
        self.last_w = {}
        self.readers = {}
        self.same_eng_dist = same_eng_dist
        self.all = []

    def add(self, eng, fn, reads=(), writes=(), dma=False, chan=None, ndma=1, name=""):
        o = Op()
        o.eng = eng; o.fn = fn; o.reads = tuple(reads); o.writes = tuple(writes)
        o.dma = dma; o.chan = chan; o.ndma = ndma; o.name = name
        o.idx = len(self.ops[eng]); o.deps = []; o.sig = None; o.waits = []
        deps = {}
        for r in o.reads:
            w = self.last_w.get(r)
            if w is not None:
                deps[id(w)] = (w, True)
            if r.startswith("bank") or r == "pTb":
                for rd in self.readers.get(r, ()):
                    if id(rd) not in deps and rd.eng != eng:
                        deps[id(rd)] = (rd, False)
        for r in o.writes:
            w = self.last_w.get(r)
            if w is not None and id(w) not in deps:
                deps[id(w)] = (w, False)
            for rd in self.readers.get(r, ()):
                if id(rd) not in deps:
                    deps[id(rd)] = (rd, False)
        for d, israw in deps.values():
            if d is o:
                continue
            if (not d.dma) and (not o.dma) and d.eng == o.eng:
                if o.eng == "pe":
                    continue
                if not israw:
                    continue
                if o.idx - d.idx >= self.same_eng_dist:
                    continue
            o.deps.append(d)
        for r in o.reads:
            self.readers.setdefault(r, []).append(o)
        for r in o.writes:
            self.last_w[r] = o
            self.readers[r] = []
        self.ops[eng].append(o)
        self.all.append(o)
        return o

    def finalize(self):
        need = set()
        for o in self.all:
            for d in o.deps:
                need.add(id(d))
        cnt = {}
        for e in self.ENGS:
            for o in self.ops[e]:
                if o.dma:
                    key = "dma_" + o.chan
                    cnt[key] = cnt.get(key, 0) + 16 * o.ndma
                    o.sig = (key, cnt[key])
                elif id(o) in need:
                    key = "eng_" + e
                    cnt[key] = cnt.get(key, 0) + 1
                    o.sig = (key, cnt[key])
        self.final_counts = cnt
        for e in self.ENGS:
            seen = {}
            for o in self.ops[e]:
                w = {}
                for d in o.deps:
                    k, v = d.sig
                    if seen.get(k, 0) >= v:
                        continue
                    w[k] = max(w.get(k, 0), v)
                for k, v in w.items():
                    seen[k] = v
                o.waits = sorted(w.items())
        return sorted(cnt.keys())


def stream_elements():
    el = []
    el.append(("u0", [("win", C_U, 512)]))
    el.append(("u1", [("win", C_U + 512, 512)]))
    el.append(("vv", [("win", C_V, 256), ("win", C_V, 256)]))
    el.append(("kk", [("win", C_K + 64 * (i // 2), 64) for i in range(8)]))
    el.append(("pg0", [("win", C_PG, 512)]))
    el.append(("pg1", [("win", C_PG + 512, 512)]))
    for g in range(4):
        el.append((f"qa{g}", [("win", C_Q + 256 * g, 256), ("win", C_AG + 256 * g, 256)]))
    for op in range(4):
        el.append((f"mg{op}", [("win", C_GP + 256 * op, 256), ("win", C_GA + 256 * op, 256)]))
        el.append((f"pr{op}", [("wpp", 256 * op, 256), ("wap", 256 * op, 256)]))
    el.append(("wo0", [("wout", 0, 512)]))
    el.append(("wo1", [("wout", 512, 512)]))
    return el


ELEMS = stream_elements()
ELEM_IDX = {n: i for i, (n, _) in enumerate(ELEMS)}
NEL = len(ELEMS)


def build_program(segs, dbg=False, limit=99):
    nc = bass.Bass("TRN2", target_bir_lowering=False)
    ntok = sum(n for _, n in segs)
    nsamp = sum(1 for k, _ in segs if k == "sample")
    ntab = sum(n + 256 for _, n in segs)

    def din(name, shape, dt=F32):
        return nc.dram_tensor(name, list(shape), dt, kind="ExternalInput").ap()

    xm = din("xm", [ntok, D])
    xh = din("xh", [max(nsamp, 1) * 2, BLK, D])
    w_in = din("w_in", [D, IN_W])
    w_pg = din("w_pg", [4, 256, 256])
    w_pp = din("w_pp", [D, D])
    w_ap = din("w_ap", [D, D])
    w_out = din("w_out", [D, D])
    gpre_b = din("gpre_b", [128, D])
    pscale_c = din("pscale_c", [128, 8])
    gpost_b = din("gpost_b", [128, D])
    sink_rows = din("sink_rows", [1, 16])
    tabs = din("tabs", [2, 128, ntab])
    bands_d = din("bands", [128, 28 * 128], BF16)
    masks_d = din("masks", [128, 4 * 512], BF16)
    cst_d = din("cst", [128, 2 * 128], BF16)
    sel_d = din("sel", [1, 2 * 128], BF16)
    ym = nc.dram_tensor("ym", [ntok, D], F32, kind="ExternalOutput").ap()
    wsc = nc.dram_tensor("wsc", [NEL, 128, 8 * 512], BF16, kind="Internal").ap()
    dbg_out = {}
    if dbg:
        for nm, shp, dt in (("d_hT", [128, 8 * 1024], BF16), ("d_kT", [128, 4 * 768], BF16),
                            ("d_ub", [128, 6 * 1024], BF16), ("d_va", [128, 6 * 4 * 192], BF16),
                            ("d_pbT", [128, 8 * 512], BF16), ("d_abT", [128, 8 * 512], BF16),
                            ("d_mT", [128, 8 * 512], BF16)):
            dbg_out[nm] = nc.dram_tensor(nm, shp, dt, kind="ExternalOutput").ap()

    wsrc = {"win": w_in, "wpp": w_pp, "wap": w_ap, "wout": w_out}
    P = Prog()
    es = ExitStack()
    with es:
        def sb(name, shape, dt):
            return es.enter_context(nc.sbuf_tensor("s_" + name, list(shape), dt))

        def ps(name, shape, dt):
            return es.enter_context(nc.psum_tensor("p_" + name, list(shape), dt))

        ring = [sb(f"ring{i}", [128, 8, 512], BF16) for i in range(NRING)]
        wpg = sb("wpg", [128, 4, 2, 256], BF16)
        xbuf = [sb(f"xbuf{i}", [128, D], F32) for i in range(2)]
        ybuf = [sb(f"ybuf{i}", [128, D], F32) for i in range(2)]
        xs = [sb(f"xs{i}", [128, D], BF16) for i in range(2)]
        hT = sb("hT", [128, 8, 8 * 128], BF16)
        kpre = [sb(f"kpre{i}", [128, 768], BF16) for i in range(2)]
        kT = sb("kT", [128, 4, 768], BF16)
        vaug = sb("vaug", [128, 6, 4, 192], BF16)
        ub = sb("ub", [128, 6, D], BF16)
        poolT = [sb(f"poolT{i}", [128, 2, 512], BF16) for i in range(2)]
        sg = [sb("sg0", [128, 512], BF16)] * 2
        pbT = sb("pbT", [128, 8, 512], BF16)
        abT = sb("abT", [128, 8, 512], BF16)
        mT = sb("mT", [128, 8, 512], BF16)
        qpre = [sb(f"qpre{i}", [128, 512], BF16) for i in range(2)]
        qg = [sb(f"qg{i}", [128, 2, 512], BF16) for i in range(2)]
        sag = [sb(f"sag{i}", [128, 2, 512], BF16) for i in range(2)]
        rt1 = [sb("rt1_0", [128, 768], F32)] * 2
        rt2 = [sb("rt2_0", [128, 768], F32)] * 2
        tab = sb("tab", [128, 2, 768], F32)
        PT = [sb(f"PT{i}", [128, 3, 512], BF16) for i in range(3)]
        rden = [sb(f"rden{i}", [128, 256], F32) for i in range(2)]
        otmp = [sb(f"otmp{i}", [128, 256], F32) for i in range(2)]
        gsig = [sb(f"gsig{i}", [128, 512], F32) for i in range(2)] * 2
        mt1 = [sb("mt1_0", [128, 512], F32)] * 2
        mt2 = [sb("mt2_0", [128, 512], F32)] * 2
        ytmp = [sb(f"ytmp{i}", [128, 512], F32) for i in range(2)]
        stat = sb("stat", [128, 64], F32)
        gpre = sb("gpre", [128, D], F32)
        pscale = sb("pscale", [128, 8], F32)
        psh = sb("psh", [128, 8], F32)
        gpost = sb("gpost", [128, D], F32)
        sinkf = sb("sinkf", [1, 16], F32)
        sinkb = sb("sinkb", [1, 16], BF16)
        sinkT = sb("sinkT", [128, 4, 256], F32)
        pvs = [sb(f"pvs{i}", [128, 512], F32) for i in range(2)]
        bands = sb("bands", [128, 28, 128], BF16)
        masks = sb("masks", [128, 4, 512], BF16)
        cst = sb("cst", [128, 2, 128], BF16)
        sel = sb("sel", [1, 2, 128], BF16)
        epsT = sb("epsT", [128, 1], F32)
        bank = [ps(f"bank{i}", [128, 512], F32) for i in range(8)]
        pTb = bank[7][:].bitcast(BF16).rearrange("p (c t) -> p c t", c=8)

        ident = cst[:, 0, :]
        perm = cst[:, 1, :]

        state = {"gen": 0, "genlist": list(range(7)), "rr": 0, "stream": 0, "n": 0}

        def gbank():
            lst = state["genlist"]
            b = lst[state["gen"] % len(lst)]
            state["gen"] += 1
            return b

        def ew_eng():
            e = ("dve", "pool", "act")[state["rr"] % 3]
            state["rr"] += 1
            return e

        def uniq(p):
            state["n"] += 1
            return f"{p}{state['n']}"

        def ld(dst_ap, src_ap, res, chan):
            P.add("sp", lambda e: e.dma_start(out=dst_ap, in_=src_ap), writes=[res], dma=True, chan=chan)

        ld(gpre[:], gpre_b, "gpre", "c0")
        ld(pscale[:], pscale_c, "pscale", "c1")
        ld(gpost[:], gpost_b, "gpost", "c2")
        ld(sinkf[:], sink_rows, "sinkf", "c3")
        ld(bands[:].rearrange("p a b -> p (a b)"), bands_d, "bands", "c4")
        ld(masks[:].rearrange("p a b -> p (a b)"), masks_d, "masks", "c5")
        ld(cst[:].rearrange("p a b -> p (a b)"), cst_d, "cst", "c6")
        ld(sel[:].rearrange("p a b -> p (a b)"), sel_d, "sel", "c7")
        P.add("act", lambda e: e.activation(out=sinkb[:], in_=sinkf[:], func=AF.Exp),
              reads=["sinkf"], writes=["sinkb"])
        for g in range(4):
            for par in range(2):
                sb0 = sinkb[0:1, 4 * g + par:4 * g + par + 1]
                srhs = bass.AP(sinkb, sb0.offset, [[sb0.ap[0][0], 1], [2, 2], [0, 128]])
                P.add("pe", lambda e, g=g, par=par, srhs=srhs: e.matmul(bank[g // 2][:, (g % 2) * 256:(g % 2 + 1) * 256].rearrange("p (j q) -> p j q", j=2),
                                                                     lhsT=sel[0:1, par, :], rhs=srhs, start=(par == 0), stop=(par == 1)),
                      reads=["sel", "sinkb"], writes=[f"bank{g // 2}"])
        for h2 in range(2):
            P.add("dve", lambda e, h2=h2: e.tensor_copy(out=sinkT[:, 2 * h2:2 * h2 + 2, :].rearrange("p a b -> p (a b)"), in_=bank[h2][:, :]),
                  reads=[f"bank{h2}"], writes=["sinkT"])
        P.add("dve", lambda e: e.memset(epsT[:], EPS), writes=["epsT"])
        P.add("dve", lambda e: e.tensor_scalar(out=psh[:], in0=pscale[:], scalar1=0.25, scalar2=None, op0=ALU.mult), reads=["pscale"], writes=["psh"])
        P.add("pool", lambda e: e.memset(vaug[:, :, :, 64:128], 4.0), writes=["vaug_ones"])

        stg32 = [(xbuf[0], "xbuf0"), (xbuf[1], "xbuf1"), (ybuf[0], "ybuf0"), (ybuf[1], "ybuf1")]
        for half in range(2):
            buf, res = stg32[half]
            src = w_pg[2 * half:2 * half + 2].rearrange("g (kc p) d -> p g kc d", p=128)
            dst = buf[:].rearrange("p (g kc d) -> p g kc d", g=2, kc=2)
            P.add("sp", lambda e, dst=dst, src=src: e.dma_start(out=dst, in_=src), writes=[res], dma=True, chan="pl" + res)
            P.add("dve", lambda e, buf=buf, half=half: e.tensor_copy(
                out=wpg[:, 2 * half:2 * half + 2, :, :].rearrange("p g kc d -> p (g kc d)"), in_=buf[:]),
                reads=[res], writes=["wpg"])
        for ei, (ename, pieces) in enumerate(ELEMS):
            dview = wsc[ei].rearrange("p (dc c) -> p dc c", dc=8)
            col = 0
            dmas = []
            for (src, c0, ncol) in pieces:
                sap = wsrc[src].rearrange("(dc p) c -> p dc c", p=128)[:, :, c0:c0 + ncol]
                dmas.append((dview[:, :, col:col + ncol], sap))
                col += ncol

            def fn(e, dmas=dmas):
                return [e.dma_start(out=d, in_=s_) for d, s_ in dmas]
            P.add("pool", fn, writes=[f"wsc{ei}"], dma=True, chan=f"pw{ei}", ndma=len(dmas))

        stream_order = []

        def issue_stream(k):
            if k >= len(stream_order):
                return
            ei = ELEM_IDX[stream_order[k]]
            slot = k % NRING
            if stream_order[k] == "vv":
                P.add("sp", lambda e, ei=ei, slot=slot: e.dma_start(out=ring[slot][:, :, 0:256],
                                                                     in_=wsc[ei].rearrange("p (dc c) -> p dc c", dc=8)[:, :, 0:256]),
                      reads=[f"wsc{ei}"], writes=[f"ring{slot}"], dma=True, chan=f"rg{slot}")
                return
            P.add("sp", lambda e, ei=ei, slot=slot: e.dma_start(out=ring[slot][:].rearrange("p a b -> p (a b)"), in_=wsc[ei]),
                  reads=[f"wsc{ei}"], writes=[f"ring{slot}"], dma=True, chan=f"rg{slot}")

        def take(name):
            k = state["stream"]
            assert stream_order[k] == name, (k, stream_order[k], name)
            state["stream"] += 1
            issue_stream(k + NRING - 2)
            return ring[k % NRING], f"ring{k % NRING}"

        tiles = []
        tok0 = 0
        tabo = 0
        si = 0
        for kind, n in segs:
            nblk = n // BLK
            for t in range(n // TILE):
                tiles.append(dict(kind=kind, nblk=nblk, b0=4 * t, tok0=tok0 + t * TILE, tabo=tabo, si=si, first=(t == 0), last=(t == n // TILE - 1)))
            tok0 += n
            tabo += n + 256
            if kind == "sample":
                si += 1
        for _ in tiles:
            stream_order.extend(n for n, _ in ELEMS)
        for k in range(NRING - 2):
            issue_stream(k)

        def front_a(tl, gb):
            kind, nblk = tl["kind"], tl["nblk"]
            if gb < 0 or gb >= nblk:
                if kind == "prompt":
                    return None
                src = xh[2 * tl["si"] + (0 if gb < 0 else 1)]
            else:
                t0 = tl["tok0"] - tl["b0"] * BLK + gb * BLK
                src = xm[t0:t0 + BLK, :]
            i = state.setdefault("xb", 0)
            state["xb"] += 1
            xb, xres = xbuf[i % 2], f"xbuf{i % 2}"
            xs_, xsres = xs[i % 2], f"xs{i % 2}"
            c = (i % 8) * 4
            P.add("sp", lambda e: e.dma_start(out=xb[:], in_=src), writes=[xres], dma=True, chan="ld" + xres)
            P.add("dve", lambda e: e.scalar_tensor_tensor(out=xs_[:], in0=xb[:], scalar=1.0, in1=xb[:], op0=ALU.mult, op1=ALU.mult, accum_out=stat[:, c:c + 1]),
                  reads=[xres], writes=[xsres, f"st{c}"])
            P.add("act", lambda e: e.activation(out=stat[:, c + 1:c + 2], in_=stat[:, c:c + 1], func=AF.Sqrt, bias=epsT[:], scale=1.0 / D),
                  reads=[f"st{c}", "epsT"], writes=[f"st{c + 1}"])
            P.add("dve", lambda e: e.reciprocal(out=stat[:, c + 2:c + 3], in_=stat[:, c + 1:c + 2]), reads=[f"st{c + 1}"], writes=[f"st{c + 2}"])
            P.add("dve", lambda e: e.scalar_tensor_tensor(out=xs_[:], in0=xb[:], scalar=stat[:, c + 2:c + 3], in1=gpre[:], op0=ALU.mult, op1=ALU.mult),
                  reads=[xres, f"st{c + 2}", "gpre"], writes=[xsres])
            return (gb % 8, xs_, xsres)

        def front_b(tok):
            if tok is None:
                return
            slot, xs_, xsres = tok
            for ch in range(8):
                P.add("pe", lambda e, ch=ch: e.transpose(pTb[:, ch, :], xs_[:, ch * 128:(ch + 1) * 128], ident),
                      reads=[xsres, "cst"], writes=["bank7"])
            P.add("act", lambda e: e.activation(out=hT[:, :, slot * 128:(slot + 1) * 128], in_=pTb, func=AF.Copy),
                  reads=["bank7"], writes=[f"hT{slot}"])

        def front(tl, gb):
            front_b(front_a(tl, gb))

        def present(tl, gb):
            return not (tl["kind"] == "prompt" and (gb < 0 or gb >= tl["nblk"]))

        def evac_copy(i, out_ap, in_ap, reads, writes):
            if i % 2 == 0:
                P.add("act", lambda e: e.activation(out=out_ap, in_=in_ap, func=AF.Copy), reads=reads, writes=writes)
            else:
                P.add("dve", lambda e: e.tensor_copy(out=out_ap, in_=in_ap), reads=reads, writes=writes)

        def inproj_fm(el, elres, q, rhs_ap, rhs_res, n, bk, col0=0):
            for dc in range(8):
                P.add("pe", lambda e, dc=dc: e.matmul(bank[bk][:, col0:col0 + n], lhsT=el[:, dc, q * 128:(q + 1) * 128],
                                                      rhs=rhs_ap(dc), start=(dc == 0), stop=(dc == 7)),
                      reads=[elres] + rhs_res, writes=[f"bank{bk}"])

        def tile_body(ti, tl):
            b0, nblk, kind = tl["b0"], tl["nblk"], tl["kind"]
            state["genlist"] = list(range(7))
            pend_ = None
            for gb in range(b0 - 1 if tl["first"] else b0 + 1, b0 + 5):
                if (ti, gb) not in state.setdefault("fronted", set()):
                    tok_ = front_a(tl, gb)
                    front_b(pend_)
                    pend_ = tok_
            front_b(pend_)
            nxt = tiles[ti + 1] if ti + 1 < len(tiles) else None
            hq = {"L": [], "t5only": set(), "pend": None, "pend_gb": None}
            if nxt is not None:
                nb0 = nxt["b0"]
                live = {(b0 + j) % 8 for j in range(4)}
                cand = [gb for gb in range(nb0 - 1 if nxt["first"] else nb0 + 1, nb0 + 5) if present(nxt, gb)]
                t4b = [gb for gb in cand if (gb % 8) not in live]
                t5b = [gb for gb in cand if (gb % 8) in live]
                hq["L"] = t4b + t5b
                hq["t5only"] = set(t5b)
                for gb in cand:
                    state.setdefault("fronted", set()).add((ti + 1, gb))

            def hoist_event(in_t5):
                if hq["pend"] is not None and (in_t5 or hq["pend_gb"] not in hq["t5only"]):
                    front_b(hq["pend"])
                    hq["pend"] = None
                if hq["pend"] is None and hq["L"]:
                    gb_ = hq["L"].pop(0)
                    hq["pend"] = front_a(nxt, gb_)
                    hq["pend_gb"] = gb_
            mslot = (b0 % 8)
            main_res = [f"hT{mslot + j}" for j in range(4)]
            hmain = lambda dc: hT[:, dc, mslot * 128:(mslot + 4) * 128]
            kbs = [kb for kb in range(6) if present(tl, b0 - 1 + kb)]
            usl = lambda kb: (b0 + kb) % 6
            newkb = [kb for kb in kbs if tl["first"] or kb >= 2]
            hblk = lambda kb, dc: hT[:, dc, ((b0 - 1 + kb) % 8) * 128:((b0 - 1 + kb) % 8 + 1) * 128]
            hres = lambda kb: [f"hT{(b0 - 1 + kb) % 8}"]
            if limit < 2:
                return
            to = tl["tabo"] + b0 * BLK
            P.add("sp", lambda e, to=to: e.dma_start(out=tab[:], in_=tabs[:, :, to:to + 768].rearrange("a p t -> p a t")),
                  writes=["tab"], dma=True, chan="tab")
            import os as _os
            T1N = int(_os.environ.get("T1_N", "99"))
            if T1N < 3:
                return
            if T1N < 4:
                return
            elu = [take("u0"), take("u1")]
            n_e = 0
            for kb in [k_ for k_ in (1, 2, 3, 4, 0, 5) if k_ in newkb]:
                for half in range(2):
                    bk = gbank()
                    el_, r_ = elu[half]
                    for dc in range(8):
                        P.add("pe", lambda e, dc=dc, kb=kb, el_=el_, bk=bk: e.matmul(bank[bk][:, :], lhsT=hblk(kb, dc), rhs=el_[:, dc, :], start=(dc == 0), stop=(dc == 7)),
                              reads=[r_] + hres(kb), writes=[f"bank{bk}"])
                    evac_copy(n_e, ub[:, usl(kb), half * 512:(half + 1) * 512], bank[bk][:, :], [f"bank{bk}"], [f"ub{usl(kb)}_{half}"])
                    n_e += 1
            el, elres = take("vv")
            for pr in range(3):
                bk = gbank()
                any_ = False
                for j in range(2):
                    kb = 2 * pr + j
                    if kb not in newkb:
                        continue
                    any_ = True
                    for dc in range(8):
                        P.add("pe", lambda e, dc=dc, kb=kb, j=j, bk=bk: e.matmul(bank[bk][:, j * 256:(j + 1) * 256], lhsT=hblk(kb, dc), rhs=el[:, dc, 0:256],
                                                                         start=(dc == 0), stop=(dc == 7)),
                              reads=[elres] + hres(kb), writes=[f"bank{bk}"])
                    for cp, off in enumerate((0, 128)):
                        evac_copy(kb + cp, vaug[:, usl(kb), :, off:off + 64], bank[bk][:, j * 256:(j + 1) * 256].rearrange("p (g d) -> p g d", g=4),
                                  [f"bank{bk}"], [f"vaug{usl(kb)}_{cp}"])
            elk, elkres = take("kk")

            kbanks = {}

            def k_main(g):
                bkm = gbank()
                kbanks[g] = bkm
                for dc in range(8):
                    P.add("pe", lambda e, dc=dc: e.matmul(bank[bkm][:, :], lhsT=elk[:, dc, g * 128:(g + 1) * 128], rhs=hmain(dc), start=(dc == 0), stop=(dc == 7)),
                          reads=[elkres] + main_res, writes=[f"bank{bkm}"])

            def k_halo(g):
                kp = kpre[g % 2]
                kr = f"kpre{g % 2}"
                bkm = kbanks[g]
                bkh = gbank()
                for hi, kb in enumerate((0, 5)):
                    if kb not in kbs:
                        continue
                    for dc in range(8):
                        P.add("pe", lambda e, dc=dc, hi=hi, kb=kb: e.matmul(bank[bkh][:, hi * 128:(hi + 1) * 128], lhsT=elk[:, dc, g * 128:(g + 1) * 128], rhs=hblk(kb, dc),
                                                                         start=(dc == 0), stop=(dc == 7)),
                              reads=[elkres] + hres(kb), writes=[f"bank{bkh}"])
                P.add("act", lambda e: e.activation(out=kp[:, 128:640], in_=bank[bkm][:, :], func=AF.Copy), reads=[f"bank{bkm}"], writes=[kr + "_m"])
                for hi, kb in enumerate((0, 5)):
                    if kb in kbs:
                        P.add("dve", lambda e, hi=hi, kb=kb: e.tensor_copy(out=kp[:, kb * 128:(kb + 1) * 128], in_=bank[bkh][:, hi * 128:(hi + 1) * 128]),
                              reads=[f"bank{bkh}"], writes=[kr + f"_h{hi}"])
                    else:
                        P.add("dve", lambda e, kb=kb: e.memset(kp[:, kb * 128:(kb + 1) * 128], 0.0), writes=[kr + f"_h{hi}"])

            def k_rope(g):
                kp = kpre[g % 2]
                kr = f"kpre{g % 2}"
                r = g % 2
                bp0 = gbank()
                bp1 = gbank()
                P.add("pe", lambda e: e.matmul(bank[bp0][:, :], lhsT=perm, rhs=kp[:, 0:512], start=True, stop=True),
                      reads=["cst", kr + "_m", kr + "_h0"], writes=[f"bank{bp0}"])
                P.add("pe", lambda e: e.matmul(bank[bp1][:, 0:256], lhsT=perm, rhs=kp[:, 512:768], start=True, stop=True),
                      reads=["cst", kr + "_m", kr + "_h1"], writes=[f"bank{bp1}"])
                P.add("pool", lambda e: e.tensor_tensor(out=rt1[r][:, :], in0=kp[:, :], in1=tab[:, 0, :], op=ALU.mult),
                      reads=[kr + "_m", kr + "_h0", kr + "_h1", "tab"], writes=["rt1_0"])
                P.add("dve", lambda e: e.tensor_tensor(out=rt2[r][:, 0:512], in0=bank[bp0][:, :], in1=tab[:, 1, 0:512], op=ALU.mult),
                      reads=[f"bank{bp0}", "tab"], writes=["rt2_0a"])
                P.add("dve", lambda e: e.tensor_tensor(out=rt2[r][:, 512:768], in0=bank[bp1][:, 0:256], in1=tab[:, 1, 512:768], op=ALU.mult),
                      reads=[f"bank{bp1}", "tab"], writes=["rt2_0b"])
                P.add("pool", lambda e: e.tensor_tensor(out=kT[:, g, :], in0=rt1[r][:, :], in1=rt2[r][:, :], op=ALU.add),
                      reads=["rt1_0", "rt2_0a", "rt2_0b"], writes=[f"kT{g}"])

            for g in range(4):
                k_main(g)
            state["genlist"] = [b for b in range(7) if b not in kbanks.values()]
            k_halo(0)
            for g in range(4):
                if g + 1 < 4:
                    k_halo(g + 1)
                k_rope(g)
            state["genlist"] = list(range(7))
            if dbg and ti == 0:
                P.add("sp", lambda e: e.dma_start(out=dbg_out["d_hT"], in_=hT[:].rearrange("p a b -> p (a b)")), reads=[f"hT{s}" for s in range(8)], dma=True, chan="dbg")
                P.add("sp", lambda e: e.dma_start(out=dbg_out["d_kT"], in_=kT[:].rearrange("p a b -> p (a b)")), reads=[f"kT{g}" for g in range(4)], dma=True, chan="dbg")
                P.add("sp", lambda e: e.dma_start(out=dbg_out["d_ub"], in_=ub[:].rearrange("p a b -> p (a b)")), reads=[f"ub{usl(kb)}_{h}" for kb in kbs for h in range(2)], dma=True, chan="dbg")
                P.add("sp", lambda e: e.dma_start(out=dbg_out["d_va"], in_=vaug[:].rearrange("p a b c -> p (a b c)")),
                      reads=[f"vaug{usl(kb)}_{cp}" for kb in kbs for cp in range(2)] + ["vaug_ones"], dma=True, chan="dbg")

            if limit < 3:
                return
            elpg = [take("pg0"), take("pg1")]
            for g in range(4):
                pt = poolT[g % 2]
                ptres = f"poolT{g % 2}"
                for kc in range(2):
                    c = 2 * g + kc
                    bk = gbank()
                    for ob in range(4):
                        gb = b0 + ob
                        terms = []
                        for rel in (-1, 0, 1):
                            sgb = gb + rel
                            if kind == "prompt" and (sgb < 0 or sgb >= nblk):
                                continue
                            if rel == 0:
                                if gb == 0:
                                    bi = (12 + g) if kind == "prompt" else (20 + g)
                                elif gb == nblk - 1:
                                    bi = (16 + g) if kind == "prompt" else (24 + g)
                                else:
                                    bi = 3 * g + 1
                            else:
                                bi = 3 * g + (0 if rel < 0 else 2)
                            terms.append((ob + 1 + rel, bi))
                        for i, (kb, bi) in enumerate(terms):
                            P.add("pe", lambda e, kb=kb, bi=bi, c=c, ob=ob, i=i, nt=len(terms), bk=bk: e.matmul(
                                bank[bk][:, ob * 128:(ob + 1) * 128], lhsT=ub[:, (b0 + kb) % 6, c * 128:(c + 1) * 128], rhs=bands[:, bi, :],
                                start=(i == 0), stop=(i == nt - 1)),
                                reads=[f"ub{usl(kb)}_{c // 4}", "bands"], writes=[f"bank{bk}"])
                    evac_copy(c, pt[:, kc, :], bank[bk][:, :], [f"bank{bk}"], [ptres + f"_{kc}"])
                gtmp = []
                for oh in range(2):
                    oc = 2 * g + oh
                    bkg = gbank()
                    el_, r_ = elpg[oc // 4]
                    inproj_fm(el_, r_, oc % 4, hmain, main_res, 512, bkg)
                    s_ = sg[0]
                    yt = ytmp[oh]
                    P.add("act", lambda e, s_=s_, bkg=bkg: e.activation(out=s_[:], in_=bank[bkg][:, :], func=AF.Tanh, scale=0.5), reads=[f"bank{bkg}"], writes=["sg0"])
                    P.add("dve", lambda e, s_=s_, bkg=bkg, yt=yt: e.scalar_tensor_tensor(out=yt[:], in0=s_[:], scalar=1.0, in1=bank[bkg][:, :], op0=ALU.add, op1=ALU.mult),
                          reads=[f"bank{bkg}", "sg0"], writes=[f"ytmp{oh}"])
                for oh in range(2):
                    oc = 2 * g + oh
                    bkw = gbank()
                    yt = ytmp[oh]
                    for kc in range(2):
                        P.add("pe", lambda e, kc=kc, oh=oh, g=g, pt=pt, bkw=bkw: e.matmul(bank[bkw][:, :], lhsT=wpg[:, g, kc, oh * 128:(oh + 1) * 128], rhs=pt[:, kc, :],
                                                                                 start=(kc == 0), stop=(kc == 1)),
                              reads=["wpg", ptres + "_0", ptres + "_1"], writes=[f"bank{bkw}"])
                    P.add("dve", lambda e, bkw=bkw, oc=oc, yt=yt: e.scalar_tensor_tensor(out=pbT[:, oc, :], in0=bank[bkw][:, :], scalar=psh[:, oc:oc + 1], in1=yt[:],
                                                                                  op0=ALU.mult, op1=ALU.mult),
                          reads=[f"bank{bkw}", f"ytmp{oh}", "psh"], writes=["pbT", f"pbT{oc}"])
            if dbg and ti == 0:
                P.add("sp", lambda e: e.dma_start(out=dbg_out["d_pbT"], in_=pbT[:].rearrange("p a b -> p (a b)")), reads=[f"pbT{o}" for o in range(8)], dma=True, chan="dbg")

            if limit < 4:
                return

            def qa_items(g):
                box = {}
                qq = qg[g % 2]
                qres = f"qg{g % 2}"
                sa = sag[g % 2]
                sres = f"sag{g % 2}"

                def get_el():
                    if "el" not in box:
                        box["el"] = take(f"qa{g}")
                    return box["el"]

                def q_item(j):
                    el, elres = get_el()
                    bk = gbank()
                    inproj_fm(el, elres, j, hmain, main_res, 512, bk)
                    qp = qpre[j]
                    P.add("act", lambda e: e.activation(out=qp[:], in_=bank[bk][:, :], func=AF.Copy), reads=[f"bank{bk}"], writes=[f"qpre{j}"])
                    bp = 6

                    def tail():
                        P.add("pe", lambda e: e.matmul(bank[bp][:, :], lhsT=perm, rhs=qp[:], start=True, stop=True), reads=["cst", f"qpre{j}"], writes=[f"bank{bp}"])
                        P.add("pool", lambda e: e.tensor_tensor(out=rt1[j][:, 0:512], in0=qp[:], in1=tab[:, 0, 128:640], op=ALU.mult),
                              reads=[f"qpre{j}", "tab"], writes=["rt1_0"])
                        P.add("dve", lambda e: e.tensor_tensor(out=rt2[j][:, 0:512], in0=bank[bp][:, :], in1=tab[:, 1, 128:640], op=ALU.mult),
                              reads=[f"bank{bp}", "tab"], writes=["rt2_0a"])
                        P.add("pool", lambda e: e.tensor_tensor(out=qq[:, j, :], in0=rt1[j][:, 0:512], in1=rt2[j][:, 0:512], op=ALU.add),
                              reads=["rt1_0", "rt2_0a"], writes=[qres + f"_{j}"])
                    return tail

                def ag_item(j):
                    el, elres = get_el()
                    bk = gbank()
                    inproj_fm(el, elres, 2 + j, hmain, main_res, 512, bk)
                    P.add("act", lambda e: e.activation(out=sa[:, j, :], in_=bank[bk][:, :], func=AF.Tanh, scale=0.5), reads=[f"bank{bk}"], writes=[sres + f"_{j}"])
                    P.add("dve", lambda e: e.scalar_tensor_tensor(out=sa[:, j, :], in0=sa[:, j, :], scalar=1.0, in1=bank[bk][:, :], op0=ALU.add, op1=ALU.mult),
                          reads=[f"bank{bk}", sres + f"_{j}"], writes=[sres + f"_{j}"])

                return [lambda: q_item(0), lambda: q_item(1), lambda: ag_item(0), lambda: ag_item(1)]

            def barrier_mm():
                bk = gbank()
                P.add("pe", lambda e: e.matmul(bank[bk][:, 0:2], lhsT=ident, rhs=cst[:, 0, 0:2], start=True, stop=True), reads=["cst"], writes=[f"bank{bk}"])

            def chunks_of(qb):
                gb = b0 + qb
                return [c for c in range(3) if not (kind == "prompt" and (gb - 1 + c < 0 or gb - 1 + c >= nblk))]

            def s_part(idx, g, qb, par):
                qq = qg[g % 2]
                qres = f"qg{g % 2}"
                sset = (0, 1, 2) if idx % 2 == 0 else (3, 4, 5)
                rows = slice(0, 64) if par == 0 else slice(64, 128)
                cols = slice(par * 256, (par + 1) * 256)
                for c in chunks_of(qb):
                    kcols = slice((qb + c) * 128, (qb + c + 1) * 128)
                    bk = sset[c]
                    P.add("pe", lambda e, bk=bk, kcols=kcols: e.matmul(bank[bk][:, cols].rearrange("p (j q) -> p j q", j=2), lhsT=kT[rows, g, kcols],
                                                                      rhs=qq[rows, :, qb * 128:(qb + 1) * 128], start=True, stop=True),
                          reads=[f"kT{g}", qres + "_0", qres + "_1"], writes=[f"bank{bk}"])

            def softmax_part(idx, g, qb):
                gb = b0 + qb
                sset = (0, 1, 2) if idx % 2 == 0 else (3, 4, 5)
                pt_ = PT[idx % 3]
                ptres = f"PT{idx % 3}"
                for c in chunks_of(qb):
                    bk = sset[c]
                    P.add("act", lambda e, c=c, bk=bk: e.activation(out=pt_[:, c, :], in_=bank[bk][:, :], func=AF.Exp, scale=HD ** -0.5),
                          reads=[f"bank{bk}"], writes=[ptres + f"_{c}"])
                    mi = None
                    if c == 0:
                        mi = 2 if (kind == "sample" and gb == 0) else 0
                    elif c == 2:
                        mi = 3 if (kind == "sample" and gb == nblk - 1) else 1
                    if mi is not None:
                        P.add("dve", lambda e, c=c, mi=mi: e.tensor_tensor(out=pt_[:, c, :], in0=pt_[:, c, :], in1=masks[:, mi, :], op=ALU.mult),
                              reads=[ptres + f"_{c}", "masks"], writes=[ptres + f"_{c}"])

            def pv_part(idx, g, qb):
                pt_ = PT[idx % 3]
                ptres = f"PT{idx % 3}"
                sa = sag[g % 2]
                sres = f"sag{g % 2}"
                chunks = chunks_of(qb)
                for par in range(2):
                    cols = slice(par * 256, (par + 1) * 256)
                    for i, c in enumerate(chunks):
                        kb = qb + c
                        lw = vaug[:, usl(kb), g, 0:128] if par == 0 else vaug[:, usl(kb), g, 64:192]
                        P.add("pe", lambda e, lw=lw, c=c, cols=cols, i=i, n=len(chunks): e.matmul(bank[6][:, cols], lhsT=lw, rhs=pt_[:, c, cols], start=(i == 0), stop=(i == n - 1)),
                              reads=[f"vaug{usl(kb)}_0", f"vaug{usl(kb)}_1", "vaug_ones", ptres + f"_{c}"], writes=["bank6"])
                pv = pvs[idx % 2]
                pres = f"pvs{idx % 2}"
                ds = rden[idx % 2]
                rd = rden[idx % 2]
                ot = otmp[idx % 2]
                P.add("act", lambda e: e.activation(out=pv[:], in_=bank[6][:, :], func=AF.Copy), reads=["bank6"], writes=[pres])
                P.add("dve", lambda e: e.tensor_tensor(out=ds[0:64, :], in0=pv[64:128, 0:256], in1=sinkT[64:128, g, :], op=ALU.add),
                      reads=[pres, "sinkT"], writes=[f"dsum{idx % 2}e"])
                P.add("dve", lambda e: e.tensor_tensor(out=ds[64:128, :], in0=pv[0:64, 256:512], in1=sinkT[0:64, g, :], op=ALU.add),
                      reads=[pres, "sinkT"], writes=[f"dsum{idx % 2}o"])
                P.add("dve", lambda e: e.reciprocal(out=rd[:, :], in_=ds[:, :]), reads=[f"dsum{idx % 2}e", f"dsum{idx % 2}o"], writes=[f"rden{idx % 2}"])
                P.add("dve", lambda e: e.tensor_tensor(out=ot[0:64, :], in0=pv[0:64, 0:256], in1=rd[0:64, :], op=ALU.mult),
                      reads=[pres, f"rden{idx % 2}"], writes=[f"otmp{idx % 2}e"])
                P.add("dve", lambda e: e.tensor_tensor(out=ot[64:128, :], in0=pv[64:128, 256:512], in1=rd[64:128, :], op=ALU.mult),
                      reads=[pres, f"rden{idx % 2}"], writes=[f"otmp{idx % 2}o"])
                P.add("pool", lambda e: e.tensor_tensor(out=abT[:, 2 * g:2 * g + 2, qb * 128:(qb + 1) * 128],
                                                        in0=ot[:, :].rearrange("p (j q) -> p j q", j=2),
                                                        in1=sa[:, :, qb * 128:(qb + 1) * 128], op=ALU.mult),
                      reads=[f"otmp{idx % 2}e", f"otmp{idx % 2}o", sres + "_0", sres + "_1"], writes=["abT", f"abT{g}_{qb}"])

            state["genlist"] = [7]
            def run_item(it):
                t_ = it()
                if t_ is not None:
                    t_()

            for it in qa_items(0):
                run_item(it)
            seq = [(g, qb) for g in range(4) for qb in range(4)]
            fillers = []
            for idx, (g, qb) in enumerate(seq):
                if qb == 0:
                    for it in fillers:
                        run_item(it)
                    fillers = qa_items(g + 1) if g < 3 else []
                s_part(idx, g, qb, 0)
                if idx > 1:
                    pv_part(idx - 2, *seq[idx - 2])
                tail_ = None
                if fillers:
                    tail_ = fillers.pop(0)()
                elif idx <= 1:
                    barrier_mm()
                s_part(idx, g, qb, 1)
                if tail_ is not None:
                    tail_()
                softmax_part(idx, g, qb)
            pv_part(len(seq) - 2, *seq[-2])
            pv_part(len(seq) - 1, *seq[-1])
            state["genlist"] = list(range(7))
            if dbg and ti == 0:
                P.add("sp", lambda e: e.dma_start(out=dbg_out["d_abT"], in_=abT[:].rearrange("p a b -> p (a b)")),
                      reads=[f"abT{g}_{qb}" for g in range(4) for qb in range(4)], dma=True, chan="dbg")

            if limit < 5:
                return
            pb_all = [f"pbT{o}" for o in range(8)]
            ab_all = [f"abT{g}_{qb}" for g in range(4) for qb in range(4)]
            for op_ in range(4):
                elm, rm = take(f"mg{op_}")
                elp, rp = take(f"pr{op_}")
                for j in range(2):
                    o = 2 * op_ + j
                    gi = 0
                    bk1 = gbank()
                    inproj_fm(elm, rm, j, hmain, main_res, 512, bk1)
                    P.add("act", lambda e, gi=gi, bk1=bk1: e.activation(out=gsig[gi][:], in_=bank[bk1][:, :], func=AF.Tanh, scale=0.5), reads=[f"bank{bk1}"], writes=[f"gsig{gi}"])
                    bk2 = gbank()
                    inproj_fm(elm, rm, 2 + j, hmain, main_res, 512, bk2)
                    P.add("act", lambda e, gi=gi, bk2=bk2: e.activation(out=gsig[gi + 1][:], in_=bank[bk2][:, :], func=AF.Tanh, scale=0.5), reads=[f"bank{bk2}"], writes=[f"gsig{gi + 1}"])
                    bk3 = gbank()
                    for k in range(8):
                        P.add("pe", lambda e, k=k, j=j, bk3=bk3, elp=elp: e.matmul(bank[bk3][:, :], lhsT=elp[:, k, j * 128:(j + 1) * 128], rhs=pbT[:, k, :], start=(k == 0), stop=(k == 7)),
                              reads=[rp] + pb_all, writes=[f"bank{bk3}"])
                    bk4 = gbank()
                    for k in range(8):
                        P.add("pe", lambda e, k=k, j=j, bk4=bk4, elp=elp: e.matmul(bank[bk4][:, :], lhsT=elp[:, k, 256 + j * 128:256 + (j + 1) * 128], rhs=abT[:, k, :], start=(k == 0), stop=(k == 7)),
                              reads=[rp] + ab_all, writes=[f"bank{bk4}"])
                    m1, m2 = mt1[o % 2], mt2[o % 2]
                    P.add("dve", lambda e, m1=m1, gi=gi, bk3=bk3: e.scalar_tensor_tensor(out=m1[:], in0=gsig[gi][:], scalar=1.0, in1=bank[bk3][:, :], op0=ALU.add, op1=ALU.mult),
                          reads=[f"bank{bk3}", f"gsig{gi}"], writes=["mt1_0"])
                    P.add("dve", lambda e, m2=m2, gi=gi, bk4=bk4: e.scalar_tensor_tensor(out=m2[:], in0=gsig[gi + 1][:], scalar=1.0, in1=bank[bk4][:, :], op0=ALU.add, op1=ALU.mult),
                          reads=[f"bank{bk4}", f"gsig{gi + 1}"], writes=["mt2_0"])
                    P.add("pool", lambda e, m1=m1, m2=m2, o=o: e.tensor_tensor(out=mT[:, o, :], in0=m1[:], in1=m2[:], op=ALU.add),
                          reads=["mt1_0", "mt2_0"], writes=["mT", f"mT{o}"])
                hoist_event(False)
            if dbg and ti == 0:
                P.add("sp", lambda e: e.dma_start(out=dbg_out["d_mT"], in_=mT[:].rearrange("p a b -> p (a b)")), reads=[f"mT{o}" for o in range(8)], dma=True, chan="dbg")

            if limit < 6:
                return
            elo = [take("wo0"), take("wo1")]
            m_all = [f"mT{o}" for o in range(8)]
            for ob in range(4):
                t0 = tl["tok0"] + ob * BLK
                i = state.setdefault("yb", 0)
                state["yb"] += 1
                yb, yres = ybuf[i % 2], f"ybuf{i % 2}"
                c = 32 + (i % 4) * 8
                P.add("sp", lambda e, yb=yb, t0=t0: e.dma_start(out=yb[:], in_=xm[t0:t0 + BLK, :]), writes=[yres], dma=True, chan="ld" + yres)
                bks = [gbank(), gbank()]
                for half in range(2):
                    el_, r_ = elo[half]
                    for k in range(8):
                        P.add("pe", lambda e, k=k, half=half, ob=ob, el_=el_, bks=bks: e.matmul(bank[bks[half]][:, :], lhsT=mT[:, k, ob * 128:(ob + 1) * 128], rhs=el_[:, k, :],
                                                                                     start=(k == 0), stop=(k == 7)),
                              reads=[r_] + m_all, writes=[f"bank{bks[half]}"])
                    P.add("act", lambda e, half=half, bks=bks, c=c: e.activation(out=PT[0][:, half, :], in_=bank[bks[half]][:, :], func=AF.Square,
                                                                              accum_out=stat[:, c + half:c + half + 1]),
                          reads=[f"bank{bks[half]}"], writes=[f"PT0_{half}", f"st{c + half}"])
                P.add("dve", lambda e, c=c: e.tensor_tensor(out=stat[:, c + 2:c + 3], in0=stat[:, c:c + 1], in1=stat[:, c + 1:c + 2], op=ALU.add),
                      reads=[f"st{c}", f"st{c + 1}"], writes=[f"st{c + 2}"])
                P.add("act", lambda e, c=c: e.activation(out=stat[:, c + 3:c + 4], in_=stat[:, c + 2:c + 3], func=AF.Sqrt, bias=epsT[:], scale=1.0 / D),
                      reads=[f"st{c + 2}", "epsT"], writes=[f"st{c + 3}"])
                P.add("dve", lambda e, c=c: e.reciprocal(out=stat[:, c + 4:c + 5], in_=stat[:, c + 3:c + 4]), reads=[f"st{c + 3}"], writes=[f"st{c + 4}"])
                for half in range(2):
                    yt = ytmp[half]
                    P.add("dve", lambda e, half=half, yt=yt, bks=bks, c=c: e.scalar_tensor_tensor(out=yt[:], in0=bank[bks[half]][:, :], scalar=stat[:, c + 4:c + 5],
                                                                                        in1=gpost[:, half * 512:(half + 1) * 512], op0=ALU.mult, op1=ALU.mult),
                          reads=[f"bank{bks[half]}", f"st{c + 4}", "gpost"], writes=[f"ytmp{half}"])
                    P.add("pool", lambda e, half=half, yt=yt, yb=yb: e.tensor_tensor(out=yb[:, half * 512:(half + 1) * 512], in0=yt[:], in1=yb[:, half * 512:(half + 1) * 512], op=ALU.add),
                          reads=[f"ytmp{half}", yres], writes=[yres])
                P.add("pool", lambda e, yb=yb, t0=t0: e.dma_start(out=ym[t0:t0 + BLK, :], in_=yb[:]), reads=[yres], dma=True, chan="st" + yres)
                hoist_event(True)
            while hq["pend"] is not None or hq["L"]:
                hoist_event(True)


        for ti, tl in enumerate(tiles):
            if limit < 1 or (limit < 6 and ti > 0):
                break
            tile_body(ti, tl)

        fin = sb("fin", [128, 8], F32)
        P.add("act", lambda e: e.activation(out=fin[:, 0:1], in_=epsT[:], func=AF.Copy), reads=["epsT"], writes=["fin_act"])
        P.add("dve", lambda e: e.memset(fin[:, 1:2], 0.0), writes=["fin_dve"])
        P.add("pool", lambda e: e.memset(fin[:, 2:3], 0.0), writes=["fin_pool"])
        P.add("pe", lambda e: e.matmul(bank[0][:, 0:1], lhsT=sel[0:1, 0, :], rhs=sel[0:1, 0, 0:1], start=True, stop=True), reads=["sel"], writes=["bank0"])
        P.add("dve", lambda e: e.tensor_copy(out=fin[:, 3:4], in_=bank[0][:, 0:1]), reads=["bank0"], writes=["fin_pe"])
        P.add("sp", lambda e: e.dma_start(out=fin[:, 5:6], in_=fin[:, 4:5]), reads=["fin_act", "fin_dve", "fin_pool", "fin_pe"], writes=["fin_sp"], dma=True, chan="fin")

        keys = P.finalize()
        sems = {k: es.enter_context(nc.semaphore(k)) for k in keys}
        block = es.enter_context(nc.Block())
        out_chans = [k for k in keys if k.startswith("dma_")]

        def emit_eng(ename):
            def body(eng):
                for o in P.ops[ename]:
                    for k, v in o.waits:
                        eng.wait_ge(sems[k], v)
                    if o.dma:
                        insts = o.fn(eng)
                        if not isinstance(insts, (list, tuple)):
                            insts = [insts]
                        assert len(insts) == o.ndma
                        for i_ in insts:
                            i_.then_inc(sems[o.sig[0]], 16)
                    else:
                        inst = o.fn(eng)
                        if o.sig is not None:
                            inst.then_inc(sems[o.sig[0]], 1)
                if ename == "sp":
                    for k in out_chans:
                        eng.wait_ge(sems[k], P.final_counts[k])
            return body
        block.sync(emit_eng("sp"))
        block.scalar(emit_eng("act"))
        block.vector(emit_eng("dve"))
        block.gpsimd(emit_eng("pool"))
        block.tensor(emit_eng("pe"))
    nc._prog_stats = {e: len(P.ops[e]) for e in P.ENGS}
    return nc


def rope_tables(positions):
    half = HD // 2
    inv = (np.float32(THETA) ** (-(np.arange(half, dtype=np.float32) / np.float32(half)))).astype(np.float32)
    ang = positions.astype(np.float32)[None, :] * inv[:, None]
    cos = np.cos(ang).astype(np.float32)
    sin = np.sin(ang).astype(np.float32)
    p = np.arange(128)
    ct = cos[p % 32]
    sgn = np.where((p % 64) < 32, -1.0, 1.0).astype(np.float32)
    st = sin[p % 32] * sgn[:, None]
    return np.stack([ct, st], 0)


def band_mat(g, rel, mode):
    w = POOL_WINDOWS[g]
    B = 1024
    S = 1 << 30
    if mode == "first":
        B = 0
    if mode == "last":
        S = B + 128
    s = B + rel * 128 + np.arange(128)[:, None]
    t = B + np.arange(128)[None, :]
    lo = np.maximum(t - w // 2, 0)
    hi = np.minimum(t - w // 2 + w, S)
    inr = (s >= lo) & (s < hi)
    val = inr / (hi - lo).astype(np.float64) - (s == t)
    return val.astype(np.float32)


def make_consts(valid_left, valid_right):
    bands = np.zeros((128, 28, 128), np.float32)
    for g in range(4):
        bands[:, 3 * g + 0] = band_mat(g, -1, "int")
        bands[:, 3 * g + 1] = band_mat(g, 0, "int")
        bands[:, 3 * g + 2] = band_mat(g, 1, "int")
        bands[:, 12 + g] = band_mat(g, 0, "first")
        bands[:, 16 + g] = band_mat(g, 0, "last")
        bands[:, 20 + g] = band_mat(g, 0, "int" if valid_left else "first")
        bands[:, 24 + g] = band_mat(g, 0, "int" if valid_right else "last")
    j = np.arange(128)[:, None]
    i = np.arange(128)[None, :]
    mp = (j >= i).astype(np.float32)
    mn = (j <= i).astype(np.float32)
    masks = np.zeros((128, 4, 512), np.float32)
    masks[:, 0] = np.tile(mp, (1, 4))
    masks[:, 1] = np.tile(mn, (1, 4))
    masks[:, 2] = np.tile(mp, (1, 4)) * (1.0 if valid_left else 0.0)
    masks[:, 3] = np.tile(mn, (1, 4)) * (1.0 if valid_right else 0.0)
    cst = np.zeros((128, 2, 128), np.float32)
    cst[:, 0] = np.eye(128)
    k = np.arange(128)
    partner = (k // 64) * 64 + ((k % 64) + 32) % 64
    cst[partner, 1, k] = 1.0
    sel = np.zeros((1, 2, 128), np.float32)
    sel[0, 0, 64:] = 4.0
    sel[0, 1, :64] = 4.0
    bf = ml_dtypes.bfloat16
    return (bands.reshape(128, -1).astype(bf), masks.reshape(128, -1).astype(bf),
            cst.reshape(128, -1).astype(bf), sel.reshape(1, -1).astype(bf))


def sink_layout(attn_sink):
    return np.asarray(attn_sink, np.float32).reshape(1, NH)


def core_inputs(segs_x, halos, seg_kinds, pos0s, valid_left, valid_right, shared):
    xm = np.ascontiguousarray(np.concatenate(segs_x, 0))
    tabs = []
    for x, p0 in zip(segs_x, pos0s):
        n = x.shape[0]
        tabs.append(rope_tables(np.arange(p0 - 128, p0 + n + 128)))
    tabs = np.ascontiguousarray(np.concatenate(tabs, 2))
    bands, masks, cst, sel = make_consts(valid_left, valid_right)
    d = dict(shared)
    d.update(xm=xm, xh=np.ascontiguousarray(halos), tabs=tabs, bands=bands, masks=masks, cst=cst, sel=sel)
    return d


def shared_inputs(norm_pre, w_in, w_pool_group, pool_scale, w_pool_proj, attn_sink, w_attn_proj, w_out, norm_post):
    f = lambda a: np.ascontiguousarray(np.asarray(a, np.float32))
    return dict(
        w_in=f(w_in[0]), w_pg=f(w_pool_group[0]), w_pp=f(w_pool_proj[0]), w_ap=f(w_attn_proj[0]), w_out=f(w_out[0]),
        gpre_b=f(np.tile(np.asarray(norm_pre[0]).reshape(1, D), (128, 1))), pscale_c=f(np.asarray(pool_scale[0]).reshape(8, 128).T),
        gpost_b=f(np.tile(np.asarray(norm_post[0]).reshape(1, D), (128, 1))), sink_rows=f(sink_layout(attn_sink[0])),
    )


_NC_CACHE = {}


def kernel(x_prompt, x_sample, norm_pre, w_in, w_pool_group, pool_scale, w_pool_proj,
           attn_sink, w_attn_proj, w_out, norm_post):
    x_prompt = np.asarray(x_prompt, np.float32)
    x_sample = np.asarray(x_sample, np.float32)
    shared = shared_inputs(norm_pre, w_in, w_pool_group, pool_scale, w_pool_proj, attn_sink, w_attn_proj, w_out, norm_post)
    segs = [("prompt", 2048), ("prompt", 2048), ("sample", 4096)]
    if "nc" not in _NC_CACHE:
        _NC_CACHE["nc"] = build_program(segs)
    nc = _NC_CACHE["nc"]
    in_maps = []
    for c in range(8):
        sb_, hf = c // 2, c % 2
        sx = [x_prompt[2 * c], x_prompt[2 * c + 1], x_sample[sb_, hf * 4096:(hf + 1) * 4096]]
        halos = np.zeros((2, 128, D), np.float32)
        if hf == 1:
            halos[0] = x_sample[sb_, 4096 - 128:4096]
        else:
            halos[1] = x_sample[sb_, 4096:4096 + 128]
        in_maps.append(core_inputs(sx, halos, [k for k, _ in segs], [0, 0, hf * 4096], hf == 1, hf == 0, shared))
    res = run_bass_kernel_spmd(nc, in_maps, core_ids=list(range(8)))
    y_prompt = np.empty_like(x_prompt)
    y_sample = np.empty_like(x_sample)
    for c in range(8):
        ymc = res.results[c]["ym"]
        y_prompt[2 * c] = ymc[0:2048]
        y_prompt[2 * c + 1] = ymc[2048:4096]
        y_sample[c // 2, (c % 2) * 4096:(c % 2 + 1) * 4096] = ymc[4096:8192]
    return (y_prompt, y_sample)
```

```python
import numpy as np
import ml_dtypes
from contextlib import ExitStack
import concourse.bass as bass
import concourse.mybir as mybir
from concourse.bass_utils import run_bass_kernel_spmd

F32 = mybir.dt.float32
BF16 = mybir.dt.bfloat16
AF = mybir.ActivationFunctionType
ALU = mybir.AluOpType

D = 1024
NH, NKV, HD = 16, 4, 64
TILE = 512
BLK = 128
POOL_WINDOWS = (2, 4, 8, 16)
EPS = 1e-6
THETA = 10000.0
IN_W = 6656
C_U, C_PG, C_Q, C_K, C_V, C_AG, C_GP, C_GA = 0, 1024, 2048, 3072, 3328, 3584, 4608, 5632
NRING = 4


class Op:
    __slots__ = ("eng", "fn", "reads", "writes", "dma", "chan", "ndma", "idx", "deps", "sig", "waits", "name")


class Prog:
    ENGS = ("sp", "act", "dve", "pool", "pe")

    def __init__(self, same_eng_dist=3):
        self.ops = {e: [] for e in self.ENGS}
        self.last_w = {}
        self.readers = {}
        self.same_eng_dist = same_eng_dist
        self.all = []

    def add(self, eng, fn, reads=(), writes=(), dma=False, chan=None, ndma=1, name=""):
        o = Op()
        o.eng = eng; o.fn = fn; o.reads = tuple(reads); o.writes = tuple(writes)
        o.dma = dma; o.chan = chan; o.ndma = ndma; o.name = name
        o.idx = len(self.ops[eng]); o.deps = []; o.sig = None; o.waits = []
        deps = {}
        for r in o.reads:
            w = self.last_w.get(r)
            if w is not None:
                deps[id(w)] = (w, True)
            if r.startswith("bank") or r == "pTb":
                for rd in self.readers.get(r, ()):
                    if id(rd) not in deps and rd.eng != eng:
                        deps[id(rd)] = (rd, False)
        for r in o.writes:
            w = self.last_w.get(r)
            if w is not None and id(w) not in deps:
                deps[id(w)] = (w, False)
            for rd in self.readers.get(r, ()):
                if id(rd) not in deps:
                    deps[id(rd)] = (rd, False)
        for d, israw in deps.values():
            if d is o:
                continue
            if (not d.dma) and (not o.dma) and d.eng == o.eng:
                if o.eng == "pe":
                    continue
                if not israw:
                    continue
                if o.idx - d.idx >= self.same_eng_dist:
                    continue
            o.deps.append(d)
        for r in o.reads:
            self.readers.setdefault(r, []).append(o)
        for r in o.writes:
            self.last_w[r] = o
            self.readers[r] = []
        self.ops[eng].append(o)
        self.all.append(o)
        return o

    def finalize(self):
        need = set()
        for o in self.all:
            for d in o.deps:
                need.add(id(d))
        cnt = {}
        for e in self.ENGS:
            for o in self.ops[e]:
                if o.dma:
                    key = "dma_" + o.chan
                    cnt[key] = cnt.get(key, 0) + 16 * o.ndma
                    o.sig = (key, cnt[key])
                elif id(o) in need:
                    key = "eng_" + e
                    cnt[key] = cnt.get(key, 0) + 1
                    o.sig = (key, cnt[key])
        self.final_counts = cnt
        for e in self.ENGS:
            seen = {}
            for o in self.ops[e]:
                w = {}
                for d in o.deps:
                    k, v = d.sig
                    if seen.get(k, 0) >= v:
                        continue
                    w[k] = max(w.get(k, 0), v)
                for k, v in w.items():
                    seen[k] = v
                o.waits = sorted(w.items())
        return sorted(cnt.keys())


def stream_elements():
    el = []
    el.append(("u0", [("win", C_U, 512)]))
    el.append(("u1", [("win", C_U + 512, 512)]))
    el.append(("vv", [("win", C_V, 256), ("win", C_V, 256)]))
    el.append(("kk", [("win", C_K + 64 * (i // 2), 64) for i in range(8)]))
    el.append(("pg0", [("win", C_PG, 512)]))
    el.append(("pg1", [("win", C_PG + 512, 512)]))
    for g in range(4):
        el.append((f"qa{g}", [("win", C_Q + 256 * g, 256), ("win", C_AG + 256 * g, 256)]))
    for op in range(4):
        el.append((f"mg{op}", [("win", C_GP + 256 * op, 256), ("win", C_GA + 256 * op, 256)]))
        el.append((f"pr{op}", [("wpp", 256 * op, 256), ("wap", 256 * op, 256)]))
    el.append(("wo0", [("wout", 0, 512)]))
    el.append(("wo1", [("wout", 512, 512)]))
    return el


ELEMS = stream_elements()
ELEM_IDX = {n: i for i, (n, _) in enumerate(ELEMS)}
NEL = len(ELEMS)


def build_program(segs, dbg=False, limit=99):
    nc = bass.Bass("TRN2", target_bir_lowering=False)
    ntok = sum(n for _, n in segs)
    nsamp = sum(1 for k, _ in segs if k == "sample")
    ntab = sum(n + 256 for _, n in segs)

    def din(name, shape, dt=F32):
        return nc.dram_tensor(name, list(shape), dt, kind="ExternalInput").ap()

    xm = din("xm", [ntok, D])
    xh = din("xh", [max(nsamp, 1) * 2, BLK, D])
    w_in = din("w_in", [D, IN_W])
    w_pg = din("w_pg", [4, 256, 256])
    w_pp = din("w_pp", [D, D])
    w_ap = din("w_ap", [D, D])
    w_out = din("w_out", [D, D])
    gpre_b = din("gpre_b", [128, D])
    pscale_c = din("pscale_c", [128, 8])
    gpost_b = din("gpost_b", [128, D])
    sink_rows = din("sink_rows", [1, 16])
    tabs = din("tabs", [2, 128, ntab])
    bands_d = din("bands", [128, 28 * 128], BF16)
    masks_d = din("masks", [128, 4 * 512], BF16)
    cst_d = din("cst", [128, 2 * 128], BF16)
    sel_d = din("sel", [1, 2 * 128], BF16)
    ym = nc.dram_tensor("ym", [ntok, D], F32, kind="ExternalOutput").ap()
    wsc = nc.dram_tensor("wsc", [NEL, 128, 8 * 512], BF16, kind="Internal").ap()
    dbg_out = {}
    if dbg:
        for nm, shp, dt in (("d_hT", [128, 8 * 1024], BF16), ("d_kT", [128, 4 * 768], BF16),
                            ("d_ub", [128, 6 * 1024], BF16), ("d_va", [128, 6 * 4 * 192], BF16),
                            ("d_pbT", [128, 8 * 512], BF16), ("d_abT", [128, 8 * 512], BF16),
                            ("d_mT", [128, 8 * 512], BF16)):
            dbg_out[nm] = nc.dram_tensor(nm, shp, dt, kind="ExternalOutput").ap()

    wsrc = {"win": w_in, "wpp": w_pp, "wap": w_ap, "wout": w_out}
    P = Prog()
    es = ExitStack()
    with es:
        def sb(name, shape, dt):
            return es.enter_context(nc.sbuf_tensor("s_" + name, list(shape), dt))

        def ps(name, shape, dt):
            return es.enter_context(nc.psum_tensor("p_" + name, list(shape), dt))

        ring = [sb(f"ring{i}", [128, 8, 512], BF16) for i in range(NRING)]
        wpg = sb("wpg", [128, 4, 2, 256], BF16)
        xbuf = [sb(f"xbuf{i}", [128, D], F32) for i in range(2)]
        ybuf = [sb(f"ybuf{i}", [128, D], F32) for i in range(2)]
        xs = [sb(f"xs{i}", [128, D], BF16) for i in range(2)]
        hT = sb("hT", [128, 8, 8 * 128], BF16)
        kpre = [sb(f"kpre{i}", [128, 768], BF16) for i in range(2)]
        kT = sb("kT", [128, 4, 768], BF16)
        vaug = sb("vaug", [128, 6, 4, 192], BF16)
        ub = sb("ub", [128, 6, D], BF16)
        poolT = [sb(f"poolT{i}", [128, 2, 512], BF16) for i in range(2)]
        sg = [sb("sg0", [128, 512], BF16)] * 2
        pbT = sb("pbT", [128, 8, 512], BF16)
        abT = sb("abT", [128, 8, 512], BF16)
        mT = sb("mT", [128, 8, 512], BF16)
        qpre = [sb(f"qpre{i}", [128, 512], BF16) for i in range(2)]
        qg = [sb(f"qg{i}", [128, 2, 512], BF16) for i in range(2)]
        sag = [sb(f"sag{i}", [128, 2, 512], BF16) for i in range(2)]
        rt1 = [sb("rt1_0", [128, 768], F32)] * 2
        rt2 = [sb("rt2_0", [128, 768], F32)] * 2
        tab = sb("tab", [128, 2, 768], F32)
        PT = [sb(f"PT{i}", [128, 3, 512], BF16) for i in range(3)]
        rden = [sb(f"rden{i}", [128, 256], F32) for i in range(2)]
        otmp = [sb(f"otmp{i}", [128, 256], F32) for i in range(2)]
        gsig = [sb(f"gsig{i}", [128, 512], F32) for i in range(2)] * 2
        mt1 = [sb("mt1_0", [128, 512], F32)] * 2
        mt2 = [sb("mt2_0", [128, 512], F32)] * 2
        ytmp = [sb(f"ytmp{i}", [128, 512], F32) for i in range(2)]
        stat = sb("stat", [128, 64], F32)
        gpre = sb("gpre", [128, D], F32)
        pscale = sb("pscale", [128, 8], F32)
        psh = sb("psh", [128, 8], F32)
        gpost = sb("gpost", [128, D], F32)
        sinkf = sb("sinkf", [1, 16], F32)
        sinkb = sb("sinkb", [1, 16], BF16)
        sinkT = sb("sinkT", [128, 4, 256], F32)
        pvs = [sb(f"pvs{i}", [128, 512], F32) for i in range(2)]
        bands = sb("bands", [128, 28, 128], BF16)
        masks = sb("masks", [128, 4, 512], BF16)
        cst = sb("cst", [128, 2, 128], BF16)
        sel = sb("sel", [1, 2, 128], BF16)
        epsT = sb("epsT", [128, 1], F32)
        bank = [ps(f"bank{i}", [128, 512], F32) for i in range(8)]
        pTb = bank[7][:].bitcast(BF16).rearrange("p (c t) -> p c t", c=8)

        ident = cst[:, 0, :]
        perm = cst[:, 1, :]

        state = {"gen": 0, "genlist": list(range(7)), "rr": 0, "stream": 0, "n": 0}

        def gbank():
            lst = state["genlist"]
            b = lst[state["gen"] % len(lst)]
            state["gen"] += 1
            return b

        def ew_eng():
            e = ("dve", "pool", "act")[state["rr"] % 3]
            state["rr"] += 1
            return e

        def uniq(p):
            state["n"] += 1
            return f"{p}{state['n']}"

        def ld(dst_ap, src_ap, res, chan):
            P.add("sp", lambda e: e.dma_start(out=dst_ap, in_=src_ap), writes=[res], dma=True, chan=chan)

        ld(gpre[:], gpre_b, "gpre", "c0")
        ld(pscale[:], pscale_c, "pscale", "c1")
        ld(gpost[:], gpost_b, "gpost", "c2")
        ld(sinkf[:], sink_rows, "sinkf", "c3")
        ld(bands[:].rearrange("p a b -> p (a b)"), bands_d, "bands", "c4")
        ld(masks[:].rearrange("p a b -> p (a b)"), masks_d, "masks", "c5")
        ld(cst[:].rearrange("p a b -> p (a b)"), cst_d, "cst", "c6")
        ld(sel[:].rearrange("p a b -> p (a b)"), sel_d, "sel", "c7")
        P.add("act", lambda e: e.activation(out=sinkb[:], in_=sinkf[:], func=AF.Exp),
              reads=["sinkf"], writes=["sinkb"])
        for g in range(4):
            for par in range(2):
                sb0 = sinkb[0:1, 4 * g + par:4 * g + par + 1]
                srhs = bass.AP(sinkb, sb0.offset, [[sb0.ap[0][0], 1], [2, 2], [0, 128]])
                P.add("pe", lambda e, g=g, par=par, srhs=srhs: e.matmul(bank[g // 2][:, (g % 2) * 256:(g % 2 + 1) * 256].rearrange("p (j q) -> p j q", j=2),
                                                                     lhsT=sel[0:1, par, :], rhs=srhs, start=(par == 0), stop=(par == 1)),
                      reads=["sel", "sinkb"], writes=[f"bank{g // 2}"])
        for h2 in range(2):
            P.add("dve", lambda e, h2=h2: e.tensor_copy(out=sinkT[:, 2 * h2:2 * h2 + 2, :].rearrange("p a b -> p (a b)"), in_=bank[h2][:, :]),
                  reads=[f"bank{h2}"], writes=["sinkT"])
        P.add("dve", lambda e: e.memset(epsT[:], EPS), writes=["epsT"])
        P.add("dve", lambda e: e.tensor_scalar(out=psh[:], in0=pscale[:], scalar1=0.25, scalar2=None, op0=ALU.mult), reads=["pscale"], writes=["psh"])
        P.add("pool", lambda e: e.memset(vaug[:, :, :, 64:128], 4.0), writes=["vaug_ones"])

        stg32 = [(xbuf[0], "xbuf0"), (xbuf[1], "xbuf1"), (ybuf[0], "ybuf0"), (ybuf[1], "ybuf1")]
        for half in range(2):
            buf, res = stg32[half]
            src = w_pg[2 * half:2 * half + 2].rearrange("g (kc p) d -> p g kc d", p=128)
            dst = buf[:].rearrange("p (g kc d) -> p g kc d", g=2, kc=2)
            P.add("sp", lambda e, dst=dst, src=src: e.dma_start(out=dst, in_=src), writes=[res], dma=True, chan="pl" + res)
            P.add("dve", lambda e, buf=buf, half=half: e.tensor_copy(
                out=wpg[:, 2 * half:2 * half + 2, :, :].rearrange("p g kc d -> p (g kc d)"), in_=buf[:]),
                reads=[res], writes=["wpg"])
        for ei, (ename, pieces) in enumerate(ELEMS):
            dview = wsc[ei].rearrange("p (dc c) -> p dc c", dc=8)
            col = 0
            dmas = []
            for (src, c0, ncol) in pieces:
                sap = wsrc[src].rearrange("(dc p) c -> p dc c", p=128)[:, :, c0:c0 + ncol]
                dmas.append((dview[:, :, col:col + ncol], sap))
                col += ncol

            def fn(e, dmas=dmas):
                return [e.dma_start(out=d, in_=s_) for d, s_ in dmas]
            P.add("pool", fn, writes=[f"wsc{ei}"], dma=True, chan=f"pw{ei}", ndma=len(dmas))

        stream_order = []

        def issue_stream(k):
            if k >= len(stream_order):
                return
            ei = ELEM_IDX[stream_order[k]]
            slot = k % NRING
            if stream_order[k] == "vv":
                P.add("sp", lambda e, ei=ei, slot=slot: e.dma_start(out=ring[slot][:, :, 0:256],
                                                                     in_=wsc[ei].rearrange("p (dc c) -> p dc c", dc=8)[:, :, 0:256]),
                      reads=[f"wsc{ei}"], writes=[f"ring{slot}"], dma=True, chan=f"rg{slot}")
                return
            P.add("sp", lambda e, ei=ei, slot=slot: e.dma_start(out=ring[slot][:].rearrange("p a b -> p (a b)"), in_=wsc[ei]),
                  reads=[f"wsc{ei}"], writes=[f"ring{slot}"], dma=True, chan=f"rg{slot}")

        def take(name):
            k = state["stream"]
            assert stream_order[k] == name, (k, stream_order[k], name)
            state["stream"] += 1
            issue_stream(k + NRING - 2)
            return ring[k % NRING], f"ring{k % NRING}"

        tiles = []
        tok0 = 0
        tabo = 0
        si = 0
        for kind, n in segs:
            nblk = n // BLK
            for t in range(n // TILE):
                tiles.append(dict(kind=kind, nblk=nblk, b0=4 * t, tok0=tok0 + t * TILE, tabo=tabo, si=si, first=(t == 0), last=(t == n // TILE - 1)))
            tok0 += n
            tabo += n + 256
            if kind == "sample":
                si += 1
        for _ in tiles:
            stream_order.extend(n for n, _ in ELEMS)
        for k in range(NRING - 2):
            issue_stream(k)

        def front_a(tl, gb):
            kind, nblk = tl["kind"], tl["nblk"]
            if gb < 0 or gb >= nblk:
                if kind == "prompt":
                    return None
                src = xh[2 * tl["si"] + (0 if gb < 0 else 1)]
            else:
                t0 = tl["tok0"] - tl["b0"] * BLK + gb * BLK
                src = xm[t0:t0 + BLK, :]
            i = state.setdefault("xb", 0)
            state["xb"] += 1
            xb, xres = xbuf[i % 2], f"xbuf{i % 2}"
            xs_, xsres = xs[i % 2], f"xs{i % 2}"
            c = (i % 8) * 4
            P.add("sp", lambda e: e.dma_start(out=xb[:], in_=src), writes=[xres], dma=True, chan="ld" + xres)
            P.add("dve", lambda e: e.scalar_tensor_tensor(out=xs_[:], in0=xb[:], scalar=1.0, in1=xb[:], op0=ALU.mult, op1=ALU.mult, accum_out=stat[:, c:c + 1]),
                  reads=[xres], writes=[xsres, f"st{c}"])
            P.add("act", lambda e: e.activation(out=stat[:, c + 1:c + 2], in_=stat[:, c:c + 1], func=AF.Sqrt, bias=epsT[:], scale=1.0 / D),
                  reads=[f"st{c}", "epsT"], writes=[f"st{c + 1}"])
            P.add("dve", lambda e: e.reciprocal(out=stat[:, c + 2:c + 3], in_=stat[:, c + 1:c + 2]), reads=[f"st{c + 1}"], writes=[f"st{c + 2}"])
            P.add("dve", lambda e: e.scalar_tensor_tensor(out=xs_[:], in0=xb[:], scalar=stat[:, c + 2:c + 3], in1=gpre[:], op0=ALU.mult, op1=ALU.mult),
                  reads=[xres, f"st{c + 2}", "gpre"], writes=[xsres])
            return (gb % 8, xs_, xsres)

        def front_b(tok):
            if tok is None:
                return
            slot, xs_, xsres = tok
            for ch in range(8):
                P.add("pe", lambda e, ch=ch: e.transpose(pTb[:, ch, :], xs_[:, ch * 128:(ch + 1) * 128], ident),
                      reads=[xsres, "cst"], writes=["bank7"])
            P.add("act", lambda e: e.activation(out=hT[:, :, slot * 128:(slot + 1) * 128], in_=pTb, func=AF.Copy),
                  reads=["bank7"], writes=[f"hT{slot}"])

        def front(tl, gb):
            front_b(front_a(tl, gb))

        def present(tl, gb):
            return not (tl["kind"] == "prompt" and (gb < 0 or gb >= tl["nblk"]))

        def evac_copy(i, out_ap, in_ap, reads, writes):
            if i % 2 == 0:
                P.add("act", lambda e: e.activation(out=out_ap, in_=in_ap, func=AF.Copy), reads=reads, writes=writes)
            else:
                P.add("dve", lambda e: e.tensor_copy(out=out_ap, in_=in_ap), reads=reads, writes=writes)

        def inproj_fm(el, elres, q, rhs_ap, rhs_res, n, bk, col0=0):
            for dc in range(8):
                P.add("pe", lambda e, dc=dc: e.matmul(bank[bk][:, col0:col0 + n], lhsT=el[:, dc, q * 128:(q + 1) * 128],
                                                      rhs=rhs_ap(dc), start=(dc == 0), stop=(dc == 7)),
                      reads=[elres] + rhs_res, writes=[f"bank{bk}"])

        def tile_body(ti, tl):
            b0, nblk, kind = tl["b0"], tl["nblk"], tl["kind"]
            state["genlist"] = list(range(7))
            pend_ = None
            for gb in range(b0 - 1 if tl["first"] else b0 + 1, b0 + 5):
                if (ti, gb) not in state.setdefault("fronted", set()):
                    tok_ = front_a(tl, gb)
                    front_b(pend_)
                    pend_ = tok_
            front_b(pend_)
            nxt = tiles[ti + 1] if ti + 1 < len(tiles) else None
            hq = {"L": [], "t5only": set(), "pend": None, "pend_gb": None}
            if nxt is not None:
                nb0 = nxt["b0"]
                live = {(b0 + j) % 8 for j in range(4)}
                cand = [gb for gb in range(nb0 - 1 if nxt["first"] else nb0 + 1, nb0 + 5) if present(nxt, gb)]
                t4b = [gb for gb in cand if (gb % 8) not in live]
                t5b = [gb for gb in cand if (gb % 8) in live]
                hq["L"] = t4b + t5b
                hq["t5only"] = set(t5b)
                for gb in cand:
                    state.setdefault("fronted", set()).add((ti + 1, gb))

            def hoist_event(in_t5):
                if hq["pend"] is not None and (in_t5 or hq["pend_gb"] not in hq["t5only"]):
                    front_b(hq["pend"])
                    hq["pend"] = None
                if hq["pend"] is None and hq["L"]:
                    gb_ = hq["L"].pop(0)
                    hq["pend"] = front_a(nxt, gb_)
                    hq["pend_gb"] = gb_
            mslot = (b0 % 8)
            main_res = [f"hT{mslot + j}" for j in range(4)]
            hmain = lambda dc: hT[:, dc, mslot * 128:(mslot + 4) * 128]
            kbs = [kb for kb in range(6) if present(tl, b0 - 1 + kb)]
            usl = lambda kb: (b0 + kb) % 6
            newkb = [kb for kb in kbs if tl["first"] or kb >= 2]
            hblk = lambda kb, dc: hT[:, dc, ((b0 - 1 + kb) % 8) * 128:((b0 - 1 + kb) % 8 + 1) * 128]
            hres = lambda kb: [f"hT{(b0 - 1 + kb) % 8}"]
            if limit < 2:
                return
            to = tl["tabo"] + b0 * BLK
            if ti not in state.setdefault("tab_done", set()):
                P.add("sp", lambda e, to=to: e.dma_start(out=tab[:], in_=tabs[:, :, to:to + 768].rearrange("a p t -> p a t")),
                      writes=["tab"], dma=True, chan="tab")
            import os as _os
            T1N = int(_os.environ.get("T1_N", "99"))
            if T1N < 3:
                return
            if T1N < 4:
                return
            elu = [take("u0"), take("u1")]
            n_e = 0
            for kb in [k_ for k_ in (1, 2, 3, 4, 0, 5) if k_ in newkb]:
                for half in range(2):
                    bk = gbank()
                    el_, r_ = elu[half]
                    for dc in range(8):
                        P.add("pe", lambda e, dc=dc, kb=kb, el_=el_, bk=bk: e.matmul(bank[bk][:, :], lhsT=hblk(kb, dc), rhs=el_[:, dc, :], start=(dc == 0), stop=(dc == 7)),
                              reads=[r_] + hres(kb), writes=[f"bank{bk}"])
                    evac_copy(n_e, ub[:, usl(kb), half * 512:(half + 1) * 512], bank[bk][:, :], [f"bank{bk}"], [f"ub{usl(kb)}_{half}"])
                    n_e += 1
            el, elres = take("vv")
            for pr in range(3):
                bk = gbank()
                any_ = False
                for j in range(2):
                    kb = 2 * pr + j
                    if kb not in newkb:
                        continue
                    any_ = True
                    for dc in range(8):
                        P.add("pe", lambda e, dc=dc, kb=kb, j=j, bk=bk: e.matmul(bank[bk][:, j * 256:(j + 1) * 256], lhsT=hblk(kb, dc), rhs=el[:, dc, 0:256],
                                                                         start=(dc == 0), stop=(dc == 7)),
                              reads=[elres] + hres(kb), writes=[f"bank{bk}"])
                    for cp, off in enumerate((0, 128)):
                        evac_copy(kb + cp, vaug[:, usl(kb), :, off:off + 64], bank[bk][:, j * 256:(j + 1) * 256].rearrange("p (g d) -> p g d", g=4),
                                  [f"bank{bk}"], [f"vaug{usl(kb)}_{cp}"])
            elk, elkres = take("kk")

            kbanks = {}

            def k_main(g):
                bkm = gbank()
                kbanks[g] = bkm
                for dc in range(8):
                    P.add("pe", lambda e, dc=dc: e.matmul(bank[bkm][:, :], lhsT=elk[:, dc, g * 128:(g + 1) * 128], rhs=hmain(dc), start=(dc == 0), stop=(dc == 7)),
                          reads=[elkres] + main_res, writes=[f"bank{bkm}"])

            def k_halo(g):
                kp = kpre[g % 2]
                kr = f"kpre{g % 2}"
                bkm = kbanks[g]
                bkh = gbank()
                for hi, kb in enumerate((0, 5)):
                    if kb not in kbs:
                        continue
                    for dc in range(8):
                        P.add("pe", lambda e, dc=dc, hi=hi, kb=kb: e.matmul(bank[bkh][:, hi * 128:(hi + 1) * 128], lhsT=elk[:, dc, g * 128:(g + 1) * 128], rhs=hblk(kb, dc),
                                                                         start=(dc == 0), stop=(dc == 7)),
                              reads=[elkres] + hres(kb), writes=[f"bank{bkh}"])
                P.add("act", lambda e: e.activation(out=kp[:, 128:640], in_=bank[bkm][:, :], func=AF.Copy), reads=[f"bank{bkm}"], writes=[kr + "_m"])
                for hi, kb in enumerate((0, 5)):
                    if kb in kbs:
                        P.add("dve", lambda e, hi=hi, kb=kb: e.tensor_copy(out=kp[:, kb * 128:(kb + 1) * 128], in_=bank[bkh][:, hi * 128:(hi + 1) * 128]),
                              reads=[f"bank{bkh}"], writes=[kr + f"_h{hi}"])
                    else:
                        P.add("dve", lambda e, kb=kb: e.memset(kp[:, kb * 128:(kb + 1) * 128], 0.0), writes=[kr + f"_h{hi}"])

            def k_rope(g):
                kp = kpre[g % 2]
                kr = f"kpre{g % 2}"
                r = g % 2
                bp0 = gbank()
                bp1 = gbank()
                P.add("pe", lambda e: e.matmul(bank[bp0][:, :], lhsT=perm, rhs=kp[:, 0:512], start=True, stop=True),
                      reads=["cst", kr + "_m", kr + "_h0"], writes=[f"bank{bp0}"])
                P.add("pe", lambda e: e.matmul(bank[bp1][:, 0:256], lhsT=perm, rhs=kp[:, 512:768], start=True, stop=True),
                      reads=["cst", kr + "_m", kr + "_h1"], writes=[f"bank{bp1}"])
                P.add("pool", lambda e: e.tensor_tensor(out=rt1[r][:, :], in0=kp[:, :], in1=tab[:, 0, :], op=ALU.mult),
                      reads=[kr + "_m", kr + "_h0", kr + "_h1", "tab"], writes=["rt1_0"])
                P.add("dve", lambda e: e.tensor_tensor(out=rt2[r][:, 0:512], in0=bank[bp0][:, :], in1=tab[:, 1, 0:512], op=ALU.mult),
                      reads=[f"bank{bp0}", "tab"], writes=["rt2_0a"])
                P.add("dve", lambda e: e.tensor_tensor(out=rt2[r][:, 512:768], in0=bank[bp1][:, 0:256], in1=tab[:, 1, 512:768], op=ALU.mult),
                      reads=[f"bank{bp1}", "tab"], writes=["rt2_0b"])
                P.add("pool", lambda e: e.tensor_tensor(out=kT[:, g, :], in0=rt1[r][:, :], in1=rt2[r][:, :], op=ALU.add),
                      reads=["rt1_0", "rt2_0a", "rt2_0b"], writes=[f"kT{g}"])

            for g in range(4):
                k_main(g)
            state["genlist"] = [b for b in range(7) if b not in kbanks.values()]
            k_halo(0)
            for g in range(4):
                if g + 1 < 4:
                    k_halo(g + 1)
                k_rope(g)
            state["genlist"] = list(range(7))
            if dbg and ti == 0:
                P.add("sp", lambda e: e.dma_start(out=dbg_out["d_hT"], in_=hT[:].rearrange("p a b -> p (a b)")), reads=[f"hT{s}" for s in range(8)], dma=True, chan="dbg")
                P.add("sp", lambda e: e.dma_start(out=dbg_out["d_kT"], in_=kT[:].rearrange("p a b -> p (a b)")), reads=[f"kT{g}" for g in range(4)], dma=True, chan="dbg")
                P.add("sp", lambda e: e.dma_start(out=dbg_out["d_ub"], in_=ub[:].rearrange("p a b -> p (a b)")), reads=[f"ub{usl(kb)}_{h}" for kb in kbs for h in range(2)], dma=True, chan="dbg")
                P.add("sp", lambda e: e.dma_start(out=dbg_out["d_va"], in_=vaug[:].rearrange("p a b c -> p (a b c)")),
                      reads=[f"vaug{usl(kb)}_{cp}" for kb in kbs for cp in range(2)] + ["vaug_ones"], dma=True, chan="dbg")

            if limit < 3:
                return
            elpg = [take("pg0"), take("pg1")]
            for g in range(4):
                pt = poolT[g % 2]
                ptres = f"poolT{g % 2}"
                for kc in range(2):
                    c = 2 * g + kc
                    bk = gbank()
                    for ob in range(4):
                        gb = b0 + ob
                        terms = []
                        for rel in (-1, 0, 1):
                            sgb = gb + rel
                            if kind == "prompt" and (sgb < 0 or sgb >= nblk):
                                continue
                            if rel == 0:
                                if gb == 0:
                                    bi = (12 + g) if kind == "prompt" else (20 + g)
                                elif gb == nblk - 1:
                                    bi = (16 + g) if kind == "prompt" else (24 + g)
                                else:
                                    bi = 3 * g + 1
                            else:
                                bi = 3 * g + (0 if rel < 0 else 2)
                            terms.append((ob + 1 + rel, bi))
                        for i, (kb, bi) in enumerate(terms):
                            P.add("pe", lambda e, kb=kb, bi=bi, c=c, ob=ob, i=i, nt=len(terms), bk=bk: e.matmul(
                                bank[bk][:, ob * 128:(ob + 1) * 128], lhsT=ub[:, (b0 + kb) % 6, c * 128:(c + 1) * 128], rhs=bands[:, bi, :],
                                start=(i == 0), stop=(i == nt - 1)),
                                reads=[f"ub{usl(kb)}_{c // 4}", "bands"], writes=[f"bank{bk}"])
                    evac_copy(c, pt[:, kc, :], bank[bk][:, :], [f"bank{bk}"], [ptres + f"_{kc}"])
                gtmp = []
                for oh in range(2):
                    oc = 2 * g + oh
                    bkg = gbank()
                    el_, r_ = elpg[oc // 4]
                    inproj_fm(el_, r_, oc % 4, hmain, main_res, 512, bkg)
                    s_ = sg[0]
                    yt = ytmp[oh]
                    P.add("act", lambda e, s_=s_, bkg=bkg: e.activation(out=s_[:], in_=bank[bkg][:, :], func=AF.Tanh, scale=0.5), reads=[f"bank{bkg}"], writes=["sg0"])
                    P.add("dve", lambda e, s_=s_, bkg=bkg, yt=yt: e.scalar_tensor_tensor(out=yt[:], in0=s_[:], scalar=1.0, in1=bank[bkg][:, :], op0=ALU.add, op1=ALU.mult),
                          reads=[f"bank{bkg}", "sg0"], writes=[f"ytmp{oh}"])
                for oh in range(2):
                    oc = 2 * g + oh
                    bkw = gbank()
                    yt = ytmp[oh]
                    for kc in range(2):
                        P.add("pe", lambda e, kc=kc, oh=oh, g=g, pt=pt, bkw=bkw: e.matmul(bank[bkw][:, :], lhsT=wpg[:, g, kc, oh * 128:(oh + 1) * 128], rhs=pt[:, kc, :],
                                                                                 start=(kc == 0), stop=(kc == 1)),
                              reads=["wpg", ptres + "_0", ptres + "_1"], writes=[f"bank{bkw}"])
                    P.add("dve", lambda e, bkw=bkw, oc=oc, yt=yt: e.scalar_tensor_tensor(out=pbT[:, oc, :], in0=bank[bkw][:, :], scalar=psh[:, oc:oc + 1], in1=yt[:],
                                                                                  op0=ALU.mult, op1=ALU.mult),
                          reads=[f"bank{bkw}", f"ytmp{oh}", "psh"], writes=["pbT", f"pbT{oc}"])
            if dbg and ti == 0:
                P.add("sp", lambda e: e.dma_start(out=dbg_out["d_pbT"], in_=pbT[:].rearrange("p a b -> p (a b)")), reads=[f"pbT{o}" for o in range(8)], dma=True, chan="dbg")

            if limit < 4:
                return

            def qa_items(g):
                box = {}
                qq = qg[g % 2]
                qres = f"qg{g % 2}"
                sa = sag[g % 2]
                sres = f"sag{g % 2}"

                def get_el():
                    if "el" not in box:
                        box["el"] = take(f"qa{g}")
                    return box["el"]

                def q_item(j):
                    el, elres = get_el()
                    bk = gbank()
                    inproj_fm(el, elres, j, hmain, main_res, 512, bk)
                    qp = qpre[j]
                    P.add("act", lambda e: e.activation(out=qp[:], in_=bank[bk][:, :], func=AF.Copy), reads=[f"bank{bk}"], writes=[f"qpre{j}"])
                    bp = 6

                    def tail():
                        P.add("pe", lambda e: e.matmul(bank[bp][:, :], lhsT=perm, rhs=qp[:], start=True, stop=True), reads=["cst", f"qpre{j}"], writes=[f"bank{bp}"])
                        P.add("pool", lambda e: e.tensor_tensor(out=rt1[j][:, 0:512], in0=qp[:], in1=tab[:, 0, 128:640], op=ALU.mult),
                              reads=[f"qpre{j}", "tab"], writes=["rt1_0"])
                        P.add("dve", lambda e: e.tensor_tensor(out=rt2[j][:, 0:512], in0=bank[bp][:, :], in1=tab[:, 1, 128:640], op=ALU.mult),
                              reads=[f"bank{bp}", "tab"], writes=["rt2_0a"])
                        P.add("pool", lambda e: e.tensor_tensor(out=qq[:, j, :], in0=rt1[j][:, 0:512], in1=rt2[j][:, 0:512], op=ALU.add),
                              reads=["rt1_0", "rt2_0a"], writes=[qres + f"_{j}"])
                    return tail

                def ag_item(j):
                    el, elres = get_el()
                    bk = gbank()
                    inproj_fm(el, elres, 2 + j, hmain, main_res, 512, bk)
                    P.add("act", lambda e: e.activation(out=sa[:, j, :], in_=bank[bk][:, :], func=AF.Tanh, scale=0.5), reads=[f"bank{bk}"], writes=[sres + f"_{j}"])
                    P.add("dve", lambda e: e.scalar_tensor_tensor(out=sa[:, j, :], in0=sa[:, j, :], scalar=1.0, in1=bank[bk][:, :], op0=ALU.add, op1=ALU.mult),
                          reads=[f"bank{bk}", sres + f"_{j}"], writes=[sres + f"_{j}"])

                return [lambda: q_item(0), lambda: q_item(1), lambda: ag_item(0), lambda: ag_item(1)]

            def barrier_mm():
                bk = gbank()
                P.add("pe", lambda e: e.matmul(bank[bk][:, 0:2], lhsT=ident, rhs=cst[:, 0, 0:2], start=True, stop=True), reads=["cst"], writes=[f"bank{bk}"])

            def chunks_of(qb):
                gb = b0 + qb
                return [c for c in range(3) if not (kind == "prompt" and (gb - 1 + c < 0 or gb - 1 + c >= nblk))]

            def s_part(idx, g, qb, par):
                qq = qg[g % 2]
                qres = f"qg{g % 2}"
                sset = (0, 1, 2) if idx % 2 == 0 else (3, 4, 5)
                rows = slice(0, 64) if par == 0 else slice(64, 128)
                cols = slice(par * 256, (par + 1) * 256)
                for c in chunks_of(qb):
                    kcols = slice((qb + c) * 128, (qb + c + 1) * 128)
                    bk = sset[c]
                    P.add("pe", lambda e, bk=bk, kcols=kcols: e.matmul(bank[bk][:, cols].rearrange("p (j q) -> p j q", j=2), lhsT=kT[rows, g, kcols],
                                                                      rhs=qq[rows, :, qb * 128:(qb + 1) * 128], start=True, stop=True),
                          reads=[f"kT{g}", qres + "_0", qres + "_1"], writes=[f"bank{bk}"])

            def softmax_part(idx, g, qb):
                gb = b0 + qb
                sset = (0, 1, 2) if idx % 2 == 0 else (3, 4, 5)
                pt_ = PT[idx % 3]
                ptres = f"PT{idx % 3}"
                for c in chunks_of(qb):
                    bk = sset[c]
                    P.add("act", lambda e, c=c, bk=bk: e.activation(out=pt_[:, c, :], in_=bank[bk][:, :], func=AF.Exp, scale=HD ** -0.5),
                          reads=[f"bank{bk}"], writes=[ptres + f"_{c}"])
                    mi = None
                    if c == 0:
                        mi = 2 if (kind == "sample" and gb == 0) else 0
                    elif c == 2:
                        mi = 3 if (kind == "sample" and gb == nblk - 1) else 1
                    if mi is not None:
                        P.add("dve", lambda e, c=c, mi=mi: e.tensor_tensor(out=pt_[:, c, :], in0=pt_[:, c, :], in1=masks[:, mi, :], op=ALU.mult),
                              reads=[ptres + f"_{c}", "masks"], writes=[ptres + f"_{c}"])

            def pv_part(idx, g, qb):
                pt_ = PT[idx % 3]
                ptres = f"PT{idx % 3}"
                sa = sag[g % 2]
                sres = f"sag{g % 2}"
                chunks = chunks_of(qb)
                for par in range(2):
                    cols = slice(par * 256, (par + 1) * 256)
                    for i, c in enumerate(chunks):
                        kb = qb + c
                        lw = vaug[:, usl(kb), g, 0:128] if par == 0 else vaug[:, usl(kb), g, 64:192]
                        P.add("pe", lambda e, lw=lw, c=c, cols=cols, i=i, n=len(chunks): e.matmul(bank[6][:, cols], lhsT=lw, rhs=pt_[:, c, cols], start=(i == 0), stop=(i == n - 1)),
                              reads=[f"vaug{usl(kb)}_0", f"vaug{usl(kb)}_1", "vaug_ones", ptres + f"_{c}"], writes=["bank6"])
                pv = pvs[idx % 2]
                pres = f"pvs{idx % 2}"
                ds = rden[idx % 2]
                rd = rden[idx % 2]
                ot = otmp[idx % 2]
                P.add("act", lambda e: e.activation(out=pv[:], in_=bank[6][:, :], func=AF.Copy), reads=["bank6"], writes=[pres])
                P.add("dve", lambda e: e.tensor_tensor(out=ds[0:64, :], in0=pv[64:128, 0:256], in1=sinkT[64:128, g, :], op=ALU.add),
                      reads=[pres, "sinkT"], writes=[f"dsum{idx % 2}e"])
                P.add("dve", lambda e: e.tensor_tensor(out=ds[64:128, :], in0=pv[0:64, 256:512], in1=sinkT[0:64, g, :], op=ALU.add),
                      reads=[pres, "sinkT"], writes=[f"dsum{idx % 2}o"])
                P.add("dve", lambda e: e.reciprocal(out=rd[:, :], in_=ds[:, :]), reads=[f"dsum{idx % 2}e", f"dsum{idx % 2}o"], writes=[f"rden{idx % 2}"])
                P.add("dve", lambda e: e.tensor_tensor(out=ot[0:64, :], in0=pv[0:64, 0:256], in1=rd[0:64, :], op=ALU.mult),
                      reads=[pres, f"rden{idx % 2}"], writes=[f"otmp{idx % 2}e"])
                P.add("dve", lambda e: e.tensor_tensor(out=ot[64:128, :], in0=pv[64:128, 256:512], in1=rd[64:128, :], op=ALU.mult),
                      reads=[pres, f"rden{idx % 2}"], writes=[f"otmp{idx % 2}o"])
                P.add("pool", lambda e: e.tensor_tensor(out=abT[:, 2 * g:2 * g + 2, qb * 128:(qb + 1) * 128],
                                                        in0=ot[:, :].rearrange("p (j q) -> p j q", j=2),
                                                        in1=sa[:, :, qb * 128:(qb + 1) * 128], op=ALU.mult),
                      reads=[f"otmp{idx % 2}e", f"otmp{idx % 2}o", sres + "_0", sres + "_1"], writes=["abT", f"abT{g}_{qb}"])

            state["genlist"] = [7]
            def run_item(it):
                t_ = it()
                if t_ is not None:
                    t_()

            for it in qa_items(0):
                run_item(it)
            seq = [(g, qb) for g in range(4) for qb in range(4)]
            fillers = []
            for idx, (g, qb) in enumerate(seq):
                if qb == 0:
                    for it in fillers:
                        run_item(it)
                    fillers = qa_items(g + 1) if g < 3 else []
                s_part(idx, g, qb, 0)
                if idx > 1:
                    pv_part(idx - 2, *seq[idx - 2])
                tail_ = None
                if fillers:
                    tail_ = fillers.pop(0)()
                elif idx <= 1:
                    barrier_mm()
                s_part(idx, g, qb, 1)
                if tail_ is not None:
                    tail_()
                softmax_part(idx, g, qb)
            pv_part(len(seq) - 2, *seq[-2])
            pv_part(len(seq) - 1, *seq[-1])
            state["genlist"] = list(range(7))
            if dbg and ti == 0:
                P.add("sp", lambda e: e.dma_start(out=dbg_out["d_abT"], in_=abT[:].rearrange("p a b -> p (a b)")),
                      reads=[f"abT{g}_{qb}" for g in range(4) for qb in range(4)], dma=True, chan="dbg")

            if limit < 5:
                return
            if ti + 1 < len(tiles):
                ntl = tiles[ti + 1]
                nto = ntl["tabo"] + ntl["b0"] * BLK
                P.add("sp", lambda e, nto=nto: e.dma_start(out=tab[:], in_=tabs[:, :, nto:nto + 768].rearrange("a p t -> p a t")),
                      writes=["tab"], dma=True, chan="tab")
                state.setdefault("tab_done", set()).add(ti + 1)
            pb_all = [f"pbT{o}" for o in range(8)]
            ab_all = [f"abT{g}_{qb}" for g in range(4) for qb in range(4)]
            for op_ in range(4):
                elm, rm = take(f"mg{op_}")
                elp, rp = take(f"pr{op_}")
                for j in range(2):
                    o = 2 * op_ + j
                    gi = 0
                    bk1 = gbank()
                    inproj_fm(elm, rm, j, hmain, main_res, 512, bk1)
                    P.add("act", lambda e, gi=gi, bk1=bk1: e.activation(out=gsig[gi][:], in_=bank[bk1][:, :], func=AF.Tanh, scale=0.5), reads=[f"bank{bk1}"], writes=[f"gsig{gi}"])
                    bk2 = gbank()
                    inproj_fm(elm, rm, 2 + j, hmain, main_res, 512, bk2)
                    P.add("act", lambda e, gi=gi, bk2=bk2: e.activation(out=gsig[gi + 1][:], in_=bank[bk2][:, :], func=AF.Tanh, scale=0.5), reads=[f"bank{bk2}"], writes=[f"gsig{gi + 1}"])
                    bk3 = gbank()
                    for k in range(8):
                        P.add("pe", lambda e, k=k, j=j, bk3=bk3, elp=elp: e.matmul(bank[bk3][:, :], lhsT=elp[:, k, j * 128:(j + 1) * 128], rhs=pbT[:, k, :], start=(k == 0), stop=(k == 7)),
                              reads=[rp] + pb_all, writes=[f"bank{bk3}"])
                    bk4 = gbank()
                    for k in range(8):
                        P.add("pe", lambda e, k=k, j=j, bk4=bk4, elp=elp: e.matmul(bank[bk4][:, :], lhsT=elp[:, k, 256 + j * 128:256 + (j + 1) * 128], rhs=abT[:, k, :], start=(k == 0), stop=(k == 7)),
                              reads=[rp] + ab_all, writes=[f"bank{bk4}"])
                    m1, m2 = mt1[o % 2], mt2[o % 2]
                    P.add("dve", lambda e, m1=m1, gi=gi, bk3=bk3: e.scalar_tensor_tensor(out=m1[:], in0=gsig[gi][:], scalar=1.0, in1=bank[bk3][:, :], op0=ALU.add, op1=ALU.mult),
                          reads=[f"bank{bk3}", f"gsig{gi}"], writes=["mt1_0"])
                    P.add("dve", lambda e, m2=m2, gi=gi, bk4=bk4: e.scalar_tensor_tensor(out=m2[:], in0=gsig[gi + 1][:], scalar=1.0, in1=bank[bk4][:, :], op0=ALU.add, op1=ALU.mult),
                          reads=[f"bank{bk4}", f"gsig{gi + 1}"], writes=["mt2_0"])
                    P.add("pool", lambda e, m1=m1, m2=m2, o=o: e.tensor_tensor(out=mT[:, o, :], in0=m1[:], in1=m2[:], op=ALU.add),
                          reads=["mt1_0", "mt2_0"], writes=["mT", f"mT{o}"])
                hoist_event(False)
            if dbg and ti == 0:
                P.add("sp", lambda e: e.dma_start(out=dbg_out["d_mT"], in_=mT[:].rearrange("p a b -> p (a b)")), reads=[f"mT{o}" for o in range(8)], dma=True, chan="dbg")

            if limit < 6:
                return
            elo = [take("wo0"), take("wo1")]
            m_all = [f"mT{o}" for o in range(8)]
            for ob in range(4):
                t0 = tl["tok0"] + ob * BLK
                i = state.setdefault("yb", 0)
                state["yb"] += 1
                yb, yres = ybuf[i % 2], f"ybuf{i % 2}"
                c = 32 + (i % 4) * 8
                P.add("sp", lambda e, yb=yb, t0=t0: e.dma_start(out=yb[:], in_=xm[t0:t0 + BLK, :]), writes=[yres], dma=True, chan="ld" + yres)
                bks = [gbank(), gbank()]
                for half in range(2):
                    el_, r_ = elo[half]
                    for k in range(8):
                        P.add("pe", lambda e, k=k, half=half, ob=ob, el_=el_, bks=bks: e.matmul(bank[bks[half]][:, :], lhsT=mT[:, k, ob * 128:(ob + 1) * 128], rhs=el_[:, k, :],
                                                                                     start=(k == 0), stop=(k == 7)),
                              reads=[r_] + m_all, writes=[f"bank{bks[half]}"])
                    P.add("act", lambda e, half=half, bks=bks, c=c: e.activation(out=PT[0][:, half, :], in_=bank[bks[half]][:, :], func=AF.Square,
                                                                              accum_out=stat[:, c + half:c + half + 1]),
                          reads=[f"bank{bks[half]}"], writes=[f"PT0_{half}", f"st{c + half}"])
                P.add("dve", lambda e, c=c: e.tensor_tensor(out=stat[:, c + 2:c + 3], in0=stat[:, c:c + 1], in1=stat[:, c + 1:c + 2], op=ALU.add),
                      reads=[f"st{c}", f"st{c + 1}"], writes=[f"st{c + 2}"])
                P.add("act", lambda e, c=c: e.activation(out=stat[:, c + 3:c + 4], in_=stat[:, c + 2:c + 3], func=AF.Sqrt, bias=epsT[:], scale=1.0 / D),
                      reads=[f"st{c + 2}", "epsT"], writes=[f"st{c + 3}"])
                P.add("dve", lambda e, c=c: e.reciprocal(out=stat[:, c + 4:c + 5], in_=stat[:, c + 3:c + 4]), reads=[f"st{c + 3}"], writes=[f"st{c + 4}"])
                for half in range(2):
                    yt = ytmp[half]
                    P.add("dve", lambda e, half=half, yt=yt, bks=bks, c=c: e.scalar_tensor_tensor(out=yt[:], in0=bank[bks[half]][:, :], scalar=stat[:, c + 4:c + 5],
                                                                                        in1=gpost[:, half * 512:(half + 1) * 512], op0=ALU.mult, op1=ALU.mult),
                          reads=[f"bank{bks[half]}", f"st{c + 4}", "gpost"], writes=[f"ytmp{half}"])
                    P.add("pool", lambda e, half=half, yt=yt, yb=yb: e.tensor_tensor(out=yb[:, half * 512:(half + 1) * 512], in0=yt[:], in1=yb[:, half * 512:(half + 1) * 512], op=ALU.add),
                          reads=[f"ytmp{half}", yres], writes=[yres])
                P.add("pool", lambda e, yb=yb, t0=t0: e.dma_start(out=ym[t0:t0 + BLK, :], in_=yb[:]), reads=[yres], dma=True, chan="st" + yres)
                hoist_event(True)
            while hq["pend"] is not None or hq["L"]:
                hoist_event(True)


        for ti, tl in enumerate(tiles):
            if limit < 1 or (limit < 6 and ti > 0):
                break
            tile_body(ti, tl)

        fin = sb("fin", [128, 8], F32)
        P.add("act", lambda e: e.activation(out=fin[:, 0:1], in_=epsT[:], func=AF.Copy), reads=["epsT"], writes=["fin_act"])
        P.add("dve", lambda e: e.memset(fin[:, 1:2], 0.0), writes=["fin_dve"])
        P.add("pool", lambda e: e.memset(fin[:, 2:3], 0.0), writes=["fin_pool"])
        P.add("pe", lambda e: e.matmul(bank[0][:, 0:1], lhsT=sel[0:1, 0, :], rhs=sel[0:1, 0, 0:1], start=True, stop=True), reads=["sel"], writes=["bank0"])
        P.add("dve", lambda e: e.tensor_copy(out=fin[:, 3:4], in_=bank[0][:, 0:1]), reads=["bank0"], writes=["fin_pe"])
        P.add("sp", lambda e: e.dma_start(out=fin[:, 5:6], in_=fin[:, 4:5]), reads=["fin_act", "fin_dve", "fin_pool", "fin_pe"], writes=["fin_sp"], dma=True, chan="fin")

        keys = P.finalize()
        sems = {k: es.enter_context(nc.semaphore(k)) for k in keys}
        block = es.enter_context(nc.Block())
        out_chans = [k for k in keys if k.startswith("dma_")]

        def emit_eng(ename):
            def body(eng):
                for o in P.ops[ename]:
                    for k, v in o.waits:
                        eng.wait_ge(sems[k], v)
                    if o.dma:
                        insts = o.fn(eng)
                        if not isinstance(insts, (list, tuple)):
                            insts = [insts]
                        assert len(insts) == o.ndma
                        for i_ in insts:
                            i_.then_inc(sems[o.sig[0]], 16)
                    else:
                        inst = o.fn(eng)
                        if o.sig is not None:
                            inst.then_inc(sems[o.sig[0]], 1)
                if ename == "sp":
                    for k in out_chans:
                        eng.wait_ge(sems[k], P.final_counts[k])
            return body
        block.sync(emit_eng("sp"))
        block.scalar(emit_eng("act"))
        block.vector(emit_eng("dve"))
        block.gpsimd(emit_eng("pool"))
        block.tensor(emit_eng("pe"))
    nc._prog_stats = {e: len(P.ops[e]) for e in P.ENGS}
    return nc


def rope_tables(positions):
    half = HD // 2
    inv = (np.float32(THETA) ** (-(np.arange(half, dtype=np.float32) / np.float32(half)))).astype(np.float32)
    ang = positions.astype(np.float32)[None, :] * inv[:, None]
    cos = np.cos(ang).astype(np.float32)
    sin = np.sin(ang).astype(np.float32)
    p = np.arange(128)
    ct = cos[p % 32]
    sgn = np.where((p % 64) < 32, -1.0, 1.0).astype(np.float32)
    st = sin[p % 32] * sgn[:, None]
    return np.stack([ct, st], 0)


def band_mat(g, rel, mode):
    w = POOL_WINDOWS[g]
    B = 1024
    S = 1 << 30
    if mode == "first":
        B = 0
    if mode == "last":
        S = B + 128
    s = B + rel * 128 + np.arange(128)[:, None]
    t = B + np.arange(128)[None, :]
    lo = np.maximum(t - w // 2, 0)
    hi = np.minimum(t - w // 2 + w, S)
    inr = (s >= lo) & (s < hi)
    val = inr / (hi - lo).astype(np.float64) - (s == t)
    return val.astype(np.float32)


def make_consts(valid_left, valid_right):
    bands = np.zeros((128, 28, 128), np.float32)
    for g in range(4):
        bands[:, 3 * g + 0] = band_mat(g, -1, "int")
        bands[:, 3 * g + 1] = band_mat(g, 0, "int")
        bands[:, 3 * g + 2] = band_mat(g, 1, "int")
        bands[:, 12 + g] = band_mat(g, 0, "first")
        bands[:, 16 + g] = band_mat(g, 0, "last")
        bands[:, 20 + g] = band_mat(g, 0, "int" if valid_left else "first")
        bands[:, 24 + g] = band_mat(g, 0, "int" if valid_right else "last")
    j = np.arange(128)[:, None]
    i = np.arange(128)[None, :]
    mp = (j >= i).astype(np.float32)
    mn = (j <= i).astype(np.float32)
    masks = np.zeros((128, 4, 512), np.float32)
    masks[:, 0] = np.tile(mp, (1, 4))
    masks[:, 1] = np.tile(mn, (1, 4))
    masks[:, 2] = np.tile(mp, (1, 4)) * (1.0 if valid_left else 0.0)
    masks[:, 3] = np.tile(mn, (1, 4)) * (1.0 if valid_right else 0.0)
    cst = np.zeros((128, 2, 128), np.float32)
    cst[:, 0] = np.eye(128)
    k = np.arange(128)
    partner = (k // 64) * 64 + ((k % 64) + 32) % 64
    cst[partner, 1, k] = 1.0
    sel = np.zeros((1, 2, 128), np.float32)
    sel[0, 0, 64:] = 4.0
    sel[0, 1, :64] = 4.0
    bf = ml_dtypes.bfloat16
    return (bands.reshape(128, -1).astype(bf), masks.reshape(128, -1).astype(bf),
            cst.reshape(128, -1).astype(bf), sel.reshape(1, -1).astype(bf))


def sink_layout(attn_sink):
    return np.asarray(attn_sink, np.float32).reshape(1, NH)


def core_inputs(segs_x, halos, seg_kinds, pos0s, valid_left, valid_right, shared):
    xm = np.ascontiguousarray(np.concatenate(segs_x, 0))
    tabs = []
    for x, p0 in zip(segs_x, pos0s):
        n = x.shape[0]
        tabs.append(rope_tables(np.arange(p0 - 128, p0 + n + 128)))
    tabs = np.ascontiguousarray(np.concatenate(tabs, 2))
    bands, masks, cst, sel = make_consts(valid_left, valid_right)
    d = dict(shared)
    d.update(xm=xm, xh=np.ascontiguousarray(halos), tabs=tabs, bands=bands, masks=masks, cst=cst, sel=sel)
    return d


def shared_inputs(norm_pre, w_in, w_pool_group, pool_scale, w_pool_proj, attn_sink, w_attn_proj, w_out, norm_post):
    f = lambda a: np.ascontiguousarray(np.asarray(a, np.float32))
    return dict(
        w_in=f(w_in[0]), w_pg=f(w_pool_group[0]), w_pp=f(w_pool_proj[0]), w_ap=f(w_attn_proj[0]), w_out=f(w_out[0]),
        gpre_b=f(np.tile(np.asarray(norm_pre[0]).reshape(1, D), (128, 1))), pscale_c=f(np.asarray(pool_scale[0]).reshape(8, 128).T),
        gpost_b=f(np.tile(np.asarray(norm_post[0]).reshape(1, D), (128, 1))), sink_rows=f(sink_layout(attn_sink[0])),
    )


_NC_CACHE = {}


def kernel(x_prompt, x_sample, norm_pre, w_in, w_pool_group, pool_scale, w_pool_proj,
           attn_sink, w_attn_proj, w_out, norm_post):
    x_prompt = np.asarray(x_prompt, np.float32)
    x_sample = np.asarray(x_sample, np.float32)
    shared = shared_inputs(norm_pre, w_in, w_pool_group, pool_scale, w_pool_proj, attn_sink, w_attn_proj, w_out, norm_post)
    segs = [("prompt", 2048), ("prompt", 2048), ("sample", 4096)]
    if "nc" not in _NC_CACHE:
        _NC_CACHE["nc"] = build_program(segs)
    nc = _NC_CACHE["nc"]
    in_maps = []
    for c in range(8):
        sb_, hf = c // 2, c % 2
        sx = [x_prompt[2 * c], x_prompt[2 * c + 1], x_sample[sb_, hf * 4096:(hf + 1) * 4096]]
        halos = np.zeros((2, 128, D), np.float32)
        if hf == 1:
            halos[0] = x_sample[sb_, 4096 - 128:4096]
        else:
            halos[1] = x_sample[sb_, 4096:4096 + 128]
        in_maps.append(core_inputs(sx, halos, [k for k, _ in segs], [0, 0, hf * 4096], hf == 1, hf == 0, shared))
    res = run_bass_kernel_spmd(nc, in_maps, core_ids=list(range(8)))
    y_prompt = np.empty_like(x_prompt)
    y_sample = np.empty_like(x_sample)
    for c in range(8):
        ymc = res.results[c]["ym"]
        y_prompt[2 * c] = ymc[0:2048]
        y_prompt[2 * c + 1] = ymc[2048:4096]
        y_sample[c // 2, (c % 2) * 4096:(c % 2 + 1) * 4096] = ymc[4096:8192]
    return (y_prompt, y_sample)
```
